# Optimizing a Trainium2 kernel written in Bass

```python
import math
import jax, jax.numpy as jnp
from jax import lax
import numpy as np

D_MODEL = 2048
BATCH = 1
SEQ = 8192
DEPTH = 4

GRID_W = 64
HEAD_DIM = 128
NA_HEADS = 8
NA_ROWS = 8
NA_COLS = 16
WG_HEADS = 8
WG_KV_HEADS = 2
WG_WINDOW = 128
WG_BLOCK = 128
SC_WIDTH = 1024
SC_KSIZE = 3
NA_WIDTH = NA_HEADS * HEAD_DIM
WG_Q_WIDTH = WG_HEADS * HEAD_DIM
WG_KV_WIDTH = WG_KV_HEADS * HEAD_DIM
BRANCH_WIDTH = 1024
N_BRANCH = 3
IN_SPLITS = (NA_WIDTH, NA_WIDTH, NA_WIDTH,
             WG_Q_WIDTH, WG_KV_WIDTH, WG_KV_WIDTH,
             SC_WIDTH, SC_WIDTH, SC_WIDTH,
             N_BRANCH * D_MODEL)
IN_WIDTH = sum(IN_SPLITS)
N_EXPERTS = 16
EC_CAPACITY_FACTOR = 2
D_FF = 1536
DEEPNORM_ALPHA = (2 * DEPTH) ** 0.25
DEEPNORM_BETA = (8 * DEPTH) ** -0.25
LN_EPS = 1e-5

kernel_name = 'hybrid_natten_swa_shortconv_ec_moe_deepnorm'


def layer_norm(x, g, b):
    xf = x.astype(jnp.float32)
    mu = xf.mean(-1, keepdims=True)
    var = jnp.square(xf - mu).mean(-1, keepdims=True)
    y = (xf - mu) * lax.rsqrt(var + LN_EPS) * g.astype(jnp.float32) + b.astype(jnp.float32)
    return y.astype(x.dtype)


def neighbourhood_attention(q, k, v, rpb):
    b, s, h, dh = q.shape
    rows = s // GRID_W
    kr = min(NA_ROWS, rows)
    kc = NA_COLS
    qg = q.reshape(b, rows, GRID_W, h, dh)
    kg = k.reshape(b, rows, GRID_W, h, dh)
    vg = v.reshape(b, rows, GRID_W, h, dh)
    col = jnp.arange(GRID_W)
    col_idx = jnp.clip(col - kc // 2, 0, GRID_W - kc)[:, None] + jnp.arange(kc)[None, :]
    col_off = col_idx - col[:, None] + (NA_COLS - 1)
    row = jnp.arange(rows)
    row_start = jnp.clip(row - kr // 2, 0, rows - kr)
    scale = HEAD_DIM ** -0.5

    def row_block(args):
        q_r, r, rs = args
        k_rows = lax.dynamic_slice_in_dim(kg, rs, kr, axis=1)
        v_rows = lax.dynamic_slice_in_dim(vg, rs, kr, axis=1)
        k_win = k_rows[:, :, col_idx]
        v_win = v_rows[:, :, col_idx]
        row_off = rs + jnp.arange(kr) - r + (NA_ROWS - 1)
        bias = rpb[:, row_off[:, None, None], col_off[None]]
        bias = jnp.transpose(bias, (0, 2, 1, 3)).astype(jnp.float32)
        logits = jnp.einsum('bqhd,biqjhd->bhqij', q_r, k_win).astype(jnp.float32) * scale + bias[None]
        p = jax.nn.softmax(logits.reshape(b, h, GRID_W, kr * kc), axis=-1)
        p = p.reshape(b, h, GRID_W, kr, kc).astype(v.dtype)
        return jnp.einsum('bhqij,biqjhd->bqhd', p, v_win)

    out = lax.map(row_block, (jnp.moveaxis(qg, 1, 0), row, row_start))
    return jnp.moveaxis(out, 0, 1).reshape(b, s, h * dh)


def windowed_gqa(q, k, v, sink):
    b, s, h, dh = q.shape
    hkv = k.shape[2]
    g = h // hkv
    nb = s // WG_BLOCK
    qb = q.reshape(b, nb, WG_BLOCK, hkv, g, dh)

    def band(t):
        tp = jnp.pad(t, ((0, 0), (WG_BLOCK, WG_BLOCK), (0, 0), (0, 0)))
        tp = tp.reshape(b, nb + 2, WG_BLOCK, hkv, dh)
        return jnp.concatenate([tp[:, :-2], tp[:, 1:-1], tp[:, 2:]], axis=2)

    kb, vb = band(k), band(v)
    q_pos = jnp.arange(WG_BLOCK)
    k_pos = jnp.arange(3 * WG_BLOCK) - WG_BLOCK
    dist = jnp.abs(k_pos[None, :] - q_pos[:, None])
    k_abs = jnp.arange(nb)[:, None, None] * WG_BLOCK + k_pos[None, None, :]
    valid = (dist[None] <= WG_WINDOW) & (k_abs >= 0) & (k_abs < s)
    slopes = 2.0 ** (-8.0 * jnp.arange(1, h + 1, dtype=jnp.float32) / h)
    scale = HEAD_DIM ** -0.5
    logits = jnp.einsum('bnqkgd,bnskd->bnkgqs', qb, kb).astype(jnp.float32) * scale
    logits = logits - slopes.reshape(hkv, g)[:, :, None, None] * dist.astype(jnp.float32)
    logits = jnp.where(valid[None, :, None, None], logits, -jnp.inf)
    sink_l = sink.astype(jnp.float32).reshape(hkv, g)[:, :, None, None]
    m = jnp.maximum(logits.max(-1, keepdims=True), sink_l)
    e = jnp.exp(logits - m)
    p = e / (e.sum(-1, keepdims=True) + jnp.exp(sink_l - m))
    out = jnp.einsum('bnkgqs,bnskd->bnqkgd', p.astype(v.dtype), vb)
    return out.reshape(b, s, h * dh)


def short_conv_mixer(bg, cg, hx, conv_w):
    u = cg * hx
    u_prev = jnp.pad(u, ((0, 0), (1, 0), (0, 0)))[:, :-1]
    u_next = jnp.pad(u, ((0, 0), (0, 1), (0, 0)))[:, 1:]
    return bg * (conv_w[0] * u_prev + conv_w[1] * u + conv_w[2] * u_next)


def expert_choice_ffn(x, w_router, w_gate, w_up, w_down):
    b, s, d = x.shape
    cap = EC_CAPACITY_FACTOR * s // N_EXPERTS
    aff = jax.nn.softmax(jnp.einsum('bsd,de->bse', x, w_router).astype(jnp.float32), axis=-1)
    gates, idx = lax.top_k(jnp.swapaxes(aff, 1, 2), cap)
    xe = jax.vmap(lambda xb, ib: xb[ib])(x, idx)
    hid = jax.nn.silu(jnp.einsum('becd,edf->becf', xe, w_gate)) * jnp.einsum('becd,edf->becf', xe, w_up)
    ye = jnp.einsum('becf,efd->becd', hid, w_down) * gates[..., None].astype(x.dtype)
    return jax.vmap(lambda ib, yb: jnp.zeros((s, d), yb.dtype).at[ib.reshape(-1)].add(yb.reshape(-1, d)))(idx, ye)


def setup_inputs(seed: int = 0) -> dict:
    key = jax.random.key(seed)
    ks = jax.random.split(key, 14)
    f32 = jnp.float32
    x = jax.random.normal(ks[0], (BATCH, SEQ, D_MODEL), f32)
    w_in = jax.random.normal(ks[1], (DEPTH, D_MODEL, IN_WIDTH), f32) * D_MODEL ** -0.5
    b_gate = jax.random.normal(ks[2], (DEPTH, N_BRANCH * D_MODEL), f32) * 0.02
    rpb = jax.random.normal(ks[3], (DEPTH, NA_HEADS, 2 * NA_ROWS - 1, 2 * NA_COLS - 1), f32) * 0.1
    sink = jax.random.normal(ks[4], (DEPTH, WG_HEADS), f32)
    conv_w = jax.random.normal(ks[5], (DEPTH, SC_KSIZE, SC_WIDTH), f32) * SC_KSIZE ** -0.5
    w_branch = jax.random.normal(ks[6], (DEPTH, N_BRANCH, BRANCH_WIDTH, D_MODEL), f32) * (BRANCH_WIDTH ** -0.5 * DEEPNORM_BETA)
    w_out = jax.random.normal(ks[7], (DEPTH, D_MODEL, D_MODEL), f32) * (D_MODEL ** -0.5 * DEEPNORM_BETA)
    ln_g = 1.0 + 0.02 * jax.random.normal(ks[8], (DEPTH, 2, D_MODEL), f32)
    ln_b = 0.02 * jax.random.normal(ks[9], (DEPTH, 2, D_MODEL), f32)
    w_router = jax.random.normal(ks[10], (DEPTH, D_MODEL, N_EXPERTS), f32) * D_MODEL ** -0.5
    w_gate = jax.random.normal(ks[11], (DEPTH, N_EXPERTS, D_MODEL, D_FF), f32) * D_MODEL ** -0.5
    w_up = jax.random.normal(ks[12], (DEPTH, N_EXPERTS, D_MODEL, D_FF), f32) * D_MODEL ** -0.5
    w_down = jax.random.normal(ks[13], (DEPTH, N_EXPERTS, D_FF, D_MODEL), f32) * (D_FF ** -0.5 * DEEPNORM_BETA)
    return {'x': x, 'w_in': w_in, 'b_gate': b_gate, 'rpb': rpb, 'sink': sink, 'conv_w': conv_w,
            'w_branch': w_branch, 'w_out': w_out, 'ln_g': ln_g, 'ln_b': ln_b, 'w_router': w_router,
            'w_gate': w_gate, 'w_up': w_up, 'w_down': w_down}


def reference(x, w_in, b_gate, rpb, sink, conv_w, w_branch, w_out, ln_g, ln_b, w_router, w_gate, w_up, w_down):
    b, s, d = x.shape
    offsets = tuple(int(o) for o in np.cumsum(IN_SPLITS)[:-1])
    for l in range(DEPTH):
        proj = jnp.einsum('bsd,dn->bsn', x, w_in[l])
        (qa, ka, va, qb, kb, vb, bg, cg, hc, gate_logits) = jnp.split(proj, offsets, axis=-1)
        ya = neighbourhood_attention(qa.reshape(b, s, NA_HEADS, HEAD_DIM), ka.reshape(b, s, NA_HEADS, HEAD_DIM),
                                     va.reshape(b, s, NA_HEADS, HEAD_DIM), rpb[l])
        yb = windowed_gqa(qb.reshape(b, s, WG_HEADS, HEAD_DIM), kb.reshape(b, s, WG_KV_HEADS, HEAD_DIM),
                          vb.reshape(b, s, WG_KV_HEADS, HEAD_DIM), sink[l])
        yc = short_conv_mixer(bg, cg, hc, conv_w[l])
        branches = jnp.stack([ya, yb, yc], axis=2)
        branches = jnp.einsum('bsnc,ncd->bsnd', branches, w_branch[l])
        gates = jax.nn.sigmoid(gate_logits.reshape(b, s, N_BRANCH, d) + b_gate[l].reshape(N_BRANCH, d))
        merged = jnp.sum(gates * branches, axis=2)
        mix = jnp.einsum('bsd,de->bse', merged, w_out[l])
        x = layer_norm(DEEPNORM_ALPHA * x + mix, ln_g[l, 0], ln_b[l, 0])
        moe = expert_choice_ffn(x, w_router[l], w_gate[l], w_up[l], w_down[l])
        x = layer_norm(DEEPNORM_ALPHA * x + moe, ln_g[l, 1], ln_b[l, 1])
    return x
```

```python
import numpy as np
from contextlib import ExitStack
import ml_dtypes
import concourse.bass as bass
import concourse.mybir as mybir
from concourse.bass_utils import run_bass_kernel_spmd
from concourse.bass import IndirectOffsetOnAxis

F32 = mybir.dt.float32
BF16 = mybir.dt.bfloat16
I32 = mybir.dt.int32
ALU = mybir.AluOpType
AF = mybir.ActivationFunctionType
AX = mybir.AxisListType
NPBF = ml_dtypes.bfloat16


class _Op:
    __slots__ = ("eng", "fn", "dma", "deps", "signal", "sig", "n")

    def __init__(self, eng, fn, dma):
        self.eng, self.fn, self.dma = eng, fn, dma
        self.deps, self.signal, self.sig, self.n = [], False, None, 0


class Sched:
    ENGS = ("pe", "act", "dve", "pool", "sp")
    NSLOT = {"sp": 8, "act": 4, "pool": 4, "pe": 0, "dve": 0}

    def __init__(self, nc):
        self.nc = nc
        self.ops = {e: [] for e in self.ENGS}
        self.last_w = {}
        self.readers = {}
        self.ndma = {e: 0 for e in self.ENGS}
        self.pending_bar = {}
        self.bar_mark = {e: 0 for e in self.ENGS}

    def barrier(self):
        deps = []
        for e in self.ENGS:
            lst = self.ops[e]
            lastc = None
            for o in lst:
                if not o.dma:
                    lastc = o
            if lastc is not None:
                deps.append(lastc)
            for o in lst[self.bar_mark[e]:]:
                if o.dma:
                    deps.append(o)
            self.bar_mark[e] = len(lst)
        for e in self.ENGS:
            self.pending_bar[e] = list(deps)

    def op(self, eng, fn, r=(), w=(), dma=False, strict=False):
        o = _Op(eng, fn, dma)
        deps = []
        seen = set()
        for k in r:
            p = self.last_w.get(k)
            if p is not None and id(p) not in seen:
                seen.add(id(p)); deps.append(p)
        for k in w:
            p = self.last_w.get(k)
            if p is not None and id(p) not in seen:
                seen.add(id(p)); deps.append(p)
            for p in self.readers.get(k, ()):
                if id(p) not in seen:
                    seen.add(id(p)); deps.append(p)
        if eng in self.pending_bar:
            for p in self.pending_bar.pop(eng):
                if id(p) not in seen:
                    seen.add(id(p)); deps.append(p)
        o.deps = [p for p in deps if strict or not (p.eng == eng and not p.dma and not dma)]
        for p in o.deps:
            p.signal = True
        for k in r:
            self.readers.setdefault(k, []).append(o)
        for k in w:
            self.last_w[k] = o
            self.readers[k] = []
        if dma:
            o.n = self.ndma[eng]
            self.ndma[eng] += 1
        self.ops[eng].append(o)
        return o

    def dma(self, eng, out, in_, r=(), w=()):
        return self.op(eng, lambda e: e.dma_start(out=out, in_=in_), r, w, dma=True)

    def emit(self, stack):
        nc = self.nc
        csem = {e: stack.enter_context(nc.semaphore("c_" + e)) for e in ("pe", "act", "dve", "pool")}
        dsem = {}
        for e in self.ENGS:
            if self.ndma[e]:
                dsem[e] = [stack.enter_context(nc.semaphore(f"d_{e}{i}")) for i in range(self.NSLOT[e])]
        for e in self.ENGS:
            cnt = 0
            for o in self.ops[e]:
                if o.dma:
                    K = self.NSLOT[e]
                    o.sig = (dsem[e][o.n % K], 16 * (o.n // K + 1))
                elif o.signal:
                    cnt += 1
                    o.sig = (csem[e], cnt)
            assert cnt < 60000, (e, cnt)
        block = stack.enter_context(nc.Block())
        sched = self

        def run(e, eng):
            waited = {}
            for o in sched.ops[e]:
                waits = {}
                for p in o.deps:
                    s, v = p.sig
                    if waits.get(id(s), (None, 0))[1] < v:
                        waits[id(s)] = (s, v)
                if o.dma:
                    K = sched.NSLOT[e]
                    if o.n >= K:
                        s = dsem[e][o.n % K]
                        v = 16 * (o.n // K)
                        if waits.get(id(s), (None, 0))[1] < v:
                            waits[id(s)] = (s, v)
                for s, v in waits.values():
                    if waited.get(id(s), 0) < v:
                        eng.wait_ge(s, v)
                        waited[id(s)] = v
                ins = o.fn(eng)
                if o.dma:
                    ins.then_inc(o.sig[0], 16)
                elif o.signal:
                    ins.then_inc(o.sig[0], 1)
            if sched.ndma[e]:
                K = sched.NSLOT[e]
                n = sched.ndma[e]
                for i in range(min(K, n)):
                    cntslot = (n - 1 - i) // K + 1
                    v = 16 * cntslot
                    s = dsem[e][i]
                    if waited.get(id(s), 0) < v:
                        eng.wait_ge(s, v)

        @block.tensor
        def _(eng):
            run("pe", eng)

        @block.scalar
        def _(eng):
            run("act", eng)

        @block.vector
        def _(eng):
            run("dve", eng)

        @block.gpsimd
        def _(eng):
            run("pool", eng)

        @block.sync
        def _(eng):
            run("sp", eng)


SCALE = 128 ** -0.5
ALPHA = 8 ** 0.25
EPS = 1e-5
NT = 1024
NE = 1536
OWN0 = 256


class Arena:
    def __init__(self, t, size):
        self.t, self.size, self.off = t, size, 0

    def take(self, nbytes, dtype, shape=None):
        nel = nbytes // 2
        a = self.t[:, self.off:self.off + nel]
        self.off += nel
        assert self.off <= self.size, (self.off, self.size)
        if dtype == F32:
            a = a.bitcast(F32)
        elif dtype == I32:
            a = a.bitcast(I32)
        if shape is not None and len(shape) == 2:
            a = a.rearrange("p (a b) -> p a b", a=shape[0])
        return a


def build_M(debug=None):
    nc = bass.Bass("TRN2", target_bir_lowering=False)
    dt = nc.dram_tensor
    xT = dt("xT", [2048, NE], F32, kind="ExternalInput").ap()
    xrow = dt("xrow", [NT, 2048], F32, kind="ExternalInput").ap()
    w_in = dt("w_in", [2048, 13824], F32, kind="ExternalInput").ap()
    w_br = dt("w_br", [3, 1024, 2048], F32, kind="ExternalInput").ap()
    w_out = dt("w_out", [2048, 2048], F32, kind="ExternalInput").ap()
    bgT = dt("bgT", [128, 48], F32, kind="ExternalInput").ap()
    lng = dt("lng", [128, 2048], F32, kind="ExternalInput").ap()
    lnb = dt("lnb", [128, 2048], F32, kind="ExternalInput").ap()
    convT = dt("convT", [128, 8, 3], F32, kind="ExternalInput").ap()
    sinkb = dt("sinkb", [128, 8], F32, kind="ExternalInput").ap()
    emask = dt("emask", [128, 2], F32, kind="ExternalInput").ap()
    tabA = dt("tabA", [8, 2, 8, 128, 512], F32, kind="ExternalInput").ap()
    tabB = dt("tabB", [8, 2, 6, 128, 512], BF16, kind="ExternalInput").ap()
    x1o = dt("x1", [NT, 2048], F32, kind="ExternalOutput").ap()
    mrg = dt("mrg", [16, 128, NT], BF16, kind="Internal").ap()
    dbg = None
    if debug == "yT":
        dbg = dt("dbg", [24, 128, NT], BF16, kind="ExternalOutput").ap()
    if debug == "mrg":
        dbg = dt("dbg", [16, 128, NT], BF16, kind="ExternalOutput").ap()

    with ExitStack() as st:
        S = Sched(nc)
        ASZ = 103000
        arena_t = st.enter_context(nc.sbuf_tensor("arena", [128, ASZ], BF16))
        AR = Arena(arena_t, ASZ)
        PS = [st.enter_context(nc.psum_tensor(f"ps{i}", [128, 512], F32)) for i in range(8)]

        ones = AR.take(256, BF16)
        bg_sb = AR.take(192, F32)
        conv_sb = AR.take(96, F32, (8, 3))
        esink = AR.take(32, F32)
        em_sb = AR.take(8, F32)
        AR.off = (AR.off + 15) // 16 * 16
        wst = [AR.take(8192, F32, (4, 512)) for _ in range(2)]
        wbf = [AR.take(16384, BF16, (16, 512)) for _ in range(2)]
        yT = AR.take(49152, BF16, (24, NT))
        xTb = AR.take(49152, BF16, (16, NE))
        Z0 = AR.off
        qT = AR.take(8192, BF16, (4, NT))
        kT = AR.take(12288, BF16, (4, NE))
        vv = AR.take(12288, BF16, (12, 512))
        tab = [AR.take(2048, F32) for _ in range(2)]
        tabb = [AR.take(1024, BF16) for _ in range(2)]
        Lb = [AR.take(2048, F32) for _ in range(2)]
        PT = [AR.take(1024, BF16) for _ in range(2)]
        rden = AR.take(2048, F32)
        endAB = AR.off

        S.op("pool", lambda e: e.memset(ones, 1.0), w=["ones"])
        S.dma("sp", bg_sb, bgT[:, :], w=["bg_sb"])
        S.dma("sp", conv_sb, convT[:, :, :], w=["conv_sb"])
        S.dma("sp", esink, sinkb[:, :], w=["esink"])
        S.dma("sp", em_sb, emask[:, :], w=["em_sb"])
        S.op("act", lambda e: e.activation(out=esink, in_=esink, func=AF.Exp), r=["esink"], w=["esink"])

        stg_n = [0]

        def stage_cast(src_fn, dst_fn, nk, ncols, keyw):
            for k0 in range(0, nk, 4):
                k1 = min(nk, k0 + 4)
                i = stg_n[0] % 2
                stg_n[0] += 1
                S.dma("sp", wst[i][:, 0:k1 - k0, 0:ncols], src_fn(k0, k1), w=[f"wst{i}"])
                src = wst[i][:, 0:k1 - k0, 0:ncols]
                dst = dst_fn(k0, k1)
                S.op("pool", lambda e, s=src, d=dst: e.tensor_copy(out=d, in_=s), r=[f"wst{i}"], w=[keyw])

        xT_v = xT.rearrange("(k p) t -> p k t", p=128)
        for tb in range(3):
            stage_cast(lambda k0, k1, tb=tb: xT_v[:, k0:k1, tb * 512:(tb + 1) * 512],
                       lambda k0, k1, tb=tb: xTb[:, k0:k1, tb * 512:(tb + 1) * 512], 16, 512, "xTb")

        w_in_v = w_in.rearrange("(k p) n -> p k n", p=128)
        wn = [0]

        def load_w(src_v, col0, ncols=512, nk=16):
            i = wn[0] % 2
            wn[0] += 1
            stage_cast(lambda k0, k1: src_v[:, k0:k1, col0:col0 + ncols],
                       lambda k0, k1: wbf[i][:, k0:k1, 0:ncols], nk, ncols, f"wbf{i}")
            return wbf[i], f"wbf{i}"

        psn = [0]

        def next_ps(lo=0, n=2):
            i = lo + psn[0] % n
            psn[0] += 1
            return PS[i], f"ps{i}"

        def mm_group(pst, pskey, pairs, extra_r, ncols):
            def fn(e):
                ins = None
                n = len(pairs)
                for j, (l, r_) in enumerate(pairs):
                    ins = e.matmul(pst[:, 0:ncols], l, r_, start=(j == 0), stop=(j == n - 1))
                return ins
            S.op("pe", fn, r=list(extra_r), w=[pskey])

        EXT_BLKS = [(0, 512), (512, 512), (1024, 512)]
        OWN_BLKS = [(OWN0, 512), (OWN0 + 512, 512)]
        evn = [0]

        def evac(dst, src, rk, wk):
            evn[0] += 1
            if evn[0] % 2:
                S.op("act", lambda e: e.copy(out=dst, in_=src), r=rk, w=wk)
            else:
                S.op("dve", lambda e: e.tensor_copy(out=dst, in_=src), r=rk, w=wk)

        def proj_fm(W, wkey, wc0, blks, dst_fn, dkey):
            for bi, (t0, sz) in enumerate(blks):
                pst, pk = next_ps(0, 2)
                mm_group(pst, pk, [(W[:, k, wc0:wc0 + 128], xTb[:, k, t0:t0 + sz]) for k in range(16)],
                         [wkey, "xTb"], sz)
                evac(dst_fn(bi), pst[:, 0:sz], [pk], [dkey])

        def attention(qh, kh, vc0, yidx, nkc, tile0_fn, tab_src_fn, tab_is_bf, sink_col):
            for qb in range(2):
                acc, acck = next_ps(4, 2)
                dsel = 6 + (psn[0] % 2)
                den, denk = PS[dsel], f"ps{dsel}"
                for kc in range(nkc):
                    tile = tile0_fn(qb) + kc
                    stp, stk = next_ps(2, 2)
                    i2 = psn[0] % 2
                    if tab_is_bf:
                        tb_ap, tbk = tabb[i2], f"tabb{i2}"
                    else:
                        tb_ap, tbk = tab[i2], f"tab{i2}"
                    S.dma("sp", tb_ap, tab_src_fn(qb, kc), w=[tbk])
                    S.op("pe", lambda e, p=stp, tile=tile, qb=qb: e.matmul(
                        p[:, :], kT[:, kh, tile * 128:(tile + 1) * 128], qT[:, qh, qb * 512:(qb + 1) * 512],
                        start=True, stop=True), r=[f"kT{kh}", f"qT{qh}"], w=[stk])
                    S.op("dve", lambda e, p=stp, l=Lb[i2], t=tb_ap: e.scalar_tensor_tensor(
                        out=l, in0=p[:, :], scalar=SCALE, in1=t, op0=ALU.mult, op1=ALU.add),
                        r=[stk, tbk], w=[f"L{i2}"])
                    S.op("act", lambda e, l=Lb[i2], pt=PT[i2]: e.activation(out=pt, in_=l, func=AF.Exp),
                         r=[f"L{i2}"], w=[f"PT{i2}"])
                    S.op("pe", lambda e, a=acc, pt=PT[i2], tile=tile, kc=kc: e.matmul(
                        a[:, :], vv[:, tile, vc0:vc0 + 128], pt, start=(kc == 0), stop=(kc == nkc - 1)),
                        r=[f"vv{tile}", f"PT{i2}"], w=[acck])
                    S.op("pe", lambda e, dn=den, pt=PT[i2], kc=kc: e.matmul(
                        dn[:, :], ones, pt, start=(kc == 0), stop=(kc == nkc - 1)),
                        r=["ones", f"PT{i2}"], w=[denk])
                if sink_col is not None:
                    S.op("dve", lambda e, dn=den: e.tensor_scalar(
                        out=rden, in0=dn[:, :], scalar1=esink[:, sink_col:sink_col + 1], scalar2=None, op0=ALU.add),
                        r=[denk, "esink"], w=["rden"])
                    S.op("dve", lambda e: e.reciprocal(out=rden, in_=rden), r=["rden"], w=["rden"])
                else:
                    S.op("dve", lambda e, dn=den: e.reciprocal(out=rden, in_=dn[:, :]), r=[denk], w=["rden"])
                S.op("dve", lambda e, a=acc, qb=qb: e.tensor_tensor(
                    out=yT[:, yidx, qb * 512:(qb + 1) * 512], in0=a[:, :], in1=rden, op=ALU.mult),
                    r=[acck, "rden"], w=[f"yT{yidx}"])

        for hg in range(2):
            Wq, kq = load_w(w_in_v, hg * 512)
            for hh in range(4):
                proj_fm(Wq, kq, hh * 128, OWN_BLKS, lambda bi, hh=hh: qT[:, hh, bi * 512:(bi + 1) * 512], f"qT{hh}")
            Wk, kk = load_w(w_in_v, 1024 + hg * 512)
            for hh in range(4):
                proj_fm(Wk, kk, hh * 128, EXT_BLKS, lambda bi, hh=hh: kT[:, hh, bi * 512:(bi + 1) * 512], f"kT{hh}")
            Wv, kv = load_w(w_in_v, 2048 + hg * 512)
            for tile in range(12):
                pst, pk = next_ps(0, 2)
                mm_group(pst, pk, [(xTb[:, k, tile * 128:(tile + 1) * 128], Wv[:, k, 0:512]) for k in range(16)],
                         [kv, "xTb"], 512)
                evac(vv[:, tile, :], pst[:, :], [pk], [f"vv{tile}"])
            for hh in range(4):
                h = hg * 4 + hh
                attention(hh, hh, hh * 128, h, 8, lambda qb: qb * 4,
                          lambda qb, kc, h=h: tabA[h, qb, kc, :, :], False, None)

        Wkv, kkv = load_w(w_in_v, 4096)
        for kvh in range(2):
            proj_fm(Wkv, kkv, kvh * 128, EXT_BLKS, lambda bi, kvh=kvh: kT[:, kvh, bi * 512:(bi + 1) * 512], f"kT{kvh}")
        for tile in range(12):
            pst, pk = next_ps(0, 2)
            mm_group(pst, pk, [(xTb[:, k, tile * 128:(tile + 1) * 128], Wkv[:, k, 256:512]) for k in range(16)],
                     [kkv, "xTb"], 256)
            evac(vv[:, tile, 0:256], pst[:, 0:256], [pk], [f"vv{tile}"])
        for hg in range(2):
            Wq, kq = load_w(w_in_v, 3072 + hg * 512)
            for hh in range(4):
                proj_fm(Wq, kq, hh * 128, OWN_BLKS, lambda bi, hh=hh: qT[:, hh, bi * 512:(bi + 1) * 512], f"qT{hh}")
            for hh in range(4):
                h = hg * 4 + hh
                attention(hh, hg, hg * 128, 8 + h, 6, lambda qb: qb * 4 + 1,
                          lambda qb, i, h=h: tabB[h, qb, i, :, :], True, h)

        S.barrier()
        AR.off = Z0
        cgs = AR.take(4 * 1026 * 4, F32, (4, 1026))
        tt = AR.take(4096, F32)
        CBLK = [(OWN0 - 1, 512), (OWN0 + 511, 512), (OWN0 + 1023, 2)]
        for half in range(2):
            Wc, kc_ = load_w(w_in_v, 5632 + half * 512)
            for cc in range(4):
                for bi, (t0, sz) in enumerate(CBLK):
                    pst, pk = next_ps(0, 2)
                    mm_group(pst, pk, [(Wc[:, k, cc * 128:(cc + 1) * 128], xTb[:, k, t0:t0 + sz]) for k in range(16)],
                             [kc_, "xTb"], sz)
                    S.op("act", lambda e, p=pst, bi=bi, sz=sz, cc=cc: e.copy(
                        out=cgs[:, cc, bi * 512:bi * 512 + sz], in_=p[:, 0:sz]), r=[pk], w=[f"cgs{cc}"])
            Wh, kh_ = load_w(w_in_v, 6656 + half * 512)
            for cc in range(4):
                for bi, (t0, sz) in enumerate(CBLK):
                    pst, pk = next_ps(0, 2)
                    mm_group(pst, pk, [(Wh[:, k, cc * 128:(cc + 1) * 128], xTb[:, k, t0:t0 + sz]) for k in range(16)],
                             [kh_, "xTb"], sz)
                    S.op("dve", lambda e, p=pst, bi=bi, sz=sz, cc=cc: e.tensor_tensor(
                        out=cgs[:, cc, bi * 512:bi * 512 + sz], in0=p[:, 0:sz], in1=cgs[:, cc, bi * 512:bi * 512 + sz],
                        op=ALU.mult), r=[pk, f"cgs{cc}"], w=[f"cgs{cc}"])
                S.op("dve", lambda e, cc=cc: e.tensor_tensor(out=cgs[:, cc, 0:1], in0=cgs[:, cc, 0:1], in1=em_sb[:, 0:1],
                                                             op=ALU.mult), r=[f"cgs{cc}", "em_sb"], w=[f"cgs{cc}"])
                S.op("dve", lambda e, cc=cc: e.tensor_tensor(out=cgs[:, cc, 1025:1026], in0=cgs[:, cc, 1025:1026],
                                                             in1=em_sb[:, 1:2], op=ALU.mult),
                     r=[f"cgs{cc}", "em_sb"], w=[f"cgs{cc}"])
            Wb, kb_ = load_w(w_in_v, 4608 + half * 512)
            for cc in range(4):
                c = half * 4 + cc
                S.op("dve", lambda e, cc=cc, c=c: e.tensor_scalar(
                    out=tt, in0=cgs[:, cc, 0:1024], scalar1=conv_sb[:, c, 0:1], scalar2=None, op0=ALU.mult),
                    r=[f"cgs{cc}", "conv_sb"], w=["tt"])
                S.op("dve", lambda e, cc=cc, c=c: e.scalar_tensor_tensor(
                    out=tt, in0=cgs[:, cc, 1:1025], scalar=conv_sb[:, c, 1:2], in1=tt, op0=ALU.mult, op1=ALU.add),
                    r=[f"cgs{cc}", "conv_sb", "tt"], w=["tt"])
                S.op("dve", lambda e, cc=cc, c=c: e.scalar_tensor_tensor(
                    out=tt, in0=cgs[:, cc, 2:1026], scalar=conv_sb[:, c, 2:3], in1=tt, op0=ALU.mult, op1=ALU.add),
                    r=[f"cgs{cc}", "conv_sb", "tt"], w=["tt"])
                for bi in range(2):
                    t0 = OWN0 + bi * 512
                    pst, pk = next_ps(2, 2)
                    mm_group(pst, pk, [(Wb[:, k, cc * 128:(cc + 1) * 128], xTb[:, k, t0:t0 + 512]) for k in range(16)],
                             [kb_, "xTb"], 512)
                    S.op("dve", lambda e, p=pst, bi=bi, c=c: e.tensor_tensor(
                        out=yT[:, 16 + c, bi * 512:(bi + 1) * 512], in0=p[:, :], in1=tt[:, bi * 512:(bi + 1) * 512],
                        op=ALU.mult), r=[pk, "tt"], w=[f"yT{16 + c}"])

        if debug == "yT":
            for i in range(24):
                S.dma("sp", dbg[i, :, :], yT[:, i, :], r=[f"yT{i}"])

        S.barrier()
        AR.off = Z0
        macc = AR.take(4 * 1024 * 4, F32, (4, NT))
        gsb = [AR.take(2048, F32) for _ in range(2)]
        tmp = [AR.take(2048, F32) for _ in range(2)]
        mbf = [AR.take(1024, BF16) for _ in range(2)]
        w_br_v = w_br.rearrange("n (k p) d -> n p k d", p=128)
        gn = [0]
        for G in range(4):
            for n in range(3):
                Wg, kg = load_w(w_in_v, 7680 + n * 2048 + G * 512)
                Wr, kr = load_w(w_br_v[n], G * 512, 512, 8)
                for j in range(4):
                    chunk = G * 4 + j
                    for tb in range(2):
                        pg, pgk = next_ps(0, 2)
                        mm_group(pg, pgk, [(Wg[:, k, j * 128:(j + 1) * 128], xTb[:, k, OWN0 + tb * 512:OWN0 + (tb + 1) * 512])
                                           for k in range(16)], [kg, "xTb"], 512)
                        pb, pbk = next_ps(2, 2)
                        mm_group(pb, pbk, [(Wr[:, k, j * 128:(j + 1) * 128], yT[:, n * 8 + k, tb * 512:(tb + 1) * 512])
                                           for k in range(8)], [kr] + [f"yT{n * 8 + k}" for k in range(8)], 512)
                        gi = gn[0] % 2
                        gn[0] += 1
                        bcol = n * 16 + chunk
                        S.op("act", lambda e, p=pg, gi=gi, bcol=bcol: e.activation(
                            out=gsb[gi], in_=p[:, :], func=AF.Sigmoid, bias=bg_sb[:, bcol:bcol + 1]),
                            r=[pgk, "bg_sb"], w=[f"gsb{gi}"])
                        mslice = macc[:, j, tb * 512:(tb + 1) * 512]
                        mk = f"macc{j}_{tb}"
                        if n == 0:
                            S.op("dve", lambda e, p=pb, gi=gi, m=mslice: e.tensor_tensor(
                                out=m, in0=p[:, :], in1=gsb[gi], op=ALU.mult), r=[pbk, f"gsb{gi}"], w=[mk])
                        else:
                            S.op("dve", lambda e, p=pb, gi=gi: e.tensor_tensor(
                                out=tmp[gi], in0=p[:, :], in1=gsb[gi], op=ALU.mult), r=[pbk, f"gsb{gi}"], w=[f"tmp{gi}"])
                            if n == 1:
                                S.op("pool", lambda e, gi=gi, m=mslice: e.tensor_tensor(
                                    out=m, in0=m, in1=tmp[gi], op=ALU.add), r=[f"tmp{gi}", mk], w=[mk])
                            else:
                                S.op("pool", lambda e, gi=gi, m=mslice, tb=tb: e.tensor_tensor(
                                    out=mbf[tb], in0=m, in1=tmp[gi], op=ALU.add), r=[f"tmp{gi}", mk], w=[f"mbf{tb}"])
                                S.dma("sp", mrg[chunk, :, tb * 512:(tb + 1) * 512], mbf[tb], r=[f"mbf{tb}"], w=[f"mrg{chunk}"])
        if debug == "mrg":
            S.barrier()
            for i in range(16):
                S.dma("sp", yT[:, i, :], mrg[i, :, :], r=[f"mrg{i}"], w=[f"yTm{i}"])
                S.dma("sp", dbg[i, :, :], yT[:, i, :], r=[f"yTm{i}"])

        S.barrier()
        AR.off = Z0 - 24576
        x1buf = AR.take(8 * 2048 * 4, F32, (8, 2048))
        xr = [AR.take(2048, F32) for _ in range(2)]
        g_sb = AR.take(8192, F32)
        b_sb = AR.take(8192, F32)
        stats = AR.take(4 * 6 * 4, F32, (4, 6))
        mv = AR.take(16, F32)
        rstd = AR.take(16, F32)
        mT = yT
        for i in range(16):
            S.dma("sp", mT[:, i, :], mrg[i, :, :], r=[f"mrg{i}"], w=[f"mT{i}"])
        S.dma("sp", g_sb, lng[:, :], w=["g_sb"])
        S.dma("sp", b_sb, lnb[:, :], w=["b_sb"])
        w_out_v = w_out.rearrange("(k p) n -> p k n", p=128)
        xn = [0]
        for G in range(4):
            Wo, ko = load_w(w_out_v, G * 512)
            for t8 in range(8):
                pst, pk = next_ps(0, 4)
                mm_group(pst, pk, [(mT[:, k, t8 * 128:(t8 + 1) * 128], Wo[:, k, 0:512]) for k in range(16)],
                         [ko] + [f"mT{k}" for k in range(16)], 512)
                xi = xn[0] % 2
                xn[0] += 1
                S.dma("sp", xr[xi], xrow[t8 * 128:(t8 + 1) * 128, G * 512:(G + 1) * 512], w=[f"xr{xi}"])
                S.op("dve", lambda e, p=pst, xi=xi, t8=t8, G=G: e.scalar_tensor_tensor(
                    out=x1buf[:, t8, G * 512:(G + 1) * 512], in0=xr[xi], scalar=ALPHA, in1=p[:, :],
                    op0=ALU.mult, op1=ALU.add), r=[pk, f"xr{xi}"], w=[f"x1b{t8}"])
        for t8 in range(8):
            k8 = f"x1b{t8}"
            for q in range(4):
                S.op("dve", lambda e, t8=t8, q=q: e.bn_stats(out=stats[:, q, :], in_=x1buf[:, t8, q * 512:(q + 1) * 512]),
                     r=[k8], w=[f"stats{q}"], strict=True)
            S.op("dve", lambda e: e.bn_aggr(out=mv[:, 0:2], in_=stats.rearrange("p a b -> p (a b)")), r=[f"stats{q}" for q in range(4)], w=["mv"], strict=True)
            S.op("dve", lambda e: e.tensor_scalar(out=rstd[:, 0:1], in0=mv[:, 1:2], scalar1=EPS, scalar2=None,
                                                  op0=ALU.add), r=["mv"], w=["rstd"], strict=True)
            S.op("act", lambda e: e.activation(out=rstd[:, 0:1], in_=rstd[:, 0:1], func=AF.Sqrt), r=["rstd"], w=["rstd"])
            S.op("dve", lambda e: e.reciprocal(out=rstd[:, 0:1], in_=rstd[:, 0:1]), r=["rstd"], w=["rstd"], strict=True)
            S.op("dve", lambda e, t8=t8: e.tensor_scalar(out=x1buf[:, t8, :], in0=x1buf[:, t8, :], scalar1=mv[:, 0:1],
                                                         scalar2=rstd[:, 0:1], op0=ALU.subtract, op1=ALU.mult),
                 r=[k8, "mv", "rstd"], w=[k8], strict=True)
            S.op("pool", lambda e, t8=t8: e.tensor_tensor(out=x1buf[:, t8, :], in0=x1buf[:, t8, :], in1=g_sb, op=ALU.mult),
                 r=[k8, "g_sb"], w=[k8])
            S.op("pool", lambda e, t8=t8: e.tensor_tensor(out=x1buf[:, t8, :], in0=x1buf[:, t8, :], in1=b_sb, op=ALU.add),
                 r=[k8, "b_sb"], w=[k8])
            S.dma("sp", x1o[t8 * 128:(t8 + 1) * 128, :], x1buf[:, t8, :], r=[k8])
        S.emit(st)
    return nc

import numpy as np
import ml_dtypes
NPBF = ml_dtypes.bfloat16
NEG = -30000.0


def make_tabA(rpb_l, c):
    out = np.empty((8, 2, 8, 128, 512), np.float32)
    kk = np.arange(128)[:, None]
    qq = np.arange(512)[None, :]
    for qb in range(2):
        gq = 1024 * c + qb * 512 + qq
        QR, QC = gq // 64, gq % 64
        rs = np.clip(QR - 4, 0, 120)
        cs = np.clip(QC - 8, 0, 48)
        for kc in range(8):
            tile = qb * 4 + kc
            gk = 1024 * c - 256 + tile * 128 + kk
            KR, KC = gk // 64, gk % 64
            valid = (gk >= 0) & (gk < 8192) & (KR >= rs) & (KR <= rs + 7) & (KC >= cs) & (KC <= cs + 15)
            ri = np.clip(KR - QR + 7, 0, 14)
            ci = np.clip(KC - QC + 15, 0, 30)
            ri, ci = np.broadcast_arrays(ri, ci)
            vals = rpb_l[:, ri, ci]
            out[:, qb, kc] = np.where(valid[None], vals, np.float32(NEG))
    return out


_tabB_cache = {}


def make_tabB(c):
    if c in _tabB_cache:
        return _tabB_cache[c]
    out = np.empty((8, 2, 6, 128, 512), np.float32)
    kk = np.arange(128)[:, None]
    qq = np.arange(512)[None, :]
    slopes = 2.0 ** (-8.0 * np.arange(1, 9, dtype=np.float32) / 8)
    for qb in range(2):
        gq = 1024 * c + qb * 512 + qq
        for i in range(6):
            tile = qb * 4 + 1 + i
            gk = 1024 * c - 256 + tile * 128 + kk
            dist = np.abs(gk - gq)
            valid = (gk >= 0) & (gk < 8192) & (dist <= 128)
            for h in range(8):
                out[h, qb, i] = np.where(valid, -slopes[h] * dist.astype(np.float32), np.float32(NEG))
    o = out.astype(NPBF)
    _tabB_cache[c] = o
    return o


def m_inputs(c, x_full, w_in_l, w_br_l, w_out_l, b_gate_l, ln_g_l0, ln_b_l0, conv_w_l, sink_l, rpb_l):
    g0 = 1024 * c - 256
    ext = np.zeros((1536, 2048), np.float32)
    lo, hi = max(g0, 0), min(g0 + 1536, 8192)
    ext[lo - g0:hi - g0] = x_full[lo:hi]
    return {
        "xT": np.ascontiguousarray(ext.T),
        "xrow": np.ascontiguousarray(x_full[1024 * c:1024 * (c + 1)]),
        "w_in": w_in_l, "w_br": w_br_l, "w_out": w_out_l,
        "bgT": np.ascontiguousarray(b_gate_l.reshape(48, 128).T),
        "lng": np.ascontiguousarray(np.broadcast_to(ln_g_l0[None, :], (128, 2048))),
        "lnb": np.ascontiguousarray(np.broadcast_to(ln_b_l0[None, :], (128, 2048))),
        "convT": np.ascontiguousarray(conv_w_l.reshape(3, 8, 128).transpose(2, 1, 0)),
        "sinkb": np.ascontiguousarray(np.broadcast_to(sink_l[None, :], (128, 8))),
        "emask": np.ascontiguousarray(np.broadcast_to(
            np.array([0.0 if c == 0 else 1.0, 0.0 if c == 7 else 1.0], np.float32)[None, :], (128, 2))),
        "tabA": make_tabA(rpb_l, c),
        "tabB": make_tabB(c),
    }


def build_R1():
    nc = bass.Bass("TRN2", target_bir_lowering=False)
    dt = nc.dram_tensor
    x1T = dt("x1T", [2048, 8192], F32, kind="ExternalInput").ap()
    wr = dt("wr", [2048, 16], F32, kind="ExternalInput").ap()
    selo = dt("sel", [128, 2, 64], F32, kind="ExternalOutput").ap()
    affo = dt("aff", [128, 2, 64], F32, kind="ExternalOutput").ap()
    with ExitStack() as st:
        S = Sched(nc)
        ASZ = 60000
        AR = Arena(st.enter_context(nc.sbuf_tensor("arena", [128, ASZ], BF16)), ASZ)
        PS = [st.enter_context(nc.psum_tensor(f"ps{i}", [128, 512], F32)) for i in range(4)]
        wr_sb = AR.take(16 * 16 * 4, F32, (16, 16))
        xt = [AR.take(16 * 128 * 4, F32, (16, 128)) for _ in range(3)]
        E = AR.take(64 * 16 * 4, F32, (64, 16))
        ssum = AR.take(256, F32)
        A = AR.take(512, F32, (2, 64))
        cmp_ = AR.take(512, F32, (2, 64))
        ones = AR.take(256, BF16)
        lo = AR.take(8, F32); hi = AR.take(8, F32); mid = AR.take(8, F32)
        cnt = AR.take(8, F32); cntb = AR.take(4, BF16); AR.off += 2
        ge = AR.take(8, F32); d1 = AR.take(8, F32); d2 = AR.take(8, F32)
        S.op("pool", lambda e: e.memset(ones, 1.0), w=["ones"])
        S.op("pool", lambda e: e.memset(lo, 0.0), w=["lo"])
        S.op("pool", lambda e: e.memset(hi, 1.0), w=["hi"])
        S.dma("sp", wr_sb, wr.rearrange("(k p) e -> p k e", p=128), w=["wr"])
        x1T_v = x1T.rearrange("(k p) t -> p k t", p=128)
        for j in range(64):
            i = j % 3
            S.dma("sp", xt[i], x1T_v[:, :, j * 128:(j + 1) * 128], w=[f"xt{i}"])
            pst = PS[j // 32]

            def fn(e, i=i, j=j, pst=pst):
                ins = None
                for k in range(16):
                    ins = e.matmul(pst[:, (j % 32) * 16:(j % 32) * 16 + 16], xt[i][:, k, :], wr_sb[:, k, :],
                                   start=(k == 0), stop=(k == 15))
                return ins
            S.op("pe", fn, r=[f"xt{i}", "wr"], w=[f"lg{j // 32}"])
        for b in range(2):
            S.op("act", lambda e, b=b: e.activation(out=E[:, b * 32:(b + 1) * 32, :].rearrange("p a b -> p (a b)"),
                                                    in_=PS[b][:, :], func=AF.Exp), r=[f"lg{b}"], w=["E"])
        S.op("dve", lambda e: e.reduce_sum(out=ssum, in_=E, axis=AX.X), r=["E"], w=["ssum"], strict=True)
        S.op("dve", lambda e: e.reciprocal(out=ssum, in_=ssum), r=["ssum"], w=["ssum"], strict=True)
        for ee in range(2):
            S.op("dve", lambda e, ee=ee: e.tensor_tensor(out=A[:, ee, :], in0=E[:, :, ee], in1=ssum, op=ALU.mult),
                 r=["E", "ssum"], w=["A"], strict=True)
        S.dma("sp", affo[:, :, :], A, r=["A"])
        for it in range(36):
            S.op("dve", lambda e: e.tensor_tensor(out=mid, in0=lo, in1=hi, op=ALU.add), r=["lo", "hi"], w=["mid"], strict=True)
            S.op("dve", lambda e: e.tensor_scalar(out=mid, in0=mid, scalar1=0.5, scalar2=None, op0=ALU.mult),
                 r=["mid"], w=["mid"], strict=True)
            for ee in range(2):
                S.op("dve", lambda e, ee=ee: e.tensor_scalar(out=cmp_[:, ee, :], in0=A[:, ee, :], scalar1=mid[:, ee:ee + 1],
                                                             scalar2=None, op0=ALU.is_gt), r=["A", "mid"], w=["cmp"], strict=True)
            S.op("dve", lambda e: e.reduce_sum(out=cnt, in_=cmp_, axis=AX.X), r=["cmp"], w=["cnt"], strict=True)
            S.op("dve", lambda e: e.tensor_copy(out=cntb, in_=cnt), r=["cnt"], w=["cntb"], strict=True)
            S.op("pe", lambda e: e.matmul(PS[2][:, 0:2], ones, cntb, start=True, stop=True), r=["ones", "cntb"], w=["tot"])
            S.op("dve", lambda e: e.tensor_scalar(out=ge, in0=PS[2][:, 0:2], scalar1=1023.5, scalar2=None, op0=ALU.is_gt),
                 r=["tot"], w=["ge"], strict=True)
            S.op("dve", lambda e: e.tensor_tensor(out=d1, in0=mid, in1=lo, op=ALU.subtract), r=["mid", "lo"], w=["d1"], strict=True)
            S.op("dve", lambda e: e.tensor_tensor(out=d1, in0=d1, in1=ge, op=ALU.mult), r=["d1", "ge"], w=["d1"], strict=True)
            S.op("dve", lambda e: e.tensor_tensor(out=d2, in0=hi, in1=mid, op=ALU.subtract), r=["mid", "hi"], w=["d2"], strict=True)
            S.op("dve", lambda e: e.tensor_tensor(out=d2, in0=d2, in1=ge, op=ALU.mult), r=["d2", "ge"], w=["d2"], strict=True)
            S.op("dve", lambda e: e.tensor_tensor(out=lo, in0=lo, in1=d1, op=ALU.add), r=["lo", "d1"], w=["lo"], strict=True)
            S.op("dve", lambda e: e.tensor_tensor(out=hi, in0=mid, in1=d2, op=ALU.add), r=["mid", "d2"], w=["hi"], strict=True)
        for ee in range(2):
            S.op("dve", lambda e, ee=ee: e.tensor_scalar(out=cmp_[:, ee, :], in0=A[:, ee, :], scalar1=lo[:, ee:ee + 1],
                                                         scalar2=None, op0=ALU.is_gt), r=["A", "lo"], w=["cmp"], strict=True)
        S.dma("sp", selo[:, :, :], cmp_, r=["cmp"])
        S.emit(st)
    return nc


def build_R2():
    nc = bass.Bass("TRN2", target_bir_lowering=False)
    dt = nc.dram_tensor
    xeT = dt("xeT", [2, 2048, 1024], F32, kind="ExternalInput").ap()
    gs = dt("gs", [2, 128, 8], F32, kind="ExternalInput").ap()
    wg = dt("wg", [2, 2048, 1536], F32, kind="ExternalInput").ap()
    wu = dt("wu", [2, 2048, 1536], F32, kind="ExternalInput").ap()
    wd = dt("wd", [2, 1536, 2048], F32, kind="ExternalInput").ap()
    yeo = dt("ye", [2, 1024, 2048], F32, kind="ExternalOutput").ap()
    with ExitStack() as st:
        S = Sched(nc)
        ASZ = 90000
        AR = Arena(st.enter_context(nc.sbuf_tensor("arena", [128, ASZ], BF16)), ASZ)
        PS = [st.enter_context(nc.psum_tensor(f"ps{i}", [128, 512], F32)) for i in range(8)]
        wst = [AR.take(8192, F32, (4, 512)) for _ in range(2)]
        wbf = [AR.take(16384, BF16, (16, 512)) for _ in range(3)]
        xeb = AR.take(32768, BF16, (16, 1024))
        hid = AR.take(24576, BF16, (12, 1024))
        sg = [AR.take(2048, F32) for _ in range(2)]
        yo = [AR.take(2048, F32) for _ in range(2)]
        gs_sb = AR.take(64, F32, (2, 8))
        S.dma("sp", gs_sb, gs.rearrange("e p s -> p e s"), w=["gs"])
        stg_n = [0]

        def stage_cast(src_fn, dst_fn, nk, ncols, keyw):
            for k0 in range(0, nk, 4):
                k1 = min(nk, k0 + 4)
                i = stg_n[0] % 2
                stg_n[0] += 1
                S.dma("sp", wst[i][:, 0:k1 - k0, 0:ncols], src_fn(k0, k1), w=[f"wst{i}"])
                src = wst[i][:, 0:k1 - k0, 0:ncols]
                dst = dst_fn(k0, k1)
                if stg_n[0] % 3 == 0:
                    S.op("act", lambda e, s=src, d=dst: e.copy(out=d, in_=s), r=[f"wst{i}"], w=[keyw])
                else:
                    S.op("pool", lambda e, s=src, d=dst: e.tensor_copy(out=d, in_=s), r=[f"wst{i}"], w=[keyw])
        wn = [0]

        def load_w(src_v, col0, nk):
            i = wn[0] % 3
            wn[0] += 1
            stage_cast(lambda k0, k1: src_v[:, k0:k1, col0:col0 + 512],
                       lambda k0, k1: wbf[i][:, k0:k1, 0:512], nk, 512, f"wbf{i}")
            return wbf[i], f"wbf{i}"
        psn = [0]

        def next_ps(lo, n):
            i = lo + psn[0] % n
            psn[0] += 1
            return PS[i], f"ps{i}"

        def mm_group(pst, pskey, pairs, extra_r):
            def fn(e):
                ins = None
                n = len(pairs)
                for j, (l, r_) in enumerate(pairs):
                    ins = e.matmul(pst[:, :], l, r_, start=(j == 0), stop=(j == n - 1))
                return ins
            S.op("pe", fn, r=list(extra_r), w=[pskey])
        for e2 in range(2):
            xv = xeT[e2].rearrange("(k p) t -> p k t", p=128)
            for sbk in range(2):
                stage_cast(lambda k0, k1, sbk=sbk: xv[:, k0:k1, sbk * 512:(sbk + 1) * 512],
                           lambda k0, k1, sbk=sbk: xeb[:, k0:k1, sbk * 512:(sbk + 1) * 512], 16, 512, "xeb")
            wgv = wg[e2].rearrange("(k p) n -> p k n", p=128)
            wuv = wu[e2].rearrange("(k p) n -> p k n", p=128)
            wdv = wd[e2].rearrange("(k p) n -> p k n", p=128)
            gi = 0
            for fg in range(3):
                Wg, kg = load_w(wgv, fg * 512, 16)
                Wu, ku = load_w(wuv, fg * 512, 16)
                for j in range(4):
                    f = fg * 4 + j
                    for sbk in range(2):
                        pg, pgk = next_ps(0, 2)
                        mm_group(pg, pgk, [(Wg[:, k, j * 128:(j + 1) * 128], xeb[:, k, sbk * 512:(sbk + 1) * 512])
                                           for k in range(16)], [kg, "xeb"])
                        pu, puk = next_ps(2, 2)
                        mm_group(pu, puk, [(Wu[:, k, j * 128:(j + 1) * 128], xeb[:, k, sbk * 512:(sbk + 1) * 512])
                                           for k in range(16)], [ku, "xeb"])
                        gi += 1
                        S.op("act", lambda e, p=pg, g=sg[gi % 2]: e.activation(out=g, in_=p[:, :], func=AF.Silu),
                             r=[pgk], w=[f"sg{gi % 2}"])
                        S.op("dve", lambda e, p=pu, g=sg[gi % 2], f=f, sbk=sbk: e.tensor_tensor(
                            out=hid[:, f, sbk * 512:(sbk + 1) * 512], in0=p[:, :], in1=g, op=ALU.mult),
                            r=[puk, f"sg{gi % 2}"], w=[f"hid{f}"])
            for G in range(4):
                Wd, kd = load_w(wdv, G * 512, 12)
                for s8 in range(8):
                    py, pyk = next_ps(4, 4)
                    mm_group(py, pyk, [(hid[:, f, s8 * 128:(s8 + 1) * 128], Wd[:, f, 0:512]) for f in range(12)],
                             [kd] + [f"hid{f}" for f in range(12)])
                    gi += 1
                    S.op("dve", lambda e, p=py, y=yo[gi % 2], s8=s8, e2=e2: e.tensor_scalar(
                        out=y, in0=p[:, :], scalar1=gs_sb[:, e2, s8:s8 + 1], scalar2=None, op0=ALU.mult),
                        r=[pyk, "gs"], w=[f"yo{gi % 2}"])
                    S.dma("sp", yeo[e2, s8 * 128:(s8 + 1) * 128, G * 512:(G + 1) * 512], yo[gi % 2], r=[f"yo{gi % 2}"])
        S.emit(st)
    return nc


def build_C(K):
    nc = bass.Bass("TRN2", target_bir_lowering=False)
    dt = nc.dram_tensor
    x1 = dt("x1", [1024, 2048], F32, kind="ExternalInput").ap()
    Y = dt("Y", [K, 1024, 2048], F32, kind="ExternalInput").ap()
    lng = dt("lng", [128, 2048], F32, kind="ExternalInput").ap()
    lnb = dt("lnb", [128, 2048], F32, kind="ExternalInput").ap()
    x2 = dt("x2", [1024, 2048], F32, kind="ExternalOutput").ap()
    with ExitStack() as st:
        S = Sched(nc)
        ASZ = 60000
        AR = Arena(st.enter_context(nc.sbuf_tensor("arena", [128, ASZ], BF16)), ASZ)
        acc = [AR.take(8192, F32) for _ in range(2)]
        yb = [AR.take(8192, F32) for _ in range(3)]
        g_sb = AR.take(8192, F32)
        b_sb = AR.take(8192, F32)
        stats = AR.take(96, F32, (4, 6))
        mv = AR.take(16, F32)
        rstd = AR.take(16, F32)
        S.dma("sp", g_sb, lng[:, :], w=["g_sb"])
        S.dma("sp", b_sb, lnb[:, :], w=["b_sb"])
        yn = 0
        for t8 in range(8):
            a = acc[t8 % 2]
            ak = f"acc{t8 % 2}"
            S.dma("sp", a, x1[t8 * 128:(t8 + 1) * 128, :], w=[ak])
            S.op("act", lambda e, a=a: e.mul(out=a, in_=a, mul=ALPHA), r=[ak], w=[ak])
            for k in range(K):
                yi = yn % 3
                yn += 1
                S.dma("sp", yb[yi], Y[k, t8 * 128:(t8 + 1) * 128, :], w=[f"yb{yi}"])
                S.op("dve", lambda e, a=a, yi=yi: e.tensor_tensor(out=a, in0=a, in1=yb[yi], op=ALU.add),
                     r=[ak, f"yb{yi}"], w=[ak], strict=True)
            for q in range(4):
                S.op("dve", lambda e, a=a, q=q: e.bn_stats(out=stats[:, q, :], in_=a[:, q * 512:(q + 1) * 512]),
                     r=[ak], w=[f"stats{q}"], strict=True)
            S.op("dve", lambda e: e.bn_aggr(out=mv[:, 0:2], in_=stats.rearrange("p a b -> p (a b)")),
                 r=[f"stats{q}" for q in range(4)], w=["mv"], strict=True)
            S.op("dve", lambda e: e.tensor_scalar(out=rstd[:, 0:1], in0=mv[:, 1:2], scalar1=EPS, scalar2=None, op0=ALU.add),
                 r=["mv"], w=["rstd"], strict=True)
            S.op("act", lambda e: e.activation(out=rstd[:, 0:1], in_=rstd[:, 0:1], func=AF.Sqrt), r=["rstd"], w=["rstd"])
            S.op("dve", lambda e: e.reciprocal(out=rstd[:, 0:1], in_=rstd[:, 0:1]), r=["rstd"], w=["rstd"], strict=True)
            S.op("dve", lambda e, a=a: e.tensor_scalar(out=a, in0=a, scalar1=mv[:, 0:1], scalar2=rstd[:, 0:1],
                                                       op0=ALU.subtract, op1=ALU.mult), r=[ak, "mv", "rstd"], w=[ak], strict=True)
            S.op("pool", lambda e, a=a: e.tensor_tensor(out=a, in0=a, in1=g_sb, op=ALU.mult), r=[ak, "g_sb"], w=[ak])
            S.op("pool", lambda e, a=a: e.tensor_tensor(out=a, in0=a, in1=b_sb, op=ALU.add), r=[ak, "b_sb"], w=[ak], strict=True)
            S.dma("sp", x2[t8 * 128:(t8 + 1) * 128, :], a, r=[ak])
        S.emit(st)
    return nc

import numpy as np


_prog = {}
def prog(name, fn, *a):
    k = (name,) + a
    if k not in _prog:
        _prog[k] = fn(*a)
    return _prog[k]

def moe_layer(x1, w_router_l, w_gate_l, w_up_l, w_down_l, ln_g_l1, ln_b_l1):
    cores = list(range(8))
    x1T = np.ascontiguousarray(x1.T)
    in_maps = []
    for c in cores:
        perm = [2 * c, 2 * c + 1] + [e for e in range(16) if e not in (2 * c, 2 * c + 1)]
        in_maps.append({"x1T": x1T, "wr": np.ascontiguousarray(w_router_l[:, perm])})
    r1 = run_bass_kernel_spmd(prog("R1", build_R1), in_maps, core_ids=cores).results
    idx_all = np.zeros((16, 1024), np.int64)
    in_maps = []
    for c in cores:
        sel = r1[c]["sel"]; aff = r1[c]["aff"]
        xe = np.zeros((2, 2048, 1024), np.float32)
        gs = np.zeros((2, 128, 8), np.float32)
        for e2 in range(2):
            m = sel[:, e2, :].T.reshape(-1) > 0.5
            a = aff[:, e2, :].T.reshape(-1)
            idx = np.nonzero(m)[0]
            moe_layer.counts.append(len(idx))
            idx = idx[:1024]
            if len(idx) < 1024:
                idx = np.concatenate([idx, np.full(1024 - len(idx), -1)])
            idx_all[2 * c + e2] = idx
            ok = idx >= 0
            xe[e2][:, ok] = x1[idx[ok]].T
            g = np.zeros(1024, np.float32); g[ok] = a[idx[ok]]
            gs[e2] = g.reshape(8, 128).T
        in_maps.append({"xeT": xe, "gs": gs, "wg": w_gate_l[2 * c:2 * c + 2], "wu": w_up_l[2 * c:2 * c + 2],
                        "wd": w_down_l[2 * c:2 * c + 2]})
    r2 = run_bass_kernel_spmd(prog("R2", build_R2), in_maps, core_ids=cores).results
    ye = np.concatenate([r2[c]["ye"] for c in cores], axis=0)
    cnt = np.zeros(8192, np.int64)
    for e in range(16):
        ok = idx_all[e] >= 0
        cnt[idx_all[e][ok]] += 1
    K = max(int(cnt.max()), 1)
    Y = np.zeros((K, 8192, 2048), np.float32)
    cur = np.zeros(8192, np.int64)
    for e in range(16):
        ok = idx_all[e] >= 0
        ii = idx_all[e][ok]
        Y[cur[ii], ii] = ye[e][ok]
        cur[ii] += 1
    lng = np.ascontiguousarray(np.broadcast_to(ln_g_l1[None, :], (128, 2048)))
    lnb = np.ascontiguousarray(np.broadcast_to(ln_b_l1[None, :], (128, 2048)))
    in_maps = [{"x1": np.ascontiguousarray(x1[1024 * c:1024 * (c + 1)]),
                "Y": np.ascontiguousarray(Y[:, 1024 * c:1024 * (c + 1)]), "lng": lng, "lnb": lnb} for c in cores]
    r3 = run_bass_kernel_spmd(prog("C", build_C, K), in_maps, core_ids=cores).results
    return np.concatenate([r3[c]["x2"] for c in cores], axis=0)
moe_layer.counts = []


def kernel(x, w_in, b_gate, rpb, sink, conv_w, w_branch, w_out, ln_g, ln_b, w_router, w_gate, w_up, w_down):
    f = lambda a: np.ascontiguousarray(np.asarray(a, dtype=np.float32))
    xc = f(x)[0]
    cores = list(range(8))
    for l in range(4):
        args = (f(w_in[l]), f(w_branch[l]), f(w_out[l]), f(b_gate[l]), f(ln_g[l][0]), f(ln_b[l][0]), f(conv_w[l]),
                f(sink[l]), f(rpb[l]))
        in_maps = [m_inputs(c, xc, *args) for c in cores]
        res = run_bass_kernel_spmd(prog("M", build_M), in_maps, core_ids=cores).results
        x1 = np.concatenate([res[c]["x1"] for c in cores], axis=0)
        del in_maps, res
        xc = moe_layer(x1, f(w_router[l]), f(w_gate[l]), f(w_up[l]), f(w_down[l]), f(ln_g[l][1]), f(ln_b[l][1]))
    return xc[None].astype(np.float32)
```

```python
import numpy as np
from contextlib import ExitStack
import ml_dtypes
import concourse.bass as bass
import concourse.mybir as mybir
from concourse.bass_utils import run_bass_kernel_spmd
from concourse.bass import IndirectOffsetOnAxis

F32 = mybir.dt.float32
BF16 = mybir.dt.bfloat16
I32 = mybir.dt.int32
ALU = mybir.AluOpType
AF = mybir.ActivationFunctionType
AX = mybir.AxisListType
NPBF = ml_dtypes.bfloat16


class _Op:
    __slots__ = ("eng", "fn", "dma", "deps", "signal", "sig", "n")

    def __init__(self, eng, fn, dma):
        self.eng, self.fn, self.dma = eng, fn, dma
        self.deps, self.signal, self.sig, self.n = [], False, None, 0


class Sched:
    ENGS = ("pe", "act", "dve", "pool", "sp")
    NSLOT = {"sp": 8, "act": 4, "pool": 4, "pe": 0, "dve": 0}

    def __init__(self, nc):
        self.nc = nc
        self.ops = {e: [] for e in self.ENGS}
        self.last_w = {}
        self.readers = {}
        self.ndma = {e: 0 for e in self.ENGS}
        self.pending_bar = {}
        self.bar_mark = {e: 0 for e in self.ENGS}

    def barrier(self):
        deps = []
        for e in self.ENGS:
            lst = self.ops[e]
            lastc = None
            for o in lst:
                if not o.dma:
                    lastc = o
            if lastc is not None:
                deps.append(lastc)
            for o in lst[self.bar_mark[e]:]:
                if o.dma:
                    deps.append(o)
            self.bar_mark[e] = len(lst)
        for e in self.ENGS:
            self.pending_bar[e] = list(deps)

    def op(self, eng, fn, r=(), w=(), dma=False, strict=False):
        o = _Op(eng, fn, dma)
        deps = []
        seen = set()
        for k in r:
            p = self.last_w.get(k)
            if p is not None and id(p) not in seen:
                seen.add(id(p)); deps.append(p)
        for k in w:
            p = self.last_w.get(k)
            if p is not None and id(p) not in seen:
                seen.add(id(p)); deps.append(p)
            for p in self.readers.get(k, ()):
                if id(p) not in seen:
                    seen.add(id(p)); deps.append(p)
        if eng in self.pending_bar:
            for p in self.pending_bar.pop(eng):
                if id(p) not in seen:
                    seen.add(id(p)); deps.append(p)
        o.deps = [p for p in deps if strict or not (p.eng == eng and not p.dma and not dma)]
        for p in o.deps:
            p.signal = True
        for k in r:
            self.readers.setdefault(k, []).append(o)
        for k in w:
            self.last_w[k] = o
            self.readers[k] = []
        if dma:
            o.n = self.ndma[eng]
            self.ndma[eng] += 1
        self.ops[eng].append(o)
        return o

    def dma(self, eng, out, in_, r=(), w=()):
        return self.op(eng, lambda e: e.dma_start(out=out, in_=in_), r, w, dma=True)

    def emit(self, stack):
        nc = self.nc
        csem = {e: stack.enter_context(nc.semaphore("c_" + e)) for e in ("pe", "act", "dve", "pool")}
        dsem = {}
        for e in self.ENGS:
            if self.ndma[e]:
                dsem[e] = [stack.enter_context(nc.semaphore(f"d_{e}{i}")) for i in range(self.NSLOT[e])]
        for e in self.ENGS:
            cnt = 0
            for o in self.ops[e]:
                if o.dma:
                    K = self.NSLOT[e]
                    o.sig = (dsem[e][o.n % K], 16 * (o.n // K + 1))
                elif o.signal:
                    cnt += 1
                    o.sig = (csem[e], cnt)
            assert cnt < 60000, (e, cnt)
        block = stack.enter_context(nc.Block())
        sched = self

        def run(e, eng):
            waited = {}
            for o in sched.ops[e]:
                waits = {}
                for p in o.deps:
                    s, v = p.sig
                    if waits.get(id(s), (None, 0))[1] < v:
                        waits[id(s)] = (s, v)
                if o.dma:
                    K = sched.NSLOT[e]
                    if o.n >= K:
                        s = dsem[e][o.n % K]
                        v = 16 * (o.n // K)
                        if waits.get(id(s), (None, 0))[1] < v:
                            waits[id(s)] = (s, v)
                for s, v in waits.values():
                    if waited.get(id(s), 0) < v:
                        eng.wait_ge(s, v)
                        waited[id(s)] = v
                ins = o.fn(eng)
                if o.dma:
                    ins.then_inc(o.sig[0], 16)
                elif o.signal:
                    ins.then_inc(o.sig[0], 1)
            if sched.ndma[e]:
                K = sched.NSLOT[e]
                n = sched.ndma[e]
                for i in range(min(K, n)):
                    cntslot = (n - 1 - i) // K + 1
                    v = 16 * cntslot
                    s = dsem[e][i]
                    if waited.get(id(s), 0) < v:
                        eng.wait_ge(s, v)

        @block.tensor
        def _(eng):
            run("pe", eng)

        @block.scalar
        def _(eng):
            run("act", eng)

        @block.vector
        def _(eng):
            run("dve", eng)

        @block.gpsimd
        def _(eng):
            run("pool", eng)

        @block.sync
        def _(eng):
            run("sp", eng)


SCALE = 128 ** -0.5
ALPHA = 8 ** 0.25
EPS = 1e-5
NT = 1024
NE = 1536
OWN0 = 256


class Arena:
    def __init__(self, t, size):
        self.t, self.size, self.off = t, size, 0

    def take(self, nbytes, dtype, shape=None):
        nel = nbytes // 2
        a = self.t[:, self.off:self.off + nel]
        self.off += nel
        assert self.off <= self.size, (self.off, self.size)
        if dtype == F32:
            a = a.bitcast(F32)
        elif dtype == I32:
            a = a.bitcast(I32)
        if shape is not None and len(shape) == 2:
            a = a.rearrange("p (a b) -> p a b", a=shape[0])
        return a


def build_M(debug=None):
    nc = bass.Bass("TRN2", target_bir_lowering=False)
    dt = nc.dram_tensor
    xT = dt("xT", [2048, NE], F32, kind="ExternalInput").ap()
    xrow = dt("xrow", [NT, 2048], F32, kind="ExternalInput").ap()
    w_in = dt("w_in", [2048, 13824], F32, kind="ExternalInput").ap()
    w_br = dt("w_br", [3, 1024, 2048], F32, kind="ExternalInput").ap()
    w_out = dt("w_out", [2048, 2048], F32, kind="ExternalInput").ap()
    bgT = dt("bgT", [128, 48], F32, kind="ExternalInput").ap()
    lng = dt("lng", [128, 2048], F32, kind="ExternalInput").ap()
    lnb = dt("lnb", [128, 2048], F32, kind="ExternalInput").ap()
    convT = dt("convT", [128, 8, 3], F32, kind="ExternalInput").ap()
    sinkb = dt("sinkb", [128, 8], F32, kind="ExternalInput").ap()
    emask = dt("emask", [128, 2], F32, kind="ExternalInput").ap()
    tabA = dt("tabA", [8, 2, 8, 128, 512], F32, kind="ExternalInput").ap()
    tabB = dt("tabB", [8, 2, 6, 128, 512], BF16, kind="ExternalInput").ap()
    x1o = dt("x1", [NT, 2048], F32, kind="ExternalOutput").ap()
    mrg = dt("mrg", [16, 128, NT], BF16, kind="Internal").ap()
    dbg = None
    if debug == "yT":
        dbg = dt("dbg", [24, 128, NT], BF16, kind="ExternalOutput").ap()
    if debug == "mrg":
        dbg = dt("dbg", [16, 128, NT], BF16, kind="ExternalOutput").ap()

    with ExitStack() as st:
        S = Sched(nc)
        ASZ = 106000
        arena_t = st.enter_context(nc.sbuf_tensor("arena", [128, ASZ], BF16))
        AR = Arena(arena_t, ASZ)
        PS = [st.enter_context(nc.psum_tensor(f"ps{i}", [128, 512], F32)) for i in range(8)]

        ones = AR.take(256, BF16)
        bg_sb = AR.take(192, F32)
        conv_sb = AR.take(96, F32, (8, 3))
        esink = AR.take(32, F32)
        em_sb = AR.take(8, F32)
        AR.off = (AR.off + 15) // 16 * 16
        wst = [AR.take(8192, F32, (4, 512)) for _ in range(2)]
        wbf = [AR.take(16384, BF16, (16, 512)) for _ in range(2)]
        yT = AR.take(49152, BF16, (24, NT))
        xTb = AR.take(49152, BF16, (16, NE))
        Z0 = AR.off
        qT = AR.take(8192, BF16, (4, NT))
        kT = AR.take(12288, BF16, (4, NE))
        vv = AR.take(12288, BF16, (12, 512))
        NTAB = 4
        tab = [AR.take(2048, F32) for _ in range(NTAB)]
        tabb = [AR.take(1024, BF16) for _ in range(NTAB)]
        Lb = [AR.take(2048, F32) for _ in range(3)]
        PT = [AR.take(1024, BF16) for _ in range(3)]
        rden = AR.take(2048, F32)
        endAB = AR.off

        S.op("pool", lambda e: e.memset(ones, 1.0), w=["ones"])
        S.dma("sp", bg_sb, bgT[:, :], w=["bg_sb"])
        S.dma("sp", conv_sb, convT[:, :, :], w=["conv_sb"])
        S.dma("sp", esink, sinkb[:, :], w=["esink"])
        S.dma("sp", em_sb, emask[:, :], w=["em_sb"])
        S.op("act", lambda e: e.activation(out=esink, in_=esink, func=AF.Exp), r=["esink"], w=["esink"])

        stg_n = [0]
        CAST_RR = ("act", "dve")

        def stage_cast(src_fn, dst_fn, nk, ncols, keyw):
            for k0 in range(0, nk, 4):
                k1 = min(nk, k0 + 4)
                i = stg_n[0] % len(wst)
                stg_n[0] += 1
                S.dma("sp", wst[i][:, 0:k1 - k0, 0:ncols], src_fn(k0, k1), w=[f"wst{i}"])
                src = wst[i][:, 0:k1 - k0, 0:ncols]
                dst = dst_fn(k0, k1)
                ce = CAST_RR[stg_n[0] % len(CAST_RR)]
                if ce == "act":
                    S.op("act", lambda e, s=src, d=dst: e.copy(out=d, in_=s), r=[f"wst{i}"], w=[keyw])
                else:
                    S.op(ce, lambda e, s=src, d=dst: e.tensor_copy(out=d, in_=s), r=[f"wst{i}"], w=[keyw])

        xT_v = xT.rearrange("(k p) t -> p k t", p=128)
        for tb in range(3):
            stage_cast(lambda k0, k1, tb=tb: xT_v[:, k0:k1, tb * 512:(tb + 1) * 512],
                       lambda k0, k1, tb=tb: xTb[:, k0:k1, tb * 512:(tb + 1) * 512], 16, 512, "xTb")

        w_in_v = w_in.rearrange("(k p) n -> p k n", p=128)
        wn = [0]

        def load_w(src_v, col0, ncols=512, nk=16, bufs=None, pfx="wbf", ctr=None):
            bufs = wbf if bufs is None else bufs
            ctr = wn if ctr is None else ctr
            i = ctr[0] % len(bufs)
            ctr[0] += 1
            stage_cast(lambda k0, k1: src_v[:, k0:k1, col0:col0 + ncols],
                       lambda k0, k1: bufs[i][:, k0:k1, 0:ncols], nk, ncols, f"{pfx}{i}")
            return bufs[i], f"{pfx}{i}"

        psn = [0]

        def next_ps(lo=0, n=2):
            i = lo + psn[0] % n
            psn[0] += 1
            return PS[i], f"ps{i}"

        def mm_group(pst, pskey, pairs, extra_r, ncols):
            def fn(e):
                ins = None
                n = len(pairs)
                for j, (l, r_) in enumerate(pairs):
                    ins = e.matmul(pst[:, 0:ncols], l, r_, start=(j == 0), stop=(j == n - 1))
                return ins
            S.op("pe", fn, r=list(extra_r), w=[pskey])

        EXT_BLKS = [(0, 512), (512, 512), (1024, 512)]
        OWN_BLKS = [(OWN0, 512), (OWN0 + 512, 512)]
        evn = [0]

        def evac(dst, src, rk, wk):
            evn[0] += 1
            if evn[0] % 2:
                S.op("act", lambda e: e.copy(out=dst, in_=src), r=rk, w=wk)
            else:
                S.op("dve", lambda e: e.tensor_copy(out=dst, in_=src), r=rk, w=wk)

        def proj_fm(W, wkey, wc0, blks, dst_fn, dkey):
            for bi, (t0, sz) in enumerate(blks):
                pst, pk = next_ps(0, 2)
                mm_group(pst, pk, [(W[:, k, wc0:wc0 + 128], xTb[:, k, t0:t0 + sz]) for k in range(16)],
                         [wkey, "xTb"], sz)
                evac(dst_fn(bi), pst[:, 0:sz], [pk], [dkey])

        astep = [0]
        apair = [0]

        def attention_group(heads):
            steps = []
            for hd in heads:
                for qb in range(2):
                    for kc in range(hd[4]):
                        steps.append((hd, qb, kc, apair[0]))
                    apair[0] += 1
            LA = 2
            base = astep[0]
            astep[0] += len(steps)
            for s in range(len(steps) + LA):
                if s < len(steps):
                    (qh, kh, vc0, yidx, nkc, tile0_fn, tab_src_fn, tab_is_bf, sink_col), qb, kc, pid = steps[s]
                    g = base + s
                    ti, si = g % NTAB, g % 3
                    tile = tile0_fn(qb) + kc
                    if tab_is_bf:
                        tb_ap, tbk = tabb[ti], f"tabb{ti}"
                    else:
                        tb_ap, tbk = tab[ti], f"tab{ti}"
                    S.dma("act", tb_ap, tab_src_fn(qb, kc), w=[tbk])
                    S.op("pe", lambda e, p=PS[2 + si], tile=tile, qb=qb, kh=kh, qh=qh: e.matmul(
                        p[:, :], kT[:, kh, tile * 128:(tile + 1) * 128], qT[:, qh, qb * 512:(qb + 1) * 512],
                        start=True, stop=True), r=[f"kT{kh}", f"qT{qh}"], w=[f"ps{2 + si}"])
                b = s - LA
                if b >= 0:
                    (qh, kh, vc0, yidx, nkc, tile0_fn, tab_src_fn, tab_is_bf, sink_col), qb, kc, pid = steps[b]
                    g = base + b
                    ti, si = g % NTAB, g % 3
                    tile = tile0_fn(qb) + kc
                    if tab_is_bf:
                        tb_ap, tbk = tabb[ti], f"tabb{ti}"
                    else:
                        tb_ap, tbk = tab[ti], f"tab{ti}"
                    ai = 5 + pid % 2
                    di = 7 if pid % 2 == 0 else 1
                    acc, acck, den, denk = PS[ai], f"ps{ai}", PS[di], f"ps{di}"
                    S.op("dve", lambda e, p=PS[2 + si], l=Lb[si], t=tb_ap: e.scalar_tensor_tensor(
                        out=l, in0=p[:, :], scalar=SCALE, in1=t, op0=ALU.mult, op1=ALU.add),
                        r=[f"ps{2 + si}", tbk], w=[f"L{si}"])
                    S.op("act", lambda e, l=Lb[si], pt=PT[si]: e.activation(out=pt, in_=l, func=AF.Exp),
                         r=[f"L{si}"], w=[f"PT{si}"])
                    S.op("pe", lambda e, a=acc, pt=PT[si], tile=tile, kc=kc, vc0=vc0, nkc=nkc: e.matmul(
                        a[:, :], vv[:, tile, vc0:vc0 + 128], pt, start=(kc == 0), stop=(kc == nkc - 1)),
                        r=[f"vv{tile}", f"PT{si}"], w=[acck])
                    S.op("pe", lambda e, dn=den, pt=PT[si], kc=kc, nkc=nkc: e.matmul(
                        dn[:, :], ones, pt, start=(kc == 0), stop=(kc == nkc - 1)),
                        r=["ones", f"PT{si}"], w=[denk])
                    if kc == nkc - 1:
                        if sink_col is not None:
                            S.op("dve", lambda e, dn=den, sc=sink_col: e.tensor_scalar(
                                out=rden, in0=dn[:, :], scalar1=esink[:, sc:sc + 1], scalar2=None, op0=ALU.add),
                                r=[denk, "esink"], w=["rden"])
                            S.op("dve", lambda e: e.reciprocal(out=rden, in_=rden), r=["rden"], w=["rden"], strict=True)
                        else:
                            S.op("dve", lambda e, dn=den: e.reciprocal(out=rden, in_=dn[:, :]), r=[denk], w=["rden"])
                        S.op("dve", lambda e, a=acc, qb=qb, yidx=yidx: e.tensor_tensor(
                            out=yT[:, yidx, qb * 512:(qb + 1) * 512], in0=a[:, :], in1=rden, op=ALU.mult),
                            r=[acck, "rden"], w=[f"yT{yidx}"], strict=True)

        for hg in range(2):
            Wq, kq = load_w(w_in_v, hg * 512)
            for hh in range(4):
                proj_fm(Wq, kq, hh * 128, OWN_BLKS, lambda bi, hh=hh: qT[:, hh, bi * 512:(bi + 1) * 512], f"qT{hh}")
            Wk, kk = load_w(w_in_v, 1024 + hg * 512)
            for hh in range(4):
                proj_fm(Wk, kk, hh * 128, EXT_BLKS, lambda bi, hh=hh: kT[:, hh, bi * 512:(bi + 1) * 512], f"kT{hh}")
            Wv, kv = load_w(w_in_v, 2048 + hg * 512)
            for tile in range(12):
                pst, pk = next_ps(0, 2)
                mm_group(pst, pk, [(xTb[:, k, tile * 128:(tile + 1) * 128], Wv[:, k, 0:512]) for k in range(16)],
                         [kv, "xTb"], 512)
                evac(vv[:, tile, :], pst[:, :], [pk], [f"vv{tile}"])
            attention_group([(hh, hh, hh * 128, hg * 4 + hh, 8, (lambda qb: qb * 4),
                              (lambda qb, kc, h=hg * 4 + hh: tabA[h, qb, kc, :, :]), False, None) for hh in range(4)])

        Wkv, kkv = load_w(w_in_v, 4096)
        for kvh in range(2):
            proj_fm(Wkv, kkv, kvh * 128, EXT_BLKS, lambda bi, kvh=kvh: kT[:, kvh, bi * 512:(bi + 1) * 512], f"kT{kvh}")
        for tile in range(12):
            pst, pk = next_ps(0, 2)
            mm_group(pst, pk, [(xTb[:, k, tile * 128:(tile + 1) * 128], Wkv[:, k, 256:512]) for k in range(16)],
                     [kkv, "xTb"], 256)
            evac(vv[:, tile, 0:256], pst[:, 0:256], [pk], [f"vv{tile}"])
        for hg in range(2):
            Wq, kq = load_w(w_in_v, 3072 + hg * 512)
            for hh in range(4):
                proj_fm(Wq, kq, hh * 128, OWN_BLKS, lambda bi, hh=hh: qT[:, hh, bi * 512:(bi + 1) * 512], f"qT{hh}")
            attention_group([(hh, hg, hg * 128, 8 + hg * 4 + hh, 6, (lambda qb: qb * 4 + 1),
                              (lambda qb, i, h=hg * 4 + hh: tabB[h, qb, i, :, :]), True, hg * 4 + hh) for hh in range(4)])

        S.barrier()
        AR.off = ASZ - 8192
        wst.append(AR.take(8192, F32, (4, 512)))
        wst.append(AR.take(8192, F32, (4, 512)))
        AR.off = Z0
        cgs = AR.take(4 * 1026 * 4, F32, (4, 1026))
        tt = AR.take(4096, F32)
        CBLK = [(OWN0 - 1, 512), (OWN0 + 511, 512), (OWN0 + 1023, 2)]
        for half in range(2):
            Wc, kc_ = load_w(w_in_v, 5632 + half * 512)
            for cc in range(4):
                for bi, (t0, sz) in enumerate(CBLK):
                    pst, pk = next_ps(0, 2)
                    mm_group(pst, pk, [(Wc[:, k, cc * 128:(cc + 1) * 128], xTb[:, k, t0:t0 + sz]) for k in range(16)],
                             [kc_, "xTb"], sz)
                    S.op("act", lambda e, p=pst, bi=bi, sz=sz, cc=cc: e.copy(
                        out=cgs[:, cc, bi * 512:bi * 512 + sz], in_=p[:, 0:sz]), r=[pk], w=[f"cgs{cc}"])
            Wh, kh_ = load_w(w_in_v, 6656 + half * 512)
            for cc in range(4):
                for bi, (t0, sz) in enumerate(CBLK):
                    pst, pk = next_ps(0, 2)
                    mm_group(pst, pk, [(Wh[:, k, cc * 128:(cc + 1) * 128], xTb[:, k, t0:t0 + sz]) for k in range(16)],
                             [kh_, "xTb"], sz)
                    S.op("dve", lambda e, p=pst, bi=bi, sz=sz, cc=cc: e.tensor_tensor(
                        out=cgs[:, cc, bi * 512:bi * 512 + sz], in0=p[:, 0:sz], in1=cgs[:, cc, bi * 512:bi * 512 + sz],
                        op=ALU.mult), r=[pk, f"cgs{cc}"], w=[f"cgs{cc}"])
                S.op("dve", lambda e, cc=cc: e.tensor_tensor(out=cgs[:, cc, 0:1], in0=cgs[:, cc, 0:1], in1=em_sb[:, 0:1],
                                                             op=ALU.mult), r=[f"cgs{cc}", "em_sb"], w=[f"cgs{cc}"])
                S.op("dve", lambda e, cc=cc: e.tensor_tensor(out=cgs[:, cc, 1025:1026], in0=cgs[:, cc, 1025:1026],
                                                             in1=em_sb[:, 1:2], op=ALU.mult),
                     r=[f"cgs{cc}", "em_sb"], w=[f"cgs{cc}"])
            Wb, kb_ = load_w(w_in_v, 4608 + half * 512)
            for cc in range(4):
                c = half * 4 + cc
                S.op("dve", lambda e, cc=cc, c=c: e.tensor_scalar(
                    out=tt, in0=cgs[:, cc, 0:1024], scalar1=conv_sb[:, c, 0:1], scalar2=None, op0=ALU.mult),
                    r=[f"cgs{cc}", "conv_sb"], w=["tt"])
                S.op("dve", lambda e, cc=cc, c=c: e.scalar_tensor_tensor(
                    out=tt, in0=cgs[:, cc, 1:1025], scalar=conv_sb[:, c, 1:2], in1=tt, op0=ALU.mult, op1=ALU.add),
                    r=[f"cgs{cc}", "conv_sb", "tt"], w=["tt"])
                S.op("dve", lambda e, cc=cc, c=c: e.scalar_tensor_tensor(
                    out=tt, in0=cgs[:, cc, 2:1026], scalar=conv_sb[:, c, 2:3], in1=tt, op0=ALU.mult, op1=ALU.add),
                    r=[f"cgs{cc}", "conv_sb", "tt"], w=["tt"])
                for bi in range(2):
                    t0 = OWN0 + bi * 512
                    pst, pk = next_ps(2, 2)
                    mm_group(pst, pk, [(Wb[:, k, cc * 128:(cc + 1) * 128], xTb[:, k, t0:t0 + 512]) for k in range(16)],
                             [kb_, "xTb"], 512)
                    S.op("dve", lambda e, p=pst, bi=bi, c=c: e.tensor_tensor(
                        out=yT[:, 16 + c, bi * 512:(bi + 1) * 512], in0=p[:, :], in1=tt[:, bi * 512:(bi + 1) * 512],
                        op=ALU.mult), r=[pk, "tt"], w=[f"yT{16 + c}"])

        if debug == "yT":
            for i in range(24):
                S.dma("sp", dbg[i, :, :], yT[:, i, :], r=[f"yT{i}"])

        S.barrier()
        AR.off = Z0
        macc = AR.take(4 * 1024 * 4, F32, (4, NT))
        gsb = [AR.take(2048, F32) for _ in range(2)]
        tmp = [AR.take(2048, F32) for _ in range(2)]
        mbf = [AR.take(1024, BF16) for _ in range(2)]
        wbr = [AR.take(8192, BF16, (8, 512)) for _ in range(2)]
        wbn = [0]
        w_br_v = w_br.rearrange("n (k p) d -> n p k d", p=128)
        gn = [0]
        for G in range(4):
            for n in range(3):
                Wg, kg = load_w(w_in_v, 7680 + n * 2048 + G * 512)
                Wr, kr = load_w(w_br_v[n], G * 512, 512, 8, bufs=wbr, pfx="wbr", ctr=wbn)
                for j in range(4):
                    chunk = G * 4 + j
                    for tb in range(2):
                        pg, pgk = next_ps(0, 2)
                        mm_group(pg, pgk, [(Wg[:, k, j * 128:(j + 1) * 128], xTb[:, k, OWN0 + tb * 512:OWN0 + (tb + 1) * 512])
                                           for k in range(16)], [kg, "xTb"], 512)
                        pb, pbk = next_ps(2, 2)
                        mm_group(pb, pbk, [(Wr[:, k, j * 128:(j + 1) * 128], yT[:, n * 8 + k, tb * 512:(tb + 1) * 512])
                                           for k in range(8)], [kr] + [f"yT{n * 8 + k}" for k in range(8)], 512)
                        gi = gn[0] % 2
                        gn[0] += 1
                        bcol = n * 16 + chunk
                        S.op("act", lambda e, p=pg, gi=gi, bcol=bcol: e.activation(
                            out=gsb[gi], in_=p[:, :], func=AF.Sigmoid, bias=bg_sb[:, bcol:bcol + 1]),
                            r=[pgk, "bg_sb"], w=[f"gsb{gi}"])
                        mslice = macc[:, j, tb * 512:(tb + 1) * 512]
                        mk = f"macc{j}_{tb}"
                        if n == 0:
                            S.op("dve", lambda e, p=pb, gi=gi, m=mslice: e.tensor_tensor(
                                out=m, in0=p[:, :], in1=gsb[gi], op=ALU.mult), r=[pbk, f"gsb{gi}"], w=[mk])
                        else:
                            S.op("dve", lambda e, p=pb, gi=gi: e.tensor_tensor(
                                out=tmp[gi], in0=p[:, :], in1=gsb[gi], op=ALU.mult), r=[pbk, f"gsb{gi}"], w=[f"tmp{gi}"])
                            if n == 1:
                                S.op("pool", lambda e, gi=gi, m=mslice: e.tensor_tensor(
                                    out=m, in0=m, in1=tmp[gi], op=ALU.add), r=[f"tmp{gi}", mk], w=[mk])
                            else:
                                S.op("pool", lambda e, gi=gi, m=mslice, tb=tb: e.tensor_tensor(
                                    out=mbf[tb], in0=m, in1=tmp[gi], op=ALU.add), r=[f"tmp{gi}", mk], w=[f"mbf{tb}"])
                                S.dma("pool", mrg[chunk, :, tb * 512:(tb + 1) * 512], mbf[tb], r=[f"mbf{tb}"], w=[f"mrg{chunk}"])
        if debug == "mrg":
            S.barrier()
            for i in range(16):
                S.dma("sp", yT[:, i, :], mrg[i, :, :], r=[f"mrg{i}"], w=[f"yTm{i}"])
                S.dma("sp", dbg[i, :, :], yT[:, i, :], r=[f"yTm{i}"])

        S.barrier()
        AR.off = Z0 - 24576
        x1buf = AR.take(8 * 2048 * 4, F32, (8, 2048))
        xr = [AR.take(2048, F32) for _ in range(4)]
        g_sb = AR.take(8192, F32)
        b_sb = AR.take(8192, F32)
        stats = AR.take(4 * 6 * 4, F32, (4, 6))
        mv = AR.take(16, F32)
        rstd = AR.take(16, F32)
        mT = yT
        for i in range(16):
            S.dma("sp", mT[:, i, :], mrg[i, :, :], r=[f"mrg{i}"], w=[f"mT{i}"])
        S.dma("sp", g_sb, lng[:, :], w=["g_sb"])
        S.dma("sp", b_sb, lnb[:, :], w=["b_sb"])
        w_out_v = w_out.rearrange("(k p) n -> p k n", p=128)
        xn = [0]
        for G in range(4):
            Wo, ko = load_w(w_out_v, G * 512)
            for t8 in range(8):
                pst, pk = next_ps(0, 4)
                mm_group(pst, pk, [(mT[:, k, t8 * 128:(t8 + 1) * 128], Wo[:, k, 0:512]) for k in range(16)],
                         [ko] + [f"mT{k}" for k in range(16)], 512)
                xi = xn[0] % 4
                xn[0] += 1
                S.dma("act", xr[xi], xrow[t8 * 128:(t8 + 1) * 128, G * 512:(G + 1) * 512], w=[f"xr{xi}"])
                S.op("dve", lambda e, p=pst, xi=xi, t8=t8, G=G: e.scalar_tensor_tensor(
                    out=x1buf[:, t8, G * 512:(G + 1) * 512], in0=xr[xi], scalar=ALPHA, in1=p[:, :],
                    op0=ALU.mult, op1=ALU.add), r=[pk, f"xr{xi}"], w=[f"x1b{t8}"])
        for t8 in range(8):
            k8 = f"x1b{t8}"
            for q in range(4):
                S.op("dve", lambda e, t8=t8, q=q: e.bn_stats(out=stats[:, q, :], in_=x1buf[:, t8, q * 512:(q + 1) * 512]),
                     r=[k8], w=[f"stats{q}"], strict=True)
            S.op("dve", lambda e: e.bn_aggr(out=mv[:, 0:2], in_=stats.rearrange("p a b -> p (a b)")), r=[f"stats{q}" for q in range(4)], w=["mv"], strict=True)
            S.op("dve", lambda e: e.tensor_scalar(out=rstd[:, 0:1], in0=mv[:, 1:2], scalar1=EPS, scalar2=None,
                                                  op0=ALU.add), r=["mv"], w=["rstd"], strict=True)
            S.op("act", lambda e: e.activation(out=rstd[:, 0:1], in_=rstd[:, 0:1], func=AF.Sqrt), r=["rstd"], w=["rstd"])
            S.op("dve", lambda e: e.reciprocal(out=rstd[:, 0:1], in_=rstd[:, 0:1]), r=["rstd"], w=["rstd"], strict=True)
            S.op("dve", lambda e, t8=t8: e.tensor_scalar(out=x1buf[:, t8, :], in0=x1buf[:, t8, :], scalar1=mv[:, 0:1],
                                                         scalar2=rstd[:, 0:1], op0=ALU.subtract, op1=ALU.mult),
                 r=[k8, "mv", "rstd"], w=[k8], strict=True)
            S.op("pool", lambda e, t8=t8: e.tensor_tensor(out=x1buf[:, t8, :], in0=x1buf[:, t8, :], in1=g_sb, op=ALU.mult),
                 r=[k8, "g_sb"], w=[k8])
            S.op("pool", lambda e, t8=t8: e.tensor_tensor(out=x1buf[:, t8, :], in0=x1buf[:, t8, :], in1=b_sb, op=ALU.add),
                 r=[k8, "b_sb"], w=[k8])
            S.dma("pool", x1o[t8 * 128:(t8 + 1) * 128, :], x1buf[:, t8, :], r=[k8])
        S.emit(st)
    return nc

import numpy as np
import ml_dtypes
NPBF = ml_dtypes.bfloat16
NEG = -30000.0


def make_tabA(rpb_l, c):
    out = np.empty((8, 2, 8, 128, 512), np.float32)
    kk = np.arange(128)[:, None]
    qq = np.arange(512)[None, :]
    for qb in range(2):
        gq = 1024 * c + qb * 512 + qq
        QR, QC = gq // 64, gq % 64
        rs = np.clip(QR - 4, 0, 120)
        cs = np.clip(QC - 8, 0, 48)
        for kc in range(8):
            tile = qb * 4 + kc
            gk = 1024 * c - 256 + tile * 128 + kk
            KR, KC = gk // 64, gk % 64
            valid = (gk >= 0) & (gk < 8192) & (KR >= rs) & (KR <= rs + 7) & (KC >= cs) & (KC <= cs + 15)
            ri = np.clip(KR - QR + 7, 0, 14)
            ci = np.clip(KC - QC + 15, 0, 30)
            ri, ci = np.broadcast_arrays(ri, ci)
            vals = rpb_l[:, ri, ci]
            out[:, qb, kc] = np.where(valid[None], vals, np.float32(NEG))
    return out


_tabB_cache = {}


def make_tabB(c):
    if c in _tabB_cache:
        return _tabB_cache[c]
    out = np.empty((8, 2, 6, 128, 512), np.float32)
    kk = np.arange(128)[:, None]
    qq = np.arange(512)[None, :]
    slopes = 2.0 ** (-8.0 * np.arange(1, 9, dtype=np.float32) / 8)
    for qb in range(2):
        gq = 1024 * c + qb * 512 + qq
        for i in range(6):
            tile = qb * 4 + 1 + i
            gk = 1024 * c - 256 + tile * 128 + kk
            dist = np.abs(gk - gq)
            valid = (gk >= 0) & (gk < 8192) & (dist <= 128)
            for h in range(8):
                out[h, qb, i] = np.where(valid, -slopes[h] * dist.astype(np.float32), np.float32(NEG))
    o = out.astype(NPBF)
    _tabB_cache[c] = o
    return o


def m_inputs(c, x_full, w_in_l, w_br_l, w_out_l, b_gate_l, ln_g_l0, ln_b_l0, conv_w_l, sink_l, rpb_l):
    g0 = 1024 * c - 256
    ext = np.zeros((1536, 2048), np.float32)
    lo, hi = max(g0, 0), min(g0 + 1536, 8192)
    ext[lo - g0:hi - g0] = x_full[lo:hi]
    return {
        "xT": np.ascontiguousarray(ext.T),
        "xrow": np.ascontiguousarray(x_full[1024 * c:1024 * (c + 1)]),
        "w_in": w_in_l, "w_br": w_br_l, "w_out": w_out_l,
        "bgT": np.ascontiguousarray(b_gate_l.reshape(48, 128).T),
        "lng": np.ascontiguousarray(np.broadcast_to(ln_g_l0[None, :], (128, 2048))),
        "lnb": np.ascontiguousarray(np.broadcast_to(ln_b_l0[None, :], (128, 2048))),
        "convT": np.ascontiguousarray(conv_w_l.reshape(3, 8, 128).transpose(2, 1, 0)),
        "sinkb": np.ascontiguousarray(np.broadcast_to(sink_l[None, :], (128, 8))),
        "emask": np.ascontiguousarray(np.broadcast_to(
            np.array([0.0 if c == 0 else 1.0, 0.0 if c == 7 else 1.0], np.float32)[None, :], (128, 2))),
        "tabA": make_tabA(rpb_l, c),
        "tabB": make_tabB(c),
    }


def build_R1():
    nc = bass.Bass("TRN2", target_bir_lowering=False)
    dt = nc.dram_tensor
    x1T = dt("x1T", [2048, 8192], F32, kind="ExternalInput").ap()
    wr = dt("wr", [2048, 16], F32, kind="ExternalInput").ap()
    selo = dt("sel", [128, 2, 64], F32, kind="ExternalOutput").ap()
    affo = dt("aff", [128, 2, 64], F32, kind="ExternalOutput").ap()
    with ExitStack() as st:
        S = Sched(nc)
        ASZ = 60000
        AR = Arena(st.enter_context(nc.sbuf_tensor("arena", [128, ASZ], BF16)), ASZ)
        PS = [st.enter_context(nc.psum_tensor(f"ps{i}", [128, 512], F32)) for i in range(4)]
        wr_sb = AR.take(16 * 16 * 4, F32, (16, 16))
        xt = [AR.take(16 * 128 * 4, F32, (16, 128)) for _ in range(3)]
        E = AR.take(64 * 16 * 4, F32, (64, 16))
        ssum = AR.take(256, F32)
        A = AR.take(512, F32, (2, 64))
        cmp_ = AR.take(512, F32, (2, 64))
        ones = AR.take(256, BF16)
        lo = AR.take(8, F32); hi = AR.take(8, F32); mid = AR.take(8, F32)
        cnt = AR.take(8, F32); cntb = AR.take(4, BF16); AR.off += 2
        ge = AR.take(8, F32); d1 = AR.take(8, F32); d2 = AR.take(8, F32)
        S.op("pool", lambda e: e.memset(ones, 1.0), w=["ones"])
        S.op("pool", lambda e: e.memset(lo, 0.0), w=["lo"])
        S.op("pool", lambda e: e.memset(hi, 1.0), w=["hi"])
        S.dma("sp", wr_sb, wr.rearrange("(k p) e -> p k e", p=128), w=["wr"])
        x1T_v = x1T.rearrange("(k p) t -> p k t", p=128)
        for j in range(64):
            i = j % 3
            S.dma("sp", xt[i], x1T_v[:, :, j * 128:(j + 1) * 128], w=[f"xt{i}"])
            pst = PS[j // 32]

            def fn(e, i=i, j=j, pst=pst):
                ins = None
                for k in range(16):
                    ins = e.matmul(pst[:, (j % 32) * 16:(j % 32) * 16 + 16], xt[i][:, k, :], wr_sb[:, k, :],
                                   start=(k == 0), stop=(k == 15))
                return ins
            S.op("pe", fn, r=[f"xt{i}", "wr"], w=[f"lg{j // 32}"])
        for b in range(2):
            S.op("act", lambda e, b=b: e.activation(out=E[:, b * 32:(b + 1) * 32, :].rearrange("p a b -> p (a b)"),
                                                    in_=PS[b][:, :], func=AF.Exp), r=[f"lg{b}"], w=["E"])
        S.op("dve", lambda e: e.reduce_sum(out=ssum, in_=E, axis=AX.X), r=["E"], w=["ssum"], strict=True)
        S.op("dve", lambda e: e.reciprocal(out=ssum, in_=ssum), r=["ssum"], w=["ssum"], strict=True)
        for ee in range(2):
            S.op("dve", lambda e, ee=ee: e.tensor_tensor(out=A[:, ee, :], in0=E[:, :, ee], in1=ssum, op=ALU.mult),
                 r=["E", "ssum"], w=["A"], strict=True)
        S.dma("sp", affo[:, :, :], A, r=["A"])
        for it in range(36):
            S.op("dve", lambda e: e.tensor_tensor(out=mid, in0=lo, in1=hi, op=ALU.add), r=["lo", "hi"], w=["mid"], strict=True)
            S.op("dve", lambda e: e.tensor_scalar(out=mid, in0=mid, scalar1=0.5, scalar2=None, op0=ALU.mult),
                 r=["mid"], w=["mid"], strict=True)
            for ee in range(2):
                S.op("dve", lambda e, ee=ee: e.tensor_scalar(out=cmp_[:, ee, :], in0=A[:, ee, :], scalar1=mid[:, ee:ee + 1],
                                                             scalar2=None, op0=ALU.is_gt), r=["A", "mid"], w=["cmp"], strict=True)
            S.op("dve", lambda e: e.reduce_sum(out=cnt, in_=cmp_, axis=AX.X), r=["cmp"], w=["cnt"], strict=True)
            S.op("dve", lambda e: e.tensor_copy(out=cntb, in_=cnt), r=["cnt"], w=["cntb"], strict=True)
            S.op("pe", lambda e: e.matmul(PS[2][:, 0:2], ones, cntb, start=True, stop=True), r=["ones", "cntb"], w=["tot"])
            S.op("dve", lambda e: e.tensor_scalar(out=ge, in0=PS[2][:, 0:2], scalar1=1023.5, scalar2=None, op0=ALU.is_gt),
                 r=["tot"], w=["ge"], strict=True)
            S.op("dve", lambda e: e.tensor_tensor(out=d1, in0=mid, in1=lo, op=ALU.subtract), r=["mid", "lo"], w=["d1"], strict=True)
            S.op("dve", lambda e: e.tensor_tensor(out=d1, in0=d1, in1=ge, op=ALU.mult), r=["d1", "ge"], w=["d1"], strict=True)
            S.op("dve", lambda e: e.tensor_tensor(out=d2, in0=hi, in1=mid, op=ALU.subtract), r=["mid", "hi"], w=["d2"], strict=True)
            S.op("dve", lambda e: e.tensor_tensor(out=d2, in0=d2, in1=ge, op=ALU.mult), r=["d2", "ge"], w=["d2"], strict=True)
            S.op("dve", lambda e: e.tensor_tensor(out=lo, in0=lo, in1=d1, op=ALU.add), r=["lo", "d1"], w=["lo"], strict=True)
            S.op("dve", lambda e: e.tensor_tensor(out=hi, in0=mid, in1=d2, op=ALU.add), r=["mid", "d2"], w=["hi"], strict=True)
        for ee in range(2):
            S.op("dve", lambda e, ee=ee: e.tensor_scalar(out=cmp_[:, ee, :], in0=A[:, ee, :], scalar1=lo[:, ee:ee + 1],
                                                         scalar2=None, op0=ALU.is_gt), r=["A", "lo"], w=["cmp"], strict=True)
        S.dma("sp", selo[:, :, :], cmp_, r=["cmp"])
        S.emit(st)
    return nc


def build_R2():
    nc = bass.Bass("TRN2", target_bir_lowering=False)
    dt = nc.dram_tensor
    xeT = dt("xeT", [2, 2048, 1024], F32, kind="ExternalInput").ap()
    gs = dt("gs", [2, 128, 8], F32, kind="ExternalInput").ap()
    wg = dt("wg", [2, 2048, 1536], F32, kind="ExternalInput").ap()
    wu = dt("wu", [2, 2048, 1536], F32, kind="ExternalInput").ap()
    wd = dt("wd", [2, 1536, 2048], F32, kind="ExternalInput").ap()
    yeo = dt("ye", [2, 1024, 2048], F32, kind="ExternalOutput").ap()
    with ExitStack() as st:
        S = Sched(nc)
        ASZ = 90000
        AR = Arena(st.enter_context(nc.sbuf_tensor("arena", [128, ASZ], BF16)), ASZ)
        PS = [st.enter_context(nc.psum_tensor(f"ps{i}", [128, 512], F32)) for i in range(8)]
        wst = [AR.take(8192, F32, (4, 512)) for _ in range(4)]
        wbf = [AR.take(16384, BF16, (16, 512)) for _ in range(4)]
        xeb = AR.take(32768, BF16, (16, 1024))
        hid = AR.take(24576, BF16, (12, 1024))
        sg = [AR.take(2048, F32) for _ in range(2)]
        yo = [AR.take(2048, F32) for _ in range(2)]
        gs_sb = AR.take(64, F32, (2, 8))
        S.dma("sp", gs_sb, gs.rearrange("e p s -> p e s"), w=["gs"])
        stg_n = [0]

        def stage_cast(src_fn, dst_fn, nk, ncols, keyw):
            for k0 in range(0, nk, 4):
                k1 = min(nk, k0 + 4)
                i = stg_n[0] % 4
                stg_n[0] += 1
                S.dma("sp", wst[i][:, 0:k1 - k0, 0:ncols], src_fn(k0, k1), w=[f"wst{i}"])
                src = wst[i][:, 0:k1 - k0, 0:ncols]
                dst = dst_fn(k0, k1)
                ce = ("act", "dve")[stg_n[0] % 2]
                if ce == "act":
                    S.op("act", lambda e, s=src, d=dst: e.copy(out=d, in_=s), r=[f"wst{i}"], w=[keyw])
                else:
                    S.op(ce, lambda e, s=src, d=dst: e.tensor_copy(out=d, in_=s), r=[f"wst{i}"], w=[keyw])
        wn = [0]

        def load_w(src_v, col0, nk):
            i = wn[0] % 4
            wn[0] += 1
            stage_cast(lambda k0, k1: src_v[:, k0:k1, col0:col0 + 512],
                       lambda k0, k1: wbf[i][:, k0:k1, 0:512], nk, 512, f"wbf{i}")
            return wbf[i], f"wbf{i}"
        psn = [0]

        def next_ps(lo, n):
            i = lo + psn[0] % n
            psn[0] += 1
            return PS[i], f"ps{i}"

        def mm_group(pst, pskey, pairs, extra_r):
            def fn(e):
                ins = None
                n = len(pairs)
                for j, (l, r_) in enumerate(pairs):
                    ins = e.matmul(pst[:, :], l, r_, start=(j == 0), stop=(j == n - 1))
                return ins
            S.op("pe", fn, r=list(extra_r), w=[pskey])
        for e2 in range(2):
            xv = xeT[e2].rearrange("(k p) t -> p k t", p=128)
            for sbk in range(2):
                stage_cast(lambda k0, k1, sbk=sbk: xv[:, k0:k1, sbk * 512:(sbk + 1) * 512],
                           lambda k0, k1, sbk=sbk: xeb[:, k0:k1, sbk * 512:(sbk + 1) * 512], 16, 512, "xeb")
            wgv = wg[e2].rearrange("(k p) n -> p k n", p=128)
            wuv = wu[e2].rearrange("(k p) n -> p k n", p=128)
            wdv = wd[e2].rearrange("(k p) n -> p k n", p=128)
            gi = 0
            for fg in range(3):
                Wg, kg = load_w(wgv, fg * 512, 16)
                Wu, ku = load_w(wuv, fg * 512, 16)
                for j in range(4):
                    f = fg * 4 + j
                    for sbk in range(2):
                        pg, pgk = next_ps(0, 2)
                        mm_group(pg, pgk, [(Wg[:, k, j * 128:(j + 1) * 128], xeb[:, k, sbk * 512:(sbk + 1) * 512])
                                           for k in range(16)], [kg, "xeb"])
                        pu, puk = next_ps(2, 2)
                        mm_group(pu, puk, [(Wu[:, k, j * 128:(j + 1) * 128], xeb[:, k, sbk * 512:(sbk + 1) * 512])
                                           for k in range(16)], [ku, "xeb"])
                        gi += 1
                        S.op("act", lambda e, p=pg, g=sg[gi % 2]: e.activation(out=g, in_=p[:, :], func=AF.Silu),
                             r=[pgk], w=[f"sg{gi % 2}"])
                        S.op("dve", lambda e, p=pu, g=sg[gi % 2], f=f, sbk=sbk: e.tensor_tensor(
                            out=hid[:, f, sbk * 512:(sbk + 1) * 512], in0=p[:, :], in1=g, op=ALU.mult),
                            r=[puk, f"sg{gi % 2}"], w=[f"hid{f}"])
            for G in range(4):
                Wd, kd = load_w(wdv, G * 512, 12)
                for s8 in range(8):
                    py, pyk = next_ps(4, 4)
                    mm_group(py, pyk, [(hid[:, f, s8 * 128:(s8 + 1) * 128], Wd[:, f, 0:512]) for f in range(12)],
                             [kd] + [f"hid{f}" for f in range(12)])
                    gi += 1
                    S.op("dve", lambda e, p=py, y=yo[gi % 2], s8=s8, e2=e2: e.tensor_scalar(
                        out=y, in0=p[:, :], scalar1=gs_sb[:, e2, s8:s8 + 1], scalar2=None, op0=ALU.mult),
                        r=[pyk, "gs"], w=[f"yo{gi % 2}"])
                    S.dma("act", yeo[e2, s8 * 128:(s8 + 1) * 128, G * 512:(G + 1) * 512], yo[gi % 2], r=[f"yo{gi % 2}"])
        S.emit(st)
    return nc


def build_C(K):
    nc = bass.Bass("TRN2", target_bir_lowering=False)
    dt = nc.dram_tensor
    x1 = dt("x1", [1024, 2048], F32, kind="ExternalInput").ap()
    Y = dt("Y", [K, 1024, 2048], F32, kind="ExternalInput").ap()
    lng = dt("lng", [128, 2048], F32, kind="ExternalInput").ap()
    lnb = dt("lnb", [128, 2048], F32, kind="ExternalInput").ap()
    x2 = dt("x2", [1024, 2048], F32, kind="ExternalOutput").ap()
    with ExitStack() as st:
        S = Sched(nc)
        ASZ = 60000
        AR = Arena(st.enter_context(nc.sbuf_tensor("arena", [128, ASZ], BF16)), ASZ)
        acc = [AR.take(8192, F32) for _ in range(2)]
        yb = [AR.take(8192, F32) for _ in range(3)]
        g_sb = AR.take(8192, F32)
        b_sb = AR.take(8192, F32)
        stats = AR.take(96, F32, (4, 6))
        mv = AR.take(16, F32)
        rstd = AR.take(16, F32)
        S.dma("sp", g_sb, lng[:, :], w=["g_sb"])
        S.dma("sp", b_sb, lnb[:, :], w=["b_sb"])
        yn = 0
        for t8 in range(8):
            a = acc[t8 % 2]
            ak = f"acc{t8 % 2}"
            S.dma("sp", a, x1[t8 * 128:(t8 + 1) * 128, :], w=[ak])
            S.op("act", lambda e, a=a: e.mul(out=a, in_=a, mul=ALPHA), r=[ak], w=[ak])
            for k in range(K):
                yi = yn % 3
                yn += 1
                S.dma("sp", yb[yi], Y[k, t8 * 128:(t8 + 1) * 128, :], w=[f"yb{yi}"])
                S.op("dve", lambda e, a=a, yi=yi: e.tensor_tensor(out=a, in0=a, in1=yb[yi], op=ALU.add),
                     r=[ak, f"yb{yi}"], w=[ak], strict=True)
            for q in range(4):
                S.op("dve", lambda e, a=a, q=q: e.bn_stats(out=stats[:, q, :], in_=a[:, q * 512:(q + 1) * 512]),
                     r=[ak], w=[f"stats{q}"], strict=True)
            S.op("dve", lambda e: e.bn_aggr(out=mv[:, 0:2], in_=stats.rearrange("p a b -> p (a b)")),
                 r=[f"stats{q}" for q in range(4)], w=["mv"], strict=True)
            S.op("dve", lambda e: e.tensor_scalar(out=rstd[:, 0:1], in0=mv[:, 1:2], scalar1=EPS, scalar2=None, op0=ALU.add),
                 r=["mv"], w=["rstd"], strict=True)
            S.op("act", lambda e: e.activation(out=rstd[:, 0:1], in_=rstd[:, 0:1], func=AF.Sqrt), r=["rstd"], w=["rstd"])
            S.op("dve", lambda e: e.reciprocal(out=rstd[:, 0:1], in_=rstd[:, 0:1]), r=["rstd"], w=["rstd"], strict=True)
            S.op("dve", lambda e, a=a: e.tensor_scalar(out=a, in0=a, scalar1=mv[:, 0:1], scalar2=rstd[:, 0:1],
                                                       op0=ALU.subtract, op1=ALU.mult), r=[ak, "mv", "rstd"], w=[ak], strict=True)
            S.op("pool", lambda e, a=a: e.tensor_tensor(out=a, in0=a, in1=g_sb, op=ALU.mult), r=[ak, "g_sb"], w=[ak])
            S.op("pool", lambda e, a=a: e.tensor_tensor(out=a, in0=a, in1=b_sb, op=ALU.add), r=[ak, "b_sb"], w=[ak], strict=True)
            S.dma("act", x2[t8 * 128:(t8 + 1) * 128, :], a, r=[ak])
        S.emit(st)
    return nc

import numpy as np


_prog = {}
def prog(name, fn, *a):
    k = (name,) + a
    if k not in _prog:
        _prog[k] = fn(*a)
    return _prog[k]

def moe_layer(x1, w_router_l, w_gate_l, w_up_l, w_down_l, ln_g_l1, ln_b_l1):
    cores = list(range(8))
    x1T = np.ascontiguousarray(x1.T)
    in_maps = []
    for c in cores:
        perm = [2 * c, 2 * c + 1] + [e for e in range(16) if e not in (2 * c, 2 * c + 1)]
        in_maps.append({"x1T": x1T, "wr": np.ascontiguousarray(w_router_l[:, perm])})
    r1 = run_bass_kernel_spmd(prog("R1", build_R1), in_maps, core_ids=cores).results
    idx_all = np.zeros((16, 1024), np.int64)
    in_maps = []
    for c in cores:
        sel = r1[c]["sel"]; aff = r1[c]["aff"]
        xe = np.zeros((2, 2048, 1024), np.float32)
        gs = np.zeros((2, 128, 8), np.float32)
        for e2 in range(2):
            m = sel[:, e2, :].T.reshape(-1) > 0.5
            a = aff[:, e2, :].T.reshape(-1)
            idx = np.nonzero(m)[0]
            moe_layer.counts.append(len(idx))
            idx = idx[:1024]
            if len(idx) < 1024:
                idx = np.concatenate([idx, np.full(1024 - len(idx), -1)])
            idx_all[2 * c + e2] = idx
            ok = idx >= 0
            xe[e2][:, ok] = x1[idx[ok]].T
            g = np.zeros(1024, np.float32); g[ok] = a[idx[ok]]
            gs[e2] = g.reshape(8, 128).T
        in_maps.append({"xeT": xe, "gs": gs, "wg": w_gate_l[2 * c:2 * c + 2], "wu": w_up_l[2 * c:2 * c + 2],
                        "wd": w_down_l[2 * c:2 * c + 2]})
    r2 = run_bass_kernel_spmd(prog("R2", build_R2), in_maps, core_ids=cores).results
    ye = np.concatenate([r2[c]["ye"] for c in cores], axis=0)
    cnt = np.zeros(8192, np.int64)
    for e in range(16):
        ok = idx_all[e] >= 0
        cnt[idx_all[e][ok]] += 1
    K = max(int(cnt.max()), 1)
    Y = np.zeros((K, 8192, 2048), np.float32)
    cur = np.zeros(8192, np.int64)
    for e in range(16):
        ok = idx_all[e] >= 0
        ii = idx_all[e][ok]
        Y[cur[ii], ii] = ye[e][ok]
        cur[ii] += 1
    lng = np.ascontiguousarray(np.broadcast_to(ln_g_l1[None, :], (128, 2048)))
    lnb = np.ascontiguousarray(np.broadcast_to(ln_b_l1[None, :], (128, 2048)))
    in_maps = [{"x1": np.ascontiguousarray(x1[1024 * c:1024 * (c + 1)]),
                "Y": np.ascontiguousarray(Y[:, 1024 * c:1024 * (c + 1)]), "lng": lng, "lnb": lnb} for c in cores]
    r3 = run_bass_kernel_spmd(prog("C", build_C, K), in_maps, core_ids=cores).results
    return np.concatenate([r3[c]["x2"] for c in cores], axis=0)
moe_layer.counts = []


def kernel(x, w_in, b_gate, rpb, sink, conv_w, w_branch, w_out, ln_g, ln_b, w_router, w_gate, w_up, w_down):
    f = lambda a: np.ascontiguousarray(np.asarray(a, dtype=np.float32))
    xc = f(x)[0]
    cores = list(range(8))
    for l in range(4):
        args = (f(w_in[l]), f(w_branch[l]), f(w_out[l]), f(b_gate[l]), f(ln_g[l][0]), f(ln_b[l][0]), f(conv_w[l]),
                f(sink[l]), f(rpb[l]))
        in_maps = [m_inputs(c, xc, *args) for c in cores]
        res = run_bass_kernel_spmd(prog("M", build_M), in_maps, core_ids=cores).results
        x1 = np.concatenate([res[c]["x1"] for c in cores], axis=0)
        del in_maps, res
        xc = moe_layer(x1, f(w_router[l]), f(w_gate[l]), f(w_up[l]), f(w_down[l]), f(ln_g[l][1]), f(ln_b[l][1]))
    return xc[None].astype(np.float32)
```

```python
import numpy as np
from contextlib import ExitStack
import ml_dtypes
import concourse.bass as bass
import concourse.mybir as mybir
from concourse.bass_utils import run_bass_kernel_spmd
from concourse.bass import IndirectOffsetOnAxis

F32 = mybir.dt.float32
BF16 = mybir.dt.bfloat16
I32 = mybir.dt.int32
ALU = mybir.AluOpType
AF = mybir.ActivationFunctionType
AX = mybir.AxisListType
NPBF = ml_dtypes.bfloat16


class _Op:
    __slots__ = ("eng", "fn", "dma", "deps", "signal", "sig", "n")

    def __init__(self, eng, fn, dma):
        self.eng, self.fn, self.dma = eng, fn, dma
        self.deps, self.signal, self.sig, self.n = [], False, None, 0


class Sched:
    ENGS = ("pe", "act", "dve", "pool", "sp")
    NSLOT = {"sp": 8, "act": 4, "pool": 4, "pe": 0, "dve": 0}

    def __init__(self, nc):
        self.nc = nc
        self.ops = {e: [] for e in self.ENGS}
        self.last_w = {}
        self.readers = {}
        self.ndma = {e: 0 for e in self.ENGS}
        self.pending_bar = {}
        self.bar_mark = {e: 0 for e in self.ENGS}

    def barrier(self):
        deps = []
        for e in self.ENGS:
            lst = self.ops[e]
            lastc = None
            for o in lst:
                if not o.dma:
                    lastc = o
            if lastc is not None:
                deps.append(lastc)
            for o in lst[self.bar_mark[e]:]:
                if o.dma:
                    deps.append(o)
            self.bar_mark[e] = len(lst)
        for e in self.ENGS:
            self.pending_bar[e] = list(deps)

    def op(self, eng, fn, r=(), w=(), dma=False, strict=False):
        o = _Op(eng, fn, dma)
        deps = []
        seen = set()
        for k in r:
            p = self.last_w.get(k)
            if p is not None and id(p) not in seen:
                seen.add(id(p)); deps.append(p)
        for k in w:
            p = self.last_w.get(k)
            if p is not None and id(p) not in seen:
                seen.add(id(p)); deps.append(p)
            for p in self.readers.get(k, ()):
                if id(p) not in seen:
                    seen.add(id(p)); deps.append(p)
        if eng in self.pending_bar:
            for p in self.pending_bar.pop(eng):
                if id(p) not in seen:
                    seen.add(id(p)); deps.append(p)
        o.deps = [p for p in deps if strict or not (p.eng == eng and not p.dma and not dma)]
        for p in o.deps:
            p.signal = True
        for k in r:
            self.readers.setdefault(k, []).append(o)
        for k in w:
            self.last_w[k] = o
            self.readers[k] = []
        if dma:
            o.n = self.ndma[eng]
            self.ndma[eng] += 1
        self.ops[eng].append(o)
        return o

    def dma(self, eng, out, in_, r=(), w=()):
        return self.op(eng, lambda e: e.dma_start(out=out, in_=in_), r, w, dma=True)

    def emit(self, stack):
        nc = self.nc
        csem = {e: stack.enter_context(nc.semaphore("c_" + e)) for e in ("pe", "act", "dve", "pool")}
        dsem = {}
        for e in self.ENGS:
            if self.ndma[e]:
                dsem[e] = [stack.enter_context(nc.semaphore(f"d_{e}{i}")) for i in range(self.NSLOT[e])]
        for e in self.ENGS:
            cnt = 0
            for o in self.ops[e]:
                if o.dma:
                    K = self.NSLOT[e]
                    o.sig = (dsem[e][o.n % K], 16 * (o.n // K + 1))
                elif o.signal:
                    cnt += 1
                    o.sig = (csem[e], cnt)
            assert cnt < 60000, (e, cnt)
        block = stack.enter_context(nc.Block())
        sched = self

        def run(e, eng):
            waited = {}
            for o in sched.ops[e]:
                waits = {}
                for p in o.deps:
                    s, v = p.sig
                    if waits.get(id(s), (None, 0))[1] < v:
                        waits[id(s)] = (s, v)
                if o.dma:
                    K = sched.NSLOT[e]
                    if o.n >= K:
                        s = dsem[e][o.n % K]
                        v = 16 * (o.n // K)
                        if waits.get(id(s), (None, 0))[1] < v:
                            waits[id(s)] = (s, v)
                for s, v in waits.values():
                    if waited.get(id(s), 0) < v:
                        eng.wait_ge(s, v)
                        waited[id(s)] = v
                ins = o.fn(eng)
                if o.dma:
                    ins.then_inc(o.sig[0], 16)
                elif o.signal:
                    ins.then_inc(o.sig[0], 1)
            if sched.ndma[e]:
                K = sched.NSLOT[e]
                n = sched.ndma[e]
                for i in range(min(K, n)):
                    cntslot = (n - 1 - i) // K + 1
                    v = 16 * cntslot
                    s = dsem[e][i]
                    if waited.get(id(s), 0) < v:
                        eng.wait_ge(s, v)

        @block.tensor
        def _(eng):
            run("pe", eng)

        @block.scalar
        def _(eng):
            run("act", eng)

        @block.vector
        def _(eng):
            run("dve", eng)

        @block.gpsimd
        def _(eng):
            run("pool", eng)

        @block.sync
        def _(eng):
            run("sp", eng)


SCALE = 128 ** -0.5
ALPHA = 8 ** 0.25
EPS = 1e-5
NT = 1024
NE = 1536
OWN0 = 256


class Arena:
    def __init__(self, t, size):
        self.t, self.size, self.off = t, size, 0

    def take(self, nbytes, dtype, shape=None):
        nel = nbytes // 2
        a = self.t[:, self.off:self.off + nel]
        self.off += nel
        assert self.off <= self.size, (self.off, self.size)
        if dtype == F32:
            a = a.bitcast(F32)
        elif dtype == I32:
            a = a.bitcast(I32)
        if shape is not None and len(shape) == 2:
            a = a.rearrange("p (a b) -> p a b", a=shape[0])
        return a


def build_M(debug=None):
    nc = bass.Bass("TRN2", target_bir_lowering=False)
    dt = nc.dram_tensor
    xT = dt("xT", [2048, NE], F32, kind="ExternalInput").ap()
    xrow = dt("xrow", [NT, 2048], F32, kind="ExternalInput").ap()
    w_in = dt("w_in", [2048, 13824], F32, kind="ExternalInput").ap()
    w_br = dt("w_br", [3, 1024, 2048], F32, kind="ExternalInput").ap()
    w_out = dt("w_out", [2048, 2048], F32, kind="ExternalInput").ap()
    bgT = dt("bgT", [128, 48], F32, kind="ExternalInput").ap()
    lng = dt("lng", [128, 2048], F32, kind="ExternalInput").ap()
    lnb = dt("lnb", [128, 2048], F32, kind="ExternalInput").ap()
    convT = dt("convT", [128, 8, 3], F32, kind="ExternalInput").ap()
    sinkb = dt("sinkb", [128, 8], F32, kind="ExternalInput").ap()
    emask = dt("emask", [128, 2], F32, kind="ExternalInput").ap()
    tabA = dt("tabA", [8, 2, 8, 128, 512], F32, kind="ExternalInput").ap()
    tabB = dt("tabB", [8, 2, 6, 128, 512], BF16, kind="ExternalInput").ap()
    x1o = dt("x1", [NT, 2048], F32, kind="ExternalOutput").ap()
    wr = dt("wr", [2048, 16], F32, kind="ExternalInput").ap()
    identd = dt("ident", [128, 128], F32, kind="ExternalInput").ap()
    affo = dt("aff", [128, 8, 16], F32, kind="ExternalOutput").ap()
    mrg = dt("mrg", [16, 128, NT], BF16, kind="Internal").ap()
    dbg = None
    if debug == "yT":
        dbg = dt("dbg", [24, 128, NT], BF16, kind="ExternalOutput").ap()
    if debug == "mrg":
        dbg = dt("dbg", [16, 128, NT], BF16, kind="ExternalOutput").ap()

    with ExitStack() as st:
        S = Sched(nc)
        ASZ = 106000
        arena_t = st.enter_context(nc.sbuf_tensor("arena", [128, ASZ], BF16))
        AR = Arena(arena_t, ASZ)
        PS = [st.enter_context(nc.psum_tensor(f"ps{i}", [128, 512], F32)) for i in range(8)]

        ones = AR.take(256, BF16)
        bg_sb = AR.take(192, F32)
        conv_sb = AR.take(96, F32, (8, 3))
        esink = AR.take(32, F32)
        em_sb = AR.take(8, F32)
        AR.off = (AR.off + 15) // 16 * 16
        wst = [AR.take(8192, F32, (4, 512)) for _ in range(2)]
        wbf = [AR.take(16384, BF16, (16, 512)) for _ in range(2)]
        yT = AR.take(49152, BF16, (24, NT))
        xTb = AR.take(49152, BF16, (16, NE))
        Z0 = AR.off
        qT = AR.take(8192, BF16, (4, NT))
        kT = AR.take(12288, BF16, (4, NE))
        vv = AR.take(12288, BF16, (12, 512))
        NTAB = 4
        tab = [AR.take(2048, F32) for _ in range(NTAB)]
        tabb = [AR.take(1024, BF16) for _ in range(NTAB)]
        Lb = [AR.take(2048, F32) for _ in range(3)]
        PT = [AR.take(1024, BF16) for _ in range(3)]
        rden = AR.take(2048, F32)
        endAB = AR.off

        S.op("pool", lambda e: e.memset(ones, 1.0), w=["ones"])
        S.dma("sp", bg_sb, bgT[:, :], w=["bg_sb"])
        S.dma("sp", conv_sb, convT[:, :, :], w=["conv_sb"])
        S.dma("sp", esink, sinkb[:, :], w=["esink"])
        S.dma("sp", em_sb, emask[:, :], w=["em_sb"])
        S.op("act", lambda e: e.activation(out=esink, in_=esink, func=AF.Exp), r=["esink"], w=["esink"])

        stg_n = [0]
        CAST_RR = ("act", "dve")

        def stage_cast(src_fn, dst_fn, nk, ncols, keyw):
            for k0 in range(0, nk, 4):
                k1 = min(nk, k0 + 4)
                i = stg_n[0] % len(wst)
                stg_n[0] += 1
                S.dma("sp", wst[i][:, 0:k1 - k0, 0:ncols], src_fn(k0, k1), w=[f"wst{i}"])
                src = wst[i][:, 0:k1 - k0, 0:ncols]
                dst = dst_fn(k0, k1)
                ce = CAST_RR[stg_n[0] % len(CAST_RR)]
                if ce == "act":
                    S.op("act", lambda e, s=src, d=dst: e.copy(out=d, in_=s), r=[f"wst{i}"], w=[keyw])
                else:
                    S.op(ce, lambda e, s=src, d=dst: e.tensor_copy(out=d, in_=s), r=[f"wst{i}"], w=[keyw])

        xT_v = xT.rearrange("(k p) t -> p k t", p=128)
        for tb in range(3):
            stage_cast(lambda k0, k1, tb=tb: xT_v[:, k0:k1, tb * 512:(tb + 1) * 512],
                       lambda k0, k1, tb=tb: xTb[:, k0:k1, tb * 512:(tb + 1) * 512], 16, 512, "xTb")

        w_in_v = w_in.rearrange("(k p) n -> p k n", p=128)
        wn = [0]

        def load_w(src_v, col0, ncols=512, nk=16, bufs=None, pfx="wbf", ctr=None):
            bufs = wbf if bufs is None else bufs
            ctr = wn if ctr is None else ctr
            i = ctr[0] % len(bufs)
            ctr[0] += 1
            stage_cast(lambda k0, k1: src_v[:, k0:k1, col0:col0 + ncols],
                       lambda k0, k1: bufs[i][:, k0:k1, 0:ncols], nk, ncols, f"{pfx}{i}")
            return bufs[i], f"{pfx}{i}"

        psn = [0]

        def next_ps(lo=0, n=2):
            i = lo + psn[0] % n
            psn[0] += 1
            return PS[i], f"ps{i}"

        def mm_group(pst, pskey, pairs, extra_r, ncols):
            def fn(e):
                ins = None
                n = len(pairs)
                for j, (l, r_) in enumerate(pairs):
                    ins = e.matmul(pst[:, 0:ncols], l, r_, start=(j == 0), stop=(j == n - 1))
                return ins
            S.op("pe", fn, r=list(extra_r), w=[pskey])

        EXT_BLKS = [(0, 512), (512, 512), (1024, 512)]
        OWN_BLKS = [(OWN0, 512), (OWN0 + 512, 512)]
        evn = [0]

        def evac(dst, src, rk, wk):
            evn[0] += 1
            if evn[0] % 2:
                S.op("act", lambda e: e.copy(out=dst, in_=src), r=rk, w=wk)
            else:
                S.op("dve", lambda e: e.tensor_copy(out=dst, in_=src), r=rk, w=wk)

        def proj_fm(W, wkey, wc0, blks, dst_fn, dkey):
            for bi, (t0, sz) in enumerate(blks):
                pst, pk = next_ps(0, 2)
                mm_group(pst, pk, [(W[:, k, wc0:wc0 + 128], xTb[:, k, t0:t0 + sz]) for k in range(16)],
                         [wkey, "xTb"], sz)
                evac(dst_fn(bi), pst[:, 0:sz], [pk], [dkey])

        astep = [0]
        apair = [0]

        def attention_group(heads):
            steps = []
            for hd in heads:
                for qb in range(2):
                    for kc in range(hd[4]):
                        steps.append((hd, qb, kc, apair[0]))
                    apair[0] += 1
            LA = 2
            base = astep[0]
            astep[0] += len(steps)
            for s in range(len(steps) + LA):
                if s < len(steps):
                    (qh, kh, vc0, yidx, nkc, tile0_fn, tab_src_fn, tab_is_bf, sink_col), qb, kc, pid = steps[s]
                    g = base + s
                    ti, si = g % NTAB, g % 3
                    tile = tile0_fn(qb) + kc
                    if tab_is_bf:
                        tb_ap, tbk = tabb[ti], f"tabb{ti}"
                    else:
                        tb_ap, tbk = tab[ti], f"tab{ti}"
                    S.dma("pool", tb_ap, tab_src_fn(qb, kc), w=[tbk])
                    S.op("pe", lambda e, p=PS[2 + si], tile=tile, qb=qb, kh=kh, qh=qh: e.matmul(
                        p[:, :], kT[:, kh, tile * 128:(tile + 1) * 128], qT[:, qh, qb * 512:(qb + 1) * 512],
                        start=True, stop=True), r=[f"kT{kh}", f"qT{qh}"], w=[f"ps{2 + si}"])
                b = s - LA
                if b >= 0:
                    (qh, kh, vc0, yidx, nkc, tile0_fn, tab_src_fn, tab_is_bf, sink_col), qb, kc, pid = steps[b]
                    g = base + b
                    ti, si = g % NTAB, g % 3
                    tile = tile0_fn(qb) + kc
                    if tab_is_bf:
                        tb_ap, tbk = tabb[ti], f"tabb{ti}"
                    else:
                        tb_ap, tbk = tab[ti], f"tab{ti}"
                    ai = 5 + pid % 2
                    di = 7 if pid % 2 == 0 else 1
                    acc, acck, den, denk = PS[ai], f"ps{ai}", PS[di], f"ps{di}"
                    S.op("dve", lambda e, p=PS[2 + si], l=Lb[si], t=tb_ap: e.scalar_tensor_tensor(
                        out=l, in0=p[:, :], scalar=SCALE, in1=t, op0=ALU.mult, op1=ALU.add),
                        r=[f"ps{2 + si}", tbk], w=[f"L{si}"])
                    S.op("act", lambda e, l=Lb[si], pt=PT[si]: e.activation(out=pt, in_=l, func=AF.Exp),
                         r=[f"L{si}"], w=[f"PT{si}"])
                    S.op("pe", lambda e, a=acc, pt=PT[si], tile=tile, kc=kc, vc0=vc0, nkc=nkc: e.matmul(
                        a[:, :], vv[:, tile, vc0:vc0 + 128], pt, start=(kc == 0), stop=(kc == nkc - 1)),
                        r=[f"vv{tile}", f"PT{si}"], w=[acck])
                    S.op("pe", lambda e, dn=den, pt=PT[si], kc=kc, nkc=nkc: e.matmul(
                        dn[:, :], ones, pt, start=(kc == 0), stop=(kc == nkc - 1)),
                        r=["ones", f"PT{si}"], w=[denk])
                    if kc == nkc - 1:
                        if sink_col is not None:
                            S.op("dve", lambda e, dn=den, sc=sink_col: e.tensor_scalar(
                                out=rden, in0=dn[:, :], scalar1=esink[:, sc:sc + 1], scalar2=None, op0=ALU.add),
                                r=[denk, "esink"], w=["rden"])
                            S.op("dve", lambda e: e.reciprocal(out=rden, in_=rden), r=["rden"], w=["rden"], strict=True)
                        else:
                            S.op("dve", lambda e, dn=den: e.reciprocal(out=rden, in_=dn[:, :]), r=[denk], w=["rden"])
                        S.op("dve", lambda e, a=acc, qb=qb, yidx=yidx: e.tensor_tensor(
                            out=yT[:, yidx, qb * 512:(qb + 1) * 512], in0=a[:, :], in1=rden, op=ALU.mult),
                            r=[acck, "rden"], w=[f"yT{yidx}"], strict=True)

        for hg in range(2):
            Wq, kq = load_w(w_in_v, hg * 512)
            for hh in range(4):
                proj_fm(Wq, kq, hh * 128, OWN_BLKS, lambda bi, hh=hh: qT[:, hh, bi * 512:(bi + 1) * 512], f"qT{hh}")
            Wk, kk = load_w(w_in_v, 1024 + hg * 512)
            for hh in range(4):
                proj_fm(Wk, kk, hh * 128, EXT_BLKS, lambda bi, hh=hh: kT[:, hh, bi * 512:(bi + 1) * 512], f"kT{hh}")
            Wv, kv = load_w(w_in_v, 2048 + hg * 512)
            for tile in range(12):
                pst, pk = next_ps(0, 2)
                mm_group(pst, pk, [(xTb[:, k, tile * 128:(tile + 1) * 128], Wv[:, k, 0:512]) for k in range(16)],
                         [kv, "xTb"], 512)
                evac(vv[:, tile, :], pst[:, :], [pk], [f"vv{tile}"])
            attention_group([(hh, hh, hh * 128, hg * 4 + hh, 8, (lambda qb: qb * 4),
                              (lambda qb, kc, h=hg * 4 + hh: tabA[h, qb, kc, :, :]), False, None) for hh in range(4)])

        Wkv, kkv = load_w(w_in_v, 4096)
        for kvh in range(2):
            proj_fm(Wkv, kkv, kvh * 128, EXT_BLKS, lambda bi, kvh=kvh: kT[:, kvh, bi * 512:(bi + 1) * 512], f"kT{kvh}")
        for tile in range(12):
            pst, pk = next_ps(0, 2)
            mm_group(pst, pk, [(xTb[:, k, tile * 128:(tile + 1) * 128], Wkv[:, k, 256:512]) for k in range(16)],
                     [kkv, "xTb"], 256)
            evac(vv[:, tile, 0:256], pst[:, 0:256], [pk], [f"vv{tile}"])
        for hg in range(2):
            Wq, kq = load_w(w_in_v, 3072 + hg * 512)
            for hh in range(4):
                proj_fm(Wq, kq, hh * 128, OWN_BLKS, lambda bi, hh=hh: qT[:, hh, bi * 512:(bi + 1) * 512], f"qT{hh}")
            attention_group([(hh, hg, hg * 128, 8 + hg * 4 + hh, 6, (lambda qb: qb * 4 + 1),
                              (lambda qb, i, h=hg * 4 + hh: tabB[h, qb, i, :, :]), True, hg * 4 + hh) for hh in range(4)])

        S.barrier()
        AR.off = ASZ - 8192
        wst.append(AR.take(8192, F32, (4, 512)))
        wst.append(AR.take(8192, F32, (4, 512)))
        AR.off = Z0
        cgs = AR.take(4 * 1026 * 4, F32, (4, 1026))
        tt = AR.take(4096, F32)
        CBLK = [(OWN0 - 1, 512), (OWN0 + 511, 512), (OWN0 + 1023, 2)]
        for half in range(2):
            Wc, kc_ = load_w(w_in_v, 5632 + half * 512)
            for cc in range(4):
                for bi, (t0, sz) in enumerate(CBLK):
                    pst, pk = next_ps(0, 2)
                    mm_group(pst, pk, [(Wc[:, k, cc * 128:(cc + 1) * 128], xTb[:, k, t0:t0 + sz]) for k in range(16)],
                             [kc_, "xTb"], sz)
                    S.op("act", lambda e, p=pst, bi=bi, sz=sz, cc=cc: e.copy(
                        out=cgs[:, cc, bi * 512:bi * 512 + sz], in_=p[:, 0:sz]), r=[pk], w=[f"cgs{cc}"])
            Wh, kh_ = load_w(w_in_v, 6656 + half * 512)
            for cc in range(4):
                for bi, (t0, sz) in enumerate(CBLK):
                    pst, pk = next_ps(0, 2)
                    mm_group(pst, pk, [(Wh[:, k, cc * 128:(cc + 1) * 128], xTb[:, k, t0:t0 + sz]) for k in range(16)],
                             [kh_, "xTb"], sz)
                    S.op("dve", lambda e, p=pst, bi=bi, sz=sz, cc=cc: e.tensor_tensor(
                        out=cgs[:, cc, bi * 512:bi * 512 + sz], in0=p[:, 0:sz], in1=cgs[:, cc, bi * 512:bi * 512 + sz],
                        op=ALU.mult), r=[pk, f"cgs{cc}"], w=[f"cgs{cc}"])
                S.op("dve", lambda e, cc=cc: e.tensor_tensor(out=cgs[:, cc, 0:1], in0=cgs[:, cc, 0:1], in1=em_sb[:, 0:1],
                                                             op=ALU.mult), r=[f"cgs{cc}", "em_sb"], w=[f"cgs{cc}"])
                S.op("dve", lambda e, cc=cc: e.tensor_tensor(out=cgs[:, cc, 1025:1026], in0=cgs[:, cc, 1025:1026],
                                                             in1=em_sb[:, 1:2], op=ALU.mult),
                     r=[f"cgs{cc}", "em_sb"], w=[f"cgs{cc}"])
            Wb, kb_ = load_w(w_in_v, 4608 + half * 512)
            for cc in range(4):
                c = half * 4 + cc
                S.op("dve", lambda e, cc=cc, c=c: e.tensor_scalar(
                    out=tt, in0=cgs[:, cc, 0:1024], scalar1=conv_sb[:, c, 0:1], scalar2=None, op0=ALU.mult),
                    r=[f"cgs{cc}", "conv_sb"], w=["tt"])
                S.op("dve", lambda e, cc=cc, c=c: e.scalar_tensor_tensor(
                    out=tt, in0=cgs[:, cc, 1:1025], scalar=conv_sb[:, c, 1:2], in1=tt, op0=ALU.mult, op1=ALU.add),
                    r=[f"cgs{cc}", "conv_sb", "tt"], w=["tt"])
                S.op("dve", lambda e, cc=cc, c=c: e.scalar_tensor_tensor(
                    out=tt, in0=cgs[:, cc, 2:1026], scalar=conv_sb[:, c, 2:3], in1=tt, op0=ALU.mult, op1=ALU.add),
                    r=[f"cgs{cc}", "conv_sb", "tt"], w=["tt"])
                for bi in range(2):
                    t0 = OWN0 + bi * 512
                    pst, pk = next_ps(2, 2)
                    mm_group(pst, pk, [(Wb[:, k, cc * 128:(cc + 1) * 128], xTb[:, k, t0:t0 + 512]) for k in range(16)],
                             [kb_, "xTb"], 512)
                    S.op("dve", lambda e, p=pst, bi=bi, c=c: e.tensor_tensor(
                        out=yT[:, 16 + c, bi * 512:(bi + 1) * 512], in0=p[:, :], in1=tt[:, bi * 512:(bi + 1) * 512],
                        op=ALU.mult), r=[pk, "tt"], w=[f"yT{16 + c}"])

        if debug == "yT":
            for i in range(24):
                S.dma("sp", dbg[i, :, :], yT[:, i, :], r=[f"yT{i}"])

        S.barrier()
        AR.off = Z0
        macc = AR.take(4 * 1024 * 4, F32, (4, NT))
        gsb = [AR.take(2048, F32) for _ in range(2)]
        tmp = [AR.take(2048, F32) for _ in range(2)]
        mbf = [AR.take(1024, BF16) for _ in range(2)]
        wbr = [AR.take(8192, BF16, (8, 512)) for _ in range(2)]
        wbn = [0]
        w_br_v = w_br.rearrange("n (k p) d -> n p k d", p=128)
        gn = [0]
        for G in range(4):
            for n in range(3):
                Wg, kg = load_w(w_in_v, 7680 + n * 2048 + G * 512)
                Wr, kr = load_w(w_br_v[n], G * 512, 512, 8, bufs=wbr, pfx="wbr", ctr=wbn)
                for j in range(4):
                    chunk = G * 4 + j
                    for tb in range(2):
                        pg, pgk = next_ps(0, 2)
                        mm_group(pg, pgk, [(Wg[:, k, j * 128:(j + 1) * 128], xTb[:, k, OWN0 + tb * 512:OWN0 + (tb + 1) * 512])
                                           for k in range(16)], [kg, "xTb"], 512)
                        pb, pbk = next_ps(2, 2)
                        mm_group(pb, pbk, [(Wr[:, k, j * 128:(j + 1) * 128], yT[:, n * 8 + k, tb * 512:(tb + 1) * 512])
                                           for k in range(8)], [kr] + [f"yT{n * 8 + k}" for k in range(8)], 512)
                        gi = gn[0] % 2
                        gn[0] += 1
                        bcol = n * 16 + chunk
                        S.op("act", lambda e, p=pg, gi=gi, bcol=bcol: e.activation(
                            out=gsb[gi], in_=p[:, :], func=AF.Sigmoid, bias=bg_sb[:, bcol:bcol + 1]),
                            r=[pgk, "bg_sb"], w=[f"gsb{gi}"])
                        mslice = macc[:, j, tb * 512:(tb + 1) * 512]
                        mk = f"macc{j}_{tb}"
                        if n == 0:
                            S.op("dve", lambda e, p=pb, gi=gi, m=mslice: e.tensor_tensor(
                                out=m, in0=p[:, :], in1=gsb[gi], op=ALU.mult), r=[pbk, f"gsb{gi}"], w=[mk])
                        else:
                            S.op("dve", lambda e, p=pb, gi=gi: e.tensor_tensor(
                                out=tmp[gi], in0=p[:, :], in1=gsb[gi], op=ALU.mult), r=[pbk, f"gsb{gi}"], w=[f"tmp{gi}"])
                            if n == 1:
                                S.op("pool", lambda e, gi=gi, m=mslice: e.tensor_tensor(
                                    out=m, in0=m, in1=tmp[gi], op=ALU.add), r=[f"tmp{gi}", mk], w=[mk])
                            else:
                                S.op("pool", lambda e, gi=gi, m=mslice, tb=tb: e.tensor_tensor(
                                    out=mbf[tb], in0=m, in1=tmp[gi], op=ALU.add), r=[f"tmp{gi}", mk], w=[f"mbf{tb}"])
                                S.dma("pool", mrg[chunk, :, tb * 512:(tb + 1) * 512], mbf[tb], r=[f"mbf{tb}"], w=[f"mrg{chunk}"])
        if debug == "mrg":
            S.barrier()
            for i in range(16):
                S.dma("sp", yT[:, i, :], mrg[i, :, :], r=[f"mrg{i}"], w=[f"yTm{i}"])
                S.dma("sp", dbg[i, :, :], yT[:, i, :], r=[f"yTm{i}"])

        S.barrier()
        AR.off = Z0 - 24576
        wo_extra = [AR.take(16384, BF16, (16, 512)) for _ in range(2)]
        x1buf = AR.take(2 * 2048 * 4, F32, (2, 2048))
        x1Tt = AR.take(16 * 128 * 4, F32, (16, 128))
        wr_sb = AR.take(16 * 16 * 4, F32, (16, 16))
        ident = AR.take(512, F32)
        Eb = AR.take(512, F32, (8, 16))
        affb = AR.take(512, F32, (8, 16))
        ssum = AR.take(32, F32)
        xr = [AR.take(8192, F32) for _ in range(2)]
        g_sb = AR.take(8192, F32)
        b_sb = AR.take(8192, F32)
        stats = AR.take(4 * 6 * 4, F32, (4, 6))
        mv = AR.take(16, F32)
        rstd = AR.take(16, F32)
        mT = yT
        for i in range(16):
            S.dma("sp", mT[:, i, :], mrg[i, :, :], r=[f"mrg{i}"], w=[f"mT{i}"])
        S.dma("act", g_sb, lng[:, :], w=["g_sb"])
        S.dma("act", b_sb, lnb[:, :], w=["b_sb"])
        S.dma("act", wr_sb, wr.rearrange("(k p) e -> p k e", p=128), w=["wr"])
        S.dma("act", ident, identd[:, :], w=["ident"])
        w_out_v = w_out.rearrange("(k p) n -> p k n", p=128)
        wo_bufs = [(wbf[0], "wbf0"), (wbf[1], "wbf1"), (wo_extra[0], "woe0"), (wo_extra[1], "woe1")]
        for G in range(4):
            buf, key = wo_bufs[G]
            stage_cast(lambda k0, k1, G=G: w_out_v[:, k0:k1, G * 512:(G + 1) * 512],
                       lambda k0, k1, buf=buf: buf[:, k0:k1, 0:512], 16, 512, key)
        pending_router = []

        def router_part(t8, bi, k8):
                for g4 in range(4):
                    def tfn(e, g4=g4, bi=bi):
                        ins = None
                        for kk in range(4):
                            k = g4 * 4 + kk
                            ins = e.matmul(PS[4 + g4][:, kk * 128:(kk + 1) * 128], x1buf[:, bi, k * 128:(k + 1) * 128], ident,
                                           start=True, stop=True)
                        return ins
                    S.op("pe", tfn, r=[k8, "ident"], w=[f"ps{4 + g4}"])
                    if g4 % 2:
                        S.op("act", lambda e, g4=g4: e.copy(out=x1Tt[:, g4 * 4:(g4 + 1) * 4, :].rearrange("p a b -> p (a b)"),
                                                            in_=PS[4 + g4][:, :]), r=[f"ps{4 + g4}"], w=[f"x1Tt{g4}"])
                    else:
                        S.op("dve", lambda e, g4=g4: e.tensor_copy(out=x1Tt[:, g4 * 4:(g4 + 1) * 4, :].rearrange("p a b -> p (a b)"),
                                                                   in_=PS[4 + g4][:, :]), r=[f"ps{4 + g4}"], w=[f"x1Tt{g4}"])

                def lfn(e, t8=t8):
                    ins = None
                    for k in range(16):
                        ins = e.matmul(PS[3][:, t8 * 16:(t8 + 1) * 16], x1Tt[:, k, :], wr_sb[:, k, :], start=(k == 0), stop=(k == 15))
                    return ins
                S.op("pe", lfn, r=[f"x1Tt{g4}" for g4 in range(4)] + ["wr"], w=["lgps"])

        for t8 in range(8):
            xi = t8 % 2
            bi = t8 % 2
            k8 = f"x1b{bi}"
            S.dma("act", xr[xi], xrow[t8 * 128:(t8 + 1) * 128, :], w=[f"xr{xi}"])
            for G in range(4):
                buf, key = wo_bufs[G]
                pst, pk = next_ps(0, 3)
                mm_group(pst, pk, [(mT[:, k, t8 * 128:(t8 + 1) * 128], buf[:, k, 0:512]) for k in range(16)],
                         [key] + [f"mT{k}" for k in range(16)], 512)
                S.op("dve", lambda e, p=pst, xi=xi, bi=bi, G=G: e.scalar_tensor_tensor(
                    out=x1buf[:, bi, G * 512:(G + 1) * 512], in0=xr[xi][:, G * 512:(G + 1) * 512], scalar=ALPHA, in1=p[:, :],
                    op0=ALU.mult, op1=ALU.add), r=[pk, f"xr{xi}"], w=[k8])
            for q in range(4):
                S.op("dve", lambda e, bi=bi, q=q: e.bn_stats(out=stats[:, q, :], in_=x1buf[:, bi, q * 512:(q + 1) * 512]),
                     r=[k8], w=[f"stats{q}"], strict=True)
            S.op("dve", lambda e: e.bn_aggr(out=mv[:, 0:2], in_=stats.rearrange("p a b -> p (a b)")),
                 r=[f"stats{q}" for q in range(4)], w=["mv"], strict=True)
            S.op("dve", lambda e: e.tensor_scalar(out=rstd[:, 0:1], in0=mv[:, 1:2], scalar1=EPS, scalar2=None,
                                                  op0=ALU.add), r=["mv"], w=["rstd"], strict=True)
            S.op("act", lambda e: e.activation(out=rstd[:, 0:1], in_=rstd[:, 0:1], func=AF.Sqrt), r=["rstd"], w=["rstd"])
            S.op("dve", lambda e: e.reciprocal(out=rstd[:, 0:1], in_=rstd[:, 0:1]), r=["rstd"], w=["rstd"], strict=True)
            S.op("dve", lambda e, bi=bi: e.tensor_scalar(out=x1buf[:, bi, :], in0=x1buf[:, bi, :], scalar1=mv[:, 0:1],
                                                         scalar2=rstd[:, 0:1], op0=ALU.subtract, op1=ALU.mult),
                 r=[k8, "mv", "rstd"], w=[k8], strict=True)
            S.op("dve", lambda e, bi=bi: e.tensor_tensor(out=x1buf[:, bi, :], in0=x1buf[:, bi, :], in1=g_sb, op=ALU.mult),
                 r=[k8, "g_sb"], w=[k8], strict=True)
            S.op("pool", lambda e, bi=bi: e.tensor_tensor(out=x1buf[:, bi, :], in0=x1buf[:, bi, :], in1=b_sb, op=ALU.add),
                 r=[k8, "b_sb"], w=[k8])
            S.dma("pool", x1o[t8 * 128:(t8 + 1) * 128, :], x1buf[:, bi, :], r=[k8])
            pending_router.append((t8, bi, k8))
            if len(pending_router) > 1:
                router_part(*pending_router.pop(0))
        while pending_router:
            router_part(*pending_router.pop(0))
        S.op("act", lambda e: e.activation(out=Eb.rearrange("p a b -> p (a b)"), in_=PS[3][:, 0:128], func=AF.Exp),
             r=["lgps"], w=["Eb"])
        S.op("dve", lambda e: e.reduce_sum(out=ssum, in_=Eb, axis=AX.X), r=["Eb"], w=["ssum"], strict=True)
        S.op("dve", lambda e: e.reciprocal(out=ssum, in_=ssum), r=["ssum"], w=["ssum"], strict=True)
        for jj in range(8):
            S.op("dve", lambda e, jj=jj: e.tensor_scalar(out=affb[:, jj, :], in0=Eb[:, jj, :], scalar1=ssum[:, jj:jj + 1],
                                                         scalar2=None, op0=ALU.mult), r=["Eb", "ssum"], w=["affb"], strict=True)
        S.dma("pool", affo[:, :, :], affb, r=["affb"])
        S.emit(st)
    return nc

import numpy as np
import ml_dtypes
NPBF = ml_dtypes.bfloat16
NEG = -30000.0


def make_tabA(rpb_l, c):
    out = np.empty((8, 2, 8, 128, 512), np.float32)
    kk = np.arange(128)[:, None]
    qq = np.arange(512)[None, :]
    for qb in range(2):
        gq = 1024 * c + qb * 512 + qq
        QR, QC = gq // 64, gq % 64
        rs = np.clip(QR - 4, 0, 120)
        cs = np.clip(QC - 8, 0, 48)
        for kc in range(8):
            tile = qb * 4 + kc
            gk = 1024 * c - 256 + tile * 128 + kk
            KR, KC = gk // 64, gk % 64
            valid = (gk >= 0) & (gk < 8192) & (KR >= rs) & (KR <= rs + 7) & (KC >= cs) & (KC <= cs + 15)
            ri = np.clip(KR - QR + 7, 0, 14)
            ci = np.clip(KC - QC + 15, 0, 30)
            ri, ci = np.broadcast_arrays(ri, ci)
            vals = rpb_l[:, ri, ci]
            out[:, qb, kc] = np.where(valid[None], vals, np.float32(NEG))
    return out


_tabB_cache = {}


def make_tabB(c):
    if c in _tabB_cache:
        return _tabB_cache[c]
    out = np.empty((8, 2, 6, 128, 512), np.float32)
    kk = np.arange(128)[:, None]
    qq = np.arange(512)[None, :]
    slopes = 2.0 ** (-8.0 * np.arange(1, 9, dtype=np.float32) / 8)
    for qb in range(2):
        gq = 1024 * c + qb * 512 + qq
        for i in range(6):
            tile = qb * 4 + 1 + i
            gk = 1024 * c - 256 + tile * 128 + kk
            dist = np.abs(gk - gq)
            valid = (gk >= 0) & (gk < 8192) & (dist <= 128)
            for h in range(8):
                out[h, qb, i] = np.where(valid, -slopes[h] * dist.astype(np.float32), np.float32(NEG))
    o = out.astype(NPBF)
    _tabB_cache[c] = o
    return o


def m_inputs(c, x_full, w_in_l, w_br_l, w_out_l, b_gate_l, ln_g_l0, ln_b_l0, conv_w_l, sink_l, rpb_l, w_router_l):
    g0 = 1024 * c - 256
    ext = np.zeros((1536, 2048), np.float32)
    lo, hi = max(g0, 0), min(g0 + 1536, 8192)
    ext[lo - g0:hi - g0] = x_full[lo:hi]
    return {
        "xT": np.ascontiguousarray(ext.T),
        "xrow": np.ascontiguousarray(x_full[1024 * c:1024 * (c + 1)]),
        "w_in": w_in_l, "w_br": w_br_l, "w_out": w_out_l,
        "bgT": np.ascontiguousarray(b_gate_l.reshape(48, 128).T),
        "lng": np.ascontiguousarray(np.broadcast_to(ln_g_l0[None, :], (128, 2048))),
        "lnb": np.ascontiguousarray(np.broadcast_to(ln_b_l0[None, :], (128, 2048))),
        "convT": np.ascontiguousarray(conv_w_l.reshape(3, 8, 128).transpose(2, 1, 0)),
        "sinkb": np.ascontiguousarray(np.broadcast_to(sink_l[None, :], (128, 8))),
        "emask": np.ascontiguousarray(np.broadcast_to(
            np.array([0.0 if c == 0 else 1.0, 0.0 if c == 7 else 1.0], np.float32)[None, :], (128, 2))),
        "wr": w_router_l, "ident": np.eye(128, dtype=np.float32),
        "tabA": make_tabA(rpb_l, c),
        "tabB": make_tabB(c),
    }


def build_R1():
    nc = bass.Bass("TRN2", target_bir_lowering=False)
    dt = nc.dram_tensor
    Ain = dt("A", [128, 2, 64], F32, kind="ExternalInput").ap()
    selo = dt("sel", [128, 2, 64], F32, kind="ExternalOutput").ap()
    with ExitStack() as st:
        S = Sched(nc)
        ASZ = 4000
        AR = Arena(st.enter_context(nc.sbuf_tensor("arena", [128, ASZ], BF16)), ASZ)
        PS = [st.enter_context(nc.psum_tensor(f"ps{i}", [128, 512], F32)) for i in range(4)]
        A = AR.take(512, F32, (2, 64))
        cmp_ = AR.take(512, F32, (2, 64))
        ones = AR.take(256, BF16)
        lo = AR.take(8, F32); hi = AR.take(8, F32); mid = AR.take(8, F32)
        cnt = AR.take(8, F32); cntb = AR.take(4, BF16); AR.off += 2
        ge = AR.take(8, F32); d1 = AR.take(8, F32); d2 = AR.take(8, F32)
        S.op("pool", lambda e: e.memset(ones, 1.0), w=["ones"])
        S.op("pool", lambda e: e.memset(lo, 0.0), w=["lo"])
        S.op("pool", lambda e: e.memset(hi, 1.0), w=["hi"])
        S.dma("sp", A, Ain[:, :, :], w=["A"])
        for it in range(31):
            S.op("dve", lambda e: e.tensor_tensor(out=mid, in0=lo, in1=hi, op=ALU.add), r=["lo", "hi"], w=["mid"], strict=True)
            S.op("dve", lambda e: e.tensor_scalar(out=mid, in0=mid, scalar1=0.5, scalar2=None, op0=ALU.mult),
                 r=["mid"], w=["mid"], strict=True)
            for ee in range(2):
                S.op("dve", lambda e, ee=ee: e.tensor_scalar(out=cmp_[:, ee, :], in0=A[:, ee, :], scalar1=mid[:, ee:ee + 1],
                                                             scalar2=None, op0=ALU.is_gt), r=["A", "mid"], w=["cmp"], strict=True)
            S.op("dve", lambda e: e.reduce_sum(out=cnt, in_=cmp_, axis=AX.X), r=["cmp"], w=["cnt"], strict=True)
            S.op("dve", lambda e: e.tensor_copy(out=cntb, in_=cnt), r=["cnt"], w=["cntb"], strict=True)
            S.op("pe", lambda e: e.matmul(PS[2][:, 0:2], ones, cntb, start=True, stop=True), r=["ones", "cntb"], w=["tot"])
            S.op("dve", lambda e: e.tensor_scalar(out=ge, in0=PS[2][:, 0:2], scalar1=1023.5, scalar2=None, op0=ALU.is_gt),
                 r=["tot"], w=["ge"], strict=True)
            S.op("dve", lambda e: e.tensor_tensor(out=d1, in0=mid, in1=lo, op=ALU.subtract), r=["mid", "lo"], w=["d1"], strict=True)
            S.op("dve", lambda e: e.tensor_tensor(out=d1, in0=d1, in1=ge, op=ALU.mult), r=["d1", "ge"], w=["d1"], strict=True)
            S.op("dve", lambda e: e.tensor_tensor(out=d2, in0=hi, in1=mid, op=ALU.subtract), r=["mid", "hi"], w=["d2"], strict=True)
            S.op("dve", lambda e: e.tensor_tensor(out=d2, in0=d2, in1=ge, op=ALU.mult), r=["d2", "ge"], w=["d2"], strict=True)
            S.op("dve", lambda e: e.tensor_tensor(out=lo, in0=lo, in1=d1, op=ALU.add), r=["lo", "d1"], w=["lo"], strict=True)
            S.op("dve", lambda e: e.tensor_tensor(out=hi, in0=mid, in1=d2, op=ALU.add), r=["mid", "d2"], w=["hi"], strict=True)
        for ee in range(2):
            S.op("dve", lambda e, ee=ee: e.tensor_scalar(out=cmp_[:, ee, :], in0=A[:, ee, :], scalar1=lo[:, ee:ee + 1],
                                                         scalar2=None, op0=ALU.is_gt), r=["A", "lo"], w=["cmp"], strict=True)
        S.dma("sp", selo[:, :, :], cmp_, r=["cmp"])
        S.emit(st)
    return nc


def build_R2():
    nc = bass.Bass("TRN2", target_bir_lowering=False)
    dt = nc.dram_tensor
    xeT = dt("xeT", [2, 2048, 1024], F32, kind="ExternalInput").ap()
    gs = dt("gs", [2, 128, 8], F32, kind="ExternalInput").ap()
    wg = dt("wg", [2, 2048, 1536], F32, kind="ExternalInput").ap()
    wu = dt("wu", [2, 2048, 1536], F32, kind="ExternalInput").ap()
    wd = dt("wd", [2, 1536, 2048], F32, kind="ExternalInput").ap()
    yeo = dt("ye", [2, 1024, 2048], F32, kind="ExternalOutput").ap()
    with ExitStack() as st:
        S = Sched(nc)
        ASZ = 90000
        AR = Arena(st.enter_context(nc.sbuf_tensor("arena", [128, ASZ], BF16)), ASZ)
        PS = [st.enter_context(nc.psum_tensor(f"ps{i}", [128, 512], F32)) for i in range(8)]
        wst = [AR.take(8192, F32, (4, 512)) for _ in range(4)]
        wbf = [AR.take(16384, BF16, (16, 512)) for _ in range(4)]
        xeb = AR.take(32768, BF16, (16, 1024))
        hid = AR.take(24576, BF16, (12, 1024))
        sg = [AR.take(2048, F32) for _ in range(2)]
        yo = [AR.take(2048, F32) for _ in range(2)]
        gs_sb = AR.take(64, F32, (2, 8))
        S.dma("sp", gs_sb, gs.rearrange("e p s -> p e s"), w=["gs"])
        stg_n = [0]

        def stage_cast(src_fn, dst_fn, nk, ncols, keyw):
            for k0 in range(0, nk, 4):
                k1 = min(nk, k0 + 4)
                i = stg_n[0] % 4
                stg_n[0] += 1
                S.dma("sp", wst[i][:, 0:k1 - k0, 0:ncols], src_fn(k0, k1), w=[f"wst{i}"])
                src = wst[i][:, 0:k1 - k0, 0:ncols]
                dst = dst_fn(k0, k1)
                ce = ("act", "dve")[stg_n[0] % 2]
                if ce == "act":
                    S.op("act", lambda e, s=src, d=dst: e.copy(out=d, in_=s), r=[f"wst{i}"], w=[keyw])
                else:
                    S.op(ce, lambda e, s=src, d=dst: e.tensor_copy(out=d, in_=s), r=[f"wst{i}"], w=[keyw])
        wn = [0]

        def load_w(src_v, col0, nk):
            i = wn[0] % 4
            wn[0] += 1
            stage_cast(lambda k0, k1: src_v[:, k0:k1, col0:col0 + 512],
                       lambda k0, k1: wbf[i][:, k0:k1, 0:512], nk, 512, f"wbf{i}")
            return wbf[i], f"wbf{i}"
        psn = [0]

        def next_ps(lo, n):
            i = lo + psn[0] % n
            psn[0] += 1
            return PS[i], f"ps{i}"

        def mm_group(pst, pskey, pairs, extra_r):
            def fn(e):
                ins = None
                n = len(pairs)
                for j, (l, r_) in enumerate(pairs):
                    ins = e.matmul(pst[:, :], l, r_, start=(j == 0), stop=(j == n - 1))
                return ins
            S.op("pe", fn, r=list(extra_r), w=[pskey])
        for e2 in range(2):
            xv = xeT[e2].rearrange("(k p) t -> p k t", p=128)
            for sbk in range(2):
                stage_cast(lambda k0, k1, sbk=sbk: xv[:, k0:k1, sbk * 512:(sbk + 1) * 512],
                           lambda k0, k1, sbk=sbk: xeb[:, k0:k1, sbk * 512:(sbk + 1) * 512], 16, 512, "xeb")
            wgv = wg[e2].rearrange("(k p) n -> p k n", p=128)
            wuv = wu[e2].rearrange("(k p) n -> p k n", p=128)
            wdv = wd[e2].rearrange("(k p) n -> p k n", p=128)
            gi = 0
            for fg in range(3):
                Wg, kg = load_w(wgv, fg * 512, 16)
                Wu, ku = load_w(wuv, fg * 512, 16)
                for j in range(4):
                    f = fg * 4 + j
                    for sbk in range(2):
                        pg, pgk = next_ps(0, 2)
                        mm_group(pg, pgk, [(Wg[:, k, j * 128:(j + 1) * 128], xeb[:, k, sbk * 512:(sbk + 1) * 512])
                                           for k in range(16)], [kg, "xeb"])
                        pu, puk = next_ps(2, 2)
                        mm_group(pu, puk, [(Wu[:, k, j * 128:(j + 1) * 128], xeb[:, k, sbk * 512:(sbk + 1) * 512])
                                           for k in range(16)], [ku, "xeb"])
                        gi += 1
                        S.op("act", lambda e, p=pg, g=sg[gi % 2]: e.activation(out=g, in_=p[:, :], func=AF.Silu),
                             r=[pgk], w=[f"sg{gi % 2}"])
                        S.op("dve", lambda e, p=pu, g=sg[gi % 2], f=f, sbk=sbk: e.tensor_tensor(
                            out=hid[:, f, sbk * 512:(sbk + 1) * 512], in0=p[:, :], in1=g, op=ALU.mult),
                            r=[puk, f"sg{gi % 2}"], w=[f"hid{f}"])
            for G in range(4):
                Wd, kd = load_w(wdv, G * 512, 12)
                for s8 in range(8):
                    py, pyk = next_ps(4, 4)
                    mm_group(py, pyk, [(hid[:, f, s8 * 128:(s8 + 1) * 128], Wd[:, f, 0:512]) for f in range(12)],
                             [kd] + [f"hid{f}" for f in range(12)])
                    gi += 1
                    S.op("dve", lambda e, p=py, y=yo[gi % 2], s8=s8, e2=e2: e.tensor_scalar(
                        out=y, in0=p[:, :], scalar1=gs_sb[:, e2, s8:s8 + 1], scalar2=None, op0=ALU.mult),
                        r=[pyk, "gs"], w=[f"yo{gi % 2}"])
                    S.dma("act", yeo[e2, s8 * 128:(s8 + 1) * 128, G * 512:(G + 1) * 512], yo[gi % 2], r=[f"yo{gi % 2}"])
        S.emit(st)
    return nc


def build_C(K):
    nc = bass.Bass("TRN2", target_bir_lowering=False)
    dt = nc.dram_tensor
    x1 = dt("x1", [1024, 2048], F32, kind="ExternalInput").ap()
    Y = dt("Y", [K, 1024, 2048], F32, kind="ExternalInput").ap()
    lng = dt("lng", [128, 2048], F32, kind="ExternalInput").ap()
    lnb = dt("lnb", [128, 2048], F32, kind="ExternalInput").ap()
    x2 = dt("x2", [1024, 2048], F32, kind="ExternalOutput").ap()
    with ExitStack() as st:
        S = Sched(nc)
        ASZ = 60000
        AR = Arena(st.enter_context(nc.sbuf_tensor("arena", [128, ASZ], BF16)), ASZ)
        acc = [AR.take(8192, F32) for _ in range(2)]
        yb = [AR.take(8192, F32) for _ in range(3)]
        g_sb = AR.take(8192, F32)
        b_sb = AR.take(8192, F32)
        stats = AR.take(96, F32, (4, 6))
        mv = AR.take(16, F32)
        rstd = AR.take(16, F32)
        S.dma("sp", g_sb, lng[:, :], w=["g_sb"])
        S.dma("sp", b_sb, lnb[:, :], w=["b_sb"])
        yn = 0
        for t8 in range(8):
            a = acc[t8 % 2]
            ak = f"acc{t8 % 2}"
            S.dma("sp", a, x1[t8 * 128:(t8 + 1) * 128, :], w=[ak])
            S.op("act", lambda e, a=a: e.mul(out=a, in_=a, mul=ALPHA), r=[ak], w=[ak])
            for k in range(K):
                yi = yn % 3
                yn += 1
                S.dma("sp", yb[yi], Y[k, t8 * 128:(t8 + 1) * 128, :], w=[f"yb{yi}"])
                S.op("dve", lambda e, a=a, yi=yi: e.tensor_tensor(out=a, in0=a, in1=yb[yi], op=ALU.add),
                     r=[ak, f"yb{yi}"], w=[ak], strict=True)
            for q in range(4):
                S.op("dve", lambda e, a=a, q=q: e.bn_stats(out=stats[:, q, :], in_=a[:, q * 512:(q + 1) * 512]),
                     r=[ak], w=[f"stats{q}"], strict=True)
            S.op("dve", lambda e: e.bn_aggr(out=mv[:, 0:2], in_=stats.rearrange("p a b -> p (a b)")),
                 r=[f"stats{q}" for q in range(4)], w=["mv"], strict=True)
            S.op("dve", lambda e: e.tensor_scalar(out=rstd[:, 0:1], in0=mv[:, 1:2], scalar1=EPS, scalar2=None, op0=ALU.add),
                 r=["mv"], w=["rstd"], strict=True)
            S.op("act", lambda e: e.activation(out=rstd[:, 0:1], in_=rstd[:, 0:1], func=AF.Sqrt), r=["rstd"], w=["rstd"])
            S.op("dve", lambda e: e.reciprocal(out=rstd[:, 0:1], in_=rstd[:, 0:1]), r=["rstd"], w=["rstd"], strict=True)
            S.op("dve", lambda e, a=a: e.tensor_scalar(out=a, in0=a, scalar1=mv[:, 0:1], scalar2=rstd[:, 0:1],
                                                       op0=ALU.subtract, op1=ALU.mult), r=[ak, "mv", "rstd"], w=[ak], strict=True)
            S.op("pool", lambda e, a=a: e.tensor_tensor(out=a, in0=a, in1=g_sb, op=ALU.mult), r=[ak, "g_sb"], w=[ak])
            S.op("pool", lambda e, a=a: e.tensor_tensor(out=a, in0=a, in1=b_sb, op=ALU.add), r=[ak, "b_sb"], w=[ak], strict=True)
            S.dma("act", x2[t8 * 128:(t8 + 1) * 128, :], a, r=[ak])
        S.emit(st)
    return nc

import numpy as np


_prog = {}
def prog(name, fn, *a):
    k = (name,) + a
    if k not in _prog:
        _prog[k] = fn(*a)
    return _prog[k]

def moe_layer(x1, aff_all, w_gate_l, w_up_l, w_down_l, ln_g_l1, ln_b_l1):
    cores = list(range(8))
    a3 = aff_all.reshape(64, 128, 16)
    in_maps = [{"A": np.ascontiguousarray(a3[:, :, 2 * c:2 * c + 2].transpose(1, 2, 0))} for c in cores]
    r1 = run_bass_kernel_spmd(prog("R1", build_R1), in_maps, core_ids=cores).results
    for c in cores:
        r1[c]["aff"] = in_maps[c]["A"]
    idx_all = np.zeros((16, 1024), np.int64)
    in_maps = []
    for c in cores:
        sel = r1[c]["sel"]; aff = r1[c]["aff"]
        xe = np.zeros((2, 2048, 1024), np.float32)
        gs = np.zeros((2, 128, 8), np.float32)
        for e2 in range(2):
            m = sel[:, e2, :].T.reshape(-1) > 0.5
            a = aff[:, e2, :].T.reshape(-1)
            idx = np.nonzero(m)[0]
            moe_layer.counts.append(len(idx))
            idx = idx[:1024]
            if len(idx) < 1024:
                idx = np.concatenate([idx, np.full(1024 - len(idx), -1)])
            idx_all[2 * c + e2] = idx
            ok = idx >= 0
            xe[e2][:, ok] = x1[idx[ok]].T
            g = np.zeros(1024, np.float32); g[ok] = a[idx[ok]]
            gs[e2] = g.reshape(8, 128).T
        in_maps.append({"xeT": xe, "gs": gs, "wg": w_gate_l[2 * c:2 * c + 2], "wu": w_up_l[2 * c:2 * c + 2],
                        "wd": w_down_l[2 * c:2 * c + 2]})
    r2 = run_bass_kernel_spmd(prog("R2", build_R2), in_maps, core_ids=cores).results
    ye = np.concatenate([r2[c]["ye"] for c in cores], axis=0)
    cnt = np.zeros(8192, np.int64)
    for e in range(16):
        ok = idx_all[e] >= 0
        cnt[idx_all[e][ok]] += 1
    K = max(int(cnt.max()), 1)
    Y = np.zeros((K, 8192, 2048), np.float32)
    cur = np.zeros(8192, np.int64)
    for e in range(16):
        ok = idx_all[e] >= 0
        ii = idx_all[e][ok]
        Y[cur[ii], ii] = ye[e][ok]
        cur[ii] += 1
    lng = np.ascontiguousarray(np.broadcast_to(ln_g_l1[None, :], (128, 2048)))
    lnb = np.ascontiguousarray(np.broadcast_to(ln_b_l1[None, :], (128, 2048)))
    in_maps = [{"x1": np.ascontiguousarray(x1[1024 * c:1024 * (c + 1)]),
                "Y": np.ascontiguousarray(Y[:, 1024 * c:1024 * (c + 1)]), "lng": lng, "lnb": lnb} for c in cores]
    r3 = run_bass_kernel_spmd(prog("C", build_C, K), in_maps, core_ids=cores).results
    return np.concatenate([r3[c]["x2"] for c in cores], axis=0)
moe_layer.counts = []


def kernel(x, w_in, b_gate, rpb, sink, conv_w, w_branch, w_out, ln_g, ln_b, w_router, w_gate, w_up, w_down):
    f = lambda a: np.ascontiguousarray(np.asarray(a, dtype=np.float32))
    xc = f(x)[0]
    cores = list(range(8))
    for l in range(4):
        args = (f(w_in[l]), f(w_branch[l]), f(w_out[l]), f(b_gate[l]), f(ln_g[l][0]), f(ln_b[l][0]), f(conv_w[l]),
                f(sink[l]), f(rpb[l]), f(w_router[l]))
        in_maps = [m_inputs(c, xc, *args) for c in cores]
        res = run_bass_kernel_spmd(prog("M", build_M), in_maps, core_ids=cores).results
        x1 = np.concatenate([res[c]["x1"] for c in cores], axis=0)
        aff_all = np.concatenate([res[c]["aff"].transpose(1, 0, 2).reshape(1024, 16) for c in cores], axis=0)
        del in_maps, res
        xc = moe_layer(x1, aff_all, f(w_gate[l]), f(w_up[l]), f(w_down[l]), f(ln_g[l][1]), f(ln_b[l][1]))
    return xc[None].astype(np.float32)
```

```python
import numpy as np
from contextlib import ExitStack
import ml_dtypes
import concourse.bass as bass
import concourse.mybir as mybir
from concourse.bass_utils import run_bass_kernel_spmd
from concourse.bass import IndirectOffsetOnAxis

F32 = mybir.dt.float32
BF16 = mybir.dt.bfloat16
I32 = mybir.dt.int32
ALU = mybir.AluOpType
AF = mybir.ActivationFunctionType
AX = mybir.AxisListType
NPBF = ml_dtypes.bfloat16


class _Op:
    __slots__ = ("eng", "fn", "dma", "deps", "signal", "sig", "n")

    def __init__(self, eng, fn, dma):
        self.eng, self.fn, self.dma = eng, fn, dma
        self.deps, self.signal, self.sig, self.n = [], False, None, 0


class Sched:
    ENGS = ("pe", "act", "dve", "pool", "sp")
    NSLOT = {"sp": 8, "act": 4, "pool": 4, "pe": 0, "dve": 0}

    def __init__(self, nc):
        self.nc = nc
        self.ops = {e: [] for e in self.ENGS}
        self.last_w = {}
        self.readers = {}
        self.ndma = {e: 0 for e in self.ENGS}
        self.pending_bar = {}
        self.bar_mark = {e: 0 for e in self.ENGS}

    def barrier(self):
        deps = []
        for e in self.ENGS:
            lst = self.ops[e]
            lastc = None
            for o in lst:
                if not o.dma:
                    lastc = o
            if lastc is not None:
                deps.append(lastc)
            for o in lst[self.bar_mark[e]:]:
                if o.dma:
                    deps.append(o)
            self.bar_mark[e] = len(lst)
        for e in self.ENGS:
            self.pending_bar[e] = list(deps)

    def op(self, eng, fn, r=(), w=(), dma=False, strict=False):
        o = _Op(eng, fn, dma)
        deps = []
        seen = set()
        for k in r:
            p = self.last_w.get(k)
            if p is not None and id(p) not in seen:
                seen.add(id(p)); deps.append(p)
        for k in w:
            p = self.last_w.get(k)
            if p is not None and id(p) not in seen:
                seen.add(id(p)); deps.append(p)
            for p in self.readers.get(k, ()):
                if id(p) not in seen:
                    seen.add(id(p)); deps.append(p)
        if eng in self.pending_bar:
            for p in self.pending_bar.pop(eng):
                if id(p) not in seen:
                    seen.add(id(p)); deps.append(p)
        o.deps = [p for p in deps if strict or not (p.eng == eng and not p.dma and not dma)]
        for p in o.deps:
            p.signal = True
        for k in r:
            self.readers.setdefault(k, []).append(o)
        for k in w:
            self.last_w[k] = o
            self.readers[k] = []
        if dma:
            o.n = self.ndma[eng]
            self.ndma[eng] += 1
        self.ops[eng].append(o)
        return o

    def dma(self, eng, out, in_, r=(), w=()):
        return self.op(eng, lambda e: e.dma_start(out=out, in_=in_), r, w, dma=True)

    def emit(self, stack):
        nc = self.nc
        csem = {e: stack.enter_context(nc.semaphore("c_" + e)) for e in ("pe", "act", "dve", "pool")}
        dsem = {}
        for e in self.ENGS:
            if self.ndma[e]:
                dsem[e] = [stack.enter_context(nc.semaphore(f"d_{e}{i}")) for i in range(self.NSLOT[e])]
        for e in self.ENGS:
            cnt = 0
            for o in self.ops[e]:
                if o.dma:
                    K = self.NSLOT[e]
                    o.sig = (dsem[e][o.n % K], 16 * (o.n // K + 1))
                elif o.signal:
                    cnt += 1
                    o.sig = (csem[e], cnt)
            assert cnt < 60000, (e, cnt)
        block = stack.enter_context(nc.Block())
        sched = self

        def run(e, eng):
            waited = {}
            for o in sched.ops[e]:
                waits = {}
                for p in o.deps:
                    s, v = p.sig
                    if waits.get(id(s), (None, 0))[1] < v:
                        waits[id(s)] = (s, v)
                if o.dma:
                    K = sched.NSLOT[e]
                    if o.n >= K:
                        s = dsem[e][o.n % K]
                        v = 16 * (o.n // K)
                        if waits.get(id(s), (None, 0))[1] < v:
                            waits[id(s)] = (s, v)
                for s, v in waits.values():
                    if waited.get(id(s), 0) < v:
                        eng.wait_ge(s, v)
                        waited[id(s)] = v
                ins = o.fn(eng)
                if o.dma:
                    ins.then_inc(o.sig[0], 16)
                elif o.signal:
                    ins.then_inc(o.sig[0], 1)
            if sched.ndma[e]:
                K = sched.NSLOT[e]
                n = sched.ndma[e]
                for i in range(min(K, n)):
                    cntslot = (n - 1 - i) // K + 1
                    v = 16 * cntslot
                    s = dsem[e][i]
                    if waited.get(id(s), 0) < v:
                        eng.wait_ge(s, v)

        @block.tensor
        def _(eng):
            run("pe", eng)

        @block.scalar
        def _(eng):
            run("act", eng)

        @block.vector
        def _(eng):
            run("dve", eng)

        @block.gpsimd
        def _(eng):
            run("pool", eng)

        @block.sync
        def _(eng):
            run("sp", eng)


SCALE = 128 ** -0.5
ALPHA = 8 ** 0.25
EPS = 1e-5
NT = 1024
NE = 1536
OWN0 = 256


class Arena:
    def __init__(self, t, size):
        self.t, self.size, self.off = t, size, 0

    def take(self, nbytes, dtype, shape=None):
        nel = nbytes // 2
        a = self.t[:, self.off:self.off + nel]
        self.off += nel
        assert self.off <= self.size, (self.off, self.size)
        if dtype == F32:
            a = a.bitcast(F32)
        elif dtype == I32:
            a = a.bitcast(I32)
        if shape is not None and len(shape) == 2:
            a = a.rearrange("p (a b) -> p a b", a=shape[0])
        return a


def build_M(debug=None):
    nc = bass.Bass("TRN2", target_bir_lowering=False)
    dt = nc.dram_tensor
    xT = dt("xT", [2048, NE], F32, kind="ExternalInput").ap()
    xrow = dt("xrow", [NT, 2048], F32, kind="ExternalInput").ap()
    w_in = dt("w_in", [2048, 13824], F32, kind="ExternalInput").ap()
    w_br = dt("w_br", [3, 1024, 2048], F32, kind="ExternalInput").ap()
    w_out = dt("w_out", [2048, 2048], F32, kind="ExternalInput").ap()
    bgT = dt("bgT", [128, 48], F32, kind="ExternalInput").ap()
    lng = dt("lng", [128, 2048], F32, kind="ExternalInput").ap()
    lnb = dt("lnb", [128, 2048], F32, kind="ExternalInput").ap()
    convT = dt("convT", [128, 8, 3], F32, kind="ExternalInput").ap()
    sinkb = dt("sinkb", [128, 8], F32, kind="ExternalInput").ap()
    emask = dt("emask", [128, 2], F32, kind="ExternalInput").ap()
    tabA = dt("tabA", [8, 2, 8, 128, 512], F32, kind="ExternalInput").ap()
    tabB = dt("tabB", [8, 2, 6, 128, 512], BF16, kind="ExternalInput").ap()
    x1o = dt("x1", [NT, 2048], F32, kind="ExternalOutput").ap()
    wr = dt("wr", [2048, 16], F32, kind="ExternalInput").ap()
    identd = dt("ident", [128, 128], F32, kind="ExternalInput").ap()
    affo = dt("aff", [128, 8, 16], F32, kind="ExternalOutput").ap()
    mrg = dt("mrg", [16, 128, NT], BF16, kind="Internal").ap()
    dbg = None
    if debug == "yT":
        dbg = dt("dbg", [24, 128, NT], BF16, kind="ExternalOutput").ap()
    if debug == "mrg":
        dbg = dt("dbg", [16, 128, NT], BF16, kind="ExternalOutput").ap()

    with ExitStack() as st:
        S = Sched(nc)
        ASZ = 106000
        arena_t = st.enter_context(nc.sbuf_tensor("arena", [128, ASZ], BF16))
        AR = Arena(arena_t, ASZ)
        PS = [st.enter_context(nc.psum_tensor(f"ps{i}", [128, 512], F32)) for i in range(8)]

        ones = AR.take(256, BF16)
        bg_sb = AR.take(192, F32)
        conv_sb = AR.take(96, F32, (8, 3))
        esink = AR.take(32, F32)
        em_sb = AR.take(8, F32)
        AR.off = (AR.off + 15) // 16 * 16
        wst = [AR.take(8192, F32, (4, 512)) for _ in range(2)]
        wbf = [AR.take(16384, BF16, (16, 512)) for _ in range(2)]
        yT = AR.take(49152, BF16, (24, NT))
        xTb = AR.take(49152, BF16, (16, NE))
        Z0 = AR.off
        qT = AR.take(8192, BF16, (4, NT))
        kT = AR.take(12288, BF16, (4, NE))
        vv = AR.take(12288, BF16, (12, 512))
        NTAB = 4
        tab = [AR.take(2048, F32) for _ in range(NTAB)]
        tabb = [AR.take(1024, BF16) for _ in range(NTAB)]
        Lb = [AR.take(2048, F32) for _ in range(3)]
        PT = [AR.take(1024, BF16) for _ in range(3)]
        rden = AR.take(2048, F32)
        endAB = AR.off

        S.op("pool", lambda e: e.memset(ones, 1.0), w=["ones"])
        S.dma("sp", bg_sb, bgT[:, :], w=["bg_sb"])
        S.dma("sp", conv_sb, convT[:, :, :], w=["conv_sb"])
        S.dma("sp", esink, sinkb[:, :], w=["esink"])
        S.dma("sp", em_sb, emask[:, :], w=["em_sb"])
        S.op("act", lambda e: e.activation(out=esink, in_=esink, func=AF.Exp), r=["esink"], w=["esink"])

        stg_n = [0]
        CAST_RR = ("act", "dve")

        def stage_cast(src_fn, dst_fn, nk, ncols, keyw):
            for k0 in range(0, nk, 4):
                k1 = min(nk, k0 + 4)
                i = stg_n[0] % len(wst)
                stg_n[0] += 1
                S.dma("sp", wst[i][:, 0:k1 - k0, 0:ncols], src_fn(k0, k1), w=[f"wst{i}"])
                src = wst[i][:, 0:k1 - k0, 0:ncols]
                dst = dst_fn(k0, k1)
                ce = CAST_RR[stg_n[0] % len(CAST_RR)]
                if ce == "act":
                    S.op("act", lambda e, s=src, d=dst: e.copy(out=d, in_=s), r=[f"wst{i}"], w=[keyw])
                else:
                    S.op(ce, lambda e, s=src, d=dst: e.tensor_copy(out=d, in_=s), r=[f"wst{i}"], w=[keyw])

        xT_v = xT.rearrange("(k p) t -> p k t", p=128)
        for tb in range(3):
            stage_cast(lambda k0, k1, tb=tb: xT_v[:, k0:k1, tb * 512:(tb + 1) * 512],
                       lambda k0, k1, tb=tb: xTb[:, k0:k1, tb * 512:(tb + 1) * 512], 16, 512, "xTb")

        w_in_v = w_in.rearrange("(k p) n -> p k n", p=128)
        wn = [0]

        def load_w(src_v, col0, ncols=512, nk=16, bufs=None, pfx="wbf", ctr=None):
            bufs = wbf if bufs is None else bufs
            ctr = wn if ctr is None else ctr
            i = ctr[0] % len(bufs)
            ctr[0] += 1
            stage_cast(lambda k0, k1: src_v[:, k0:k1, col0:col0 + ncols],
                       lambda k0, k1: bufs[i][:, k0:k1, 0:ncols], nk, ncols, f"{pfx}{i}")
            return bufs[i], f"{pfx}{i}"

        psn = [0]

        def next_ps(lo=0, n=2):
            i = lo + psn[0] % n
            psn[0] += 1
            return PS[i], f"ps{i}"

        def mm_group(pst, pskey, pairs, extra_r, ncols):
            def fn(e):
                ins = None
                n = len(pairs)
                for j, (l, r_) in enumerate(pairs):
                    ins = e.matmul(pst[:, 0:ncols], l, r_, start=(j == 0), stop=(j == n - 1))
                return ins
            S.op("pe", fn, r=list(extra_r), w=[pskey])

        EXT_BLKS = [(0, 512), (512, 512), (1024, 512)]
        OWN_BLKS = [(OWN0, 512), (OWN0 + 512, 512)]
        evn = [0]

        def evac(dst, src, rk, wk):
            evn[0] += 1
            if evn[0] % 2:
                S.op("act", lambda e: e.copy(out=dst, in_=src), r=rk, w=wk)
            else:
                S.op("dve", lambda e: e.tensor_copy(out=dst, in_=src), r=rk, w=wk)

        def proj_fm(W, wkey, wc0, blks, dst_fn, dkey):
            for bi, (t0, sz) in enumerate(blks):
                pst, pk = next_ps(0, 2)
                mm_group(pst, pk, [(W[:, k, wc0:wc0 + 128], xTb[:, k, t0:t0 + sz]) for k in range(16)],
                         [wkey, "xTb"], sz)
                evac(dst_fn(bi), pst[:, 0:sz], [pk], [dkey])

        astep = [0]
        apair = [0]

        def attention_group(heads):
            steps = []
            for hd in heads:
                for qb in range(2):
                    for kc in range(hd[4]):
                        steps.append((hd, qb, kc, apair[0]))
                    apair[0] += 1
            LA = 2
            base = astep[0]
            astep[0] += len(steps)
            for s in range(len(steps) + LA):
                if s < len(steps):
                    (qh, kh, vc0, yidx, nkc, tile0_fn, tab_src_fn, tab_is_bf, sink_col), qb, kc, pid = steps[s]
                    g = base + s
                    ti, si = g % NTAB, g % 3
                    tile = tile0_fn(qb) + kc
                    if tab_is_bf:
                        tb_ap, tbk = tabb[ti], f"tabb{ti}"
                    else:
                        tb_ap, tbk = tab[ti], f"tab{ti}"
                    S.dma("pool", tb_ap, tab_src_fn(qb, kc), w=[tbk])
                    S.op("pe", lambda e, p=PS[2 + si], tile=tile, qb=qb, kh=kh, qh=qh: e.matmul(
                        p[:, :], kT[:, kh, tile * 128:(tile + 1) * 128], qT[:, qh, qb * 512:(qb + 1) * 512],
                        start=True, stop=True), r=[f"kT{kh}", f"qT{qh}"], w=[f"ps{2 + si}"])
                b = s - LA
                if b >= 0:
                    (qh, kh, vc0, yidx, nkc, tile0_fn, tab_src_fn, tab_is_bf, sink_col), qb, kc, pid = steps[b]
                    g = base + b
                    ti, si = g % NTAB, g % 3
                    tile = tile0_fn(qb) + kc
                    if tab_is_bf:
                        tb_ap, tbk = tabb[ti], f"tabb{ti}"
                    else:
                        tb_ap, tbk = tab[ti], f"tab{ti}"
                    ai = 5 + pid % 2
                    di = 7 if pid % 2 == 0 else 1
                    acc, acck, den, denk = PS[ai], f"ps{ai}", PS[di], f"ps{di}"
                    S.op("dve", lambda e, p=PS[2 + si], l=Lb[si], t=tb_ap: e.scalar_tensor_tensor(
                        out=l, in0=p[:, :], scalar=SCALE, in1=t, op0=ALU.mult, op1=ALU.add),
                        r=[f"ps{2 + si}", tbk], w=[f"L{si}"])
                    S.op("act", lambda e, l=Lb[si], pt=PT[si]: e.activation(out=pt, in_=l, func=AF.Exp),
                         r=[f"L{si}"], w=[f"PT{si}"])
                    S.op("pe", lambda e, a=acc, pt=PT[si], tile=tile, kc=kc, vc0=vc0, nkc=nkc: e.matmul(
                        a[:, :], vv[:, tile, vc0:vc0 + 128], pt, start=(kc == 0), stop=(kc == nkc - 1)),
                        r=[f"vv{tile}", f"PT{si}"], w=[acck])
                    S.op("pe", lambda e, dn=den, pt=PT[si], kc=kc, nkc=nkc: e.matmul(
                        dn[:, :], ones, pt, start=(kc == 0), stop=(kc == nkc - 1)),
                        r=["ones", f"PT{si}"], w=[denk])
                    if kc == nkc - 1:
                        if sink_col is not None:
                            S.op("dve", lambda e, dn=den, sc=sink_col: e.tensor_scalar(
                                out=rden, in0=dn[:, :], scalar1=esink[:, sc:sc + 1], scalar2=None, op0=ALU.add),
                                r=[denk, "esink"], w=["rden"])
                            S.op("dve", lambda e: e.reciprocal(out=rden, in_=rden), r=["rden"], w=["rden"], strict=True)
                        else:
                            S.op("dve", lambda e, dn=den: e.reciprocal(out=rden, in_=dn[:, :]), r=[denk], w=["rden"])
                        S.op("dve", lambda e, a=acc, qb=qb, yidx=yidx: e.tensor_tensor(
                            out=yT[:, yidx, qb * 512:(qb + 1) * 512], in0=a[:, :], in1=rden, op=ALU.mult),
                            r=[acck, "rden"], w=[f"yT{yidx}"], strict=True)

        for hg in range(2):
            Wq, kq = load_w(w_in_v, hg * 512)
            for hh in range(4):
                proj_fm(Wq, kq, hh * 128, OWN_BLKS, lambda bi, hh=hh: qT[:, hh, bi * 512:(bi + 1) * 512], f"qT{hh}")
            Wk, kk = load_w(w_in_v, 1024 + hg * 512)
            for hh in range(4):
                proj_fm(Wk, kk, hh * 128, EXT_BLKS, lambda bi, hh=hh: kT[:, hh, bi * 512:(bi + 1) * 512], f"kT{hh}")
            Wv, kv = load_w(w_in_v, 2048 + hg * 512)
            for tile in range(12):
                pst, pk = next_ps(0, 2)
                mm_group(pst, pk, [(xTb[:, k, tile * 128:(tile + 1) * 128], Wv[:, k, 0:512]) for k in range(16)],
                         [kv, "xTb"], 512)
                evac(vv[:, tile, :], pst[:, :], [pk], [f"vv{tile}"])
            attention_group([(hh, hh, hh * 128, hg * 4 + hh, 8, (lambda qb: qb * 4),
                              (lambda qb, kc, h=hg * 4 + hh: tabA[h, qb, kc, :, :]), False, None) for hh in range(4)])

        Wkv, kkv = load_w(w_in_v, 4096)
        for kvh in range(2):
            proj_fm(Wkv, kkv, kvh * 128, EXT_BLKS, lambda bi, kvh=kvh: kT[:, kvh, bi * 512:(bi + 1) * 512], f"kT{kvh}")
        for tile in range(12):
            pst, pk = next_ps(0, 2)
            mm_group(pst, pk, [(xTb[:, k, tile * 128:(tile + 1) * 128], Wkv[:, k, 256:512]) for k in range(16)],
                     [kkv, "xTb"], 256)
            evac(vv[:, tile, 0:256], pst[:, 0:256], [pk], [f"vv{tile}"])
        for hg in range(2):
            Wq, kq = load_w(w_in_v, 3072 + hg * 512)
            for hh in range(4):
                proj_fm(Wq, kq, hh * 128, OWN_BLKS, lambda bi, hh=hh: qT[:, hh, bi * 512:(bi + 1) * 512], f"qT{hh}")
            attention_group([(hh, hg, hg * 128, 8 + hg * 4 + hh, 6, (lambda qb: qb * 4 + 1),
                              (lambda qb, i, h=hg * 4 + hh: tabB[h, qb, i, :, :]), True, hg * 4 + hh) for hh in range(4)])

        S.barrier()
        AR.off = ASZ - 8192
        wst.append(AR.take(8192, F32, (4, 512)))
        wst.append(AR.take(8192, F32, (4, 512)))
        AR.off = Z0
        cgs = AR.take(4 * 1026 * 4, F32, (4, 1026))
        tt = AR.take(4096, F32)
        CBLK = [(OWN0 - 1, 512), (OWN0 + 511, 512), (OWN0 + 1023, 2)]
        for half in range(2):
            Wc, kc_ = load_w(w_in_v, 5632 + half * 512)
            for cc in range(4):
                for bi, (t0, sz) in enumerate(CBLK):
                    pst, pk = next_ps(0, 2)
                    mm_group(pst, pk, [(Wc[:, k, cc * 128:(cc + 1) * 128], xTb[:, k, t0:t0 + sz]) for k in range(16)],
                             [kc_, "xTb"], sz)
                    S.op("act", lambda e, p=pst, bi=bi, sz=sz, cc=cc: e.copy(
                        out=cgs[:, cc, bi * 512:bi * 512 + sz], in_=p[:, 0:sz]), r=[pk], w=[f"cgs{cc}"])
            Wh, kh_ = load_w(w_in_v, 6656 + half * 512)
            for cc in range(4):
                for bi, (t0, sz) in enumerate(CBLK):
                    pst, pk = next_ps(0, 2)
                    mm_group(pst, pk, [(Wh[:, k, cc * 128:(cc + 1) * 128], xTb[:, k, t0:t0 + sz]) for k in range(16)],
                             [kh_, "xTb"], sz)
                    S.op("dve", lambda e, p=pst, bi=bi, sz=sz, cc=cc: e.tensor_tensor(
                        out=cgs[:, cc, bi * 512:bi * 512 + sz], in0=p[:, 0:sz], in1=cgs[:, cc, bi * 512:bi * 512 + sz],
                        op=ALU.mult), r=[pk, f"cgs{cc}"], w=[f"cgs{cc}"], strict=True)
                S.op("dve", lambda e, cc=cc: e.tensor_tensor(out=cgs[:, cc, 0:1], in0=cgs[:, cc, 0:1], in1=em_sb[:, 0:1],
                                                             op=ALU.mult), r=[f"cgs{cc}", "em_sb"], w=[f"cgs{cc}"], strict=True)
                S.op("dve", lambda e, cc=cc: e.tensor_tensor(out=cgs[:, cc, 1025:1026], in0=cgs[:, cc, 1025:1026],
                                                             in1=em_sb[:, 1:2], op=ALU.mult),
                     r=[f"cgs{cc}", "em_sb"], w=[f"cgs{cc}"], strict=True)
            Wb, kb_ = load_w(w_in_v, 4608 + half * 512)
            for cc in range(4):
                c = half * 4 + cc
                S.op("dve", lambda e, cc=cc, c=c: e.tensor_scalar(
                    out=tt, in0=cgs[:, cc, 0:1024], scalar1=conv_sb[:, c, 0:1], scalar2=None, op0=ALU.mult),
                    r=[f"cgs{cc}", "conv_sb"], w=["tt"], strict=True)
                S.op("dve", lambda e, cc=cc, c=c: e.scalar_tensor_tensor(
                    out=tt, in0=cgs[:, cc, 1:1025], scalar=conv_sb[:, c, 1:2], in1=tt, op0=ALU.mult, op1=ALU.add),
                    r=[f"cgs{cc}", "conv_sb", "tt"], w=["tt"], strict=True)
                S.op("dve", lambda e, cc=cc, c=c: e.scalar_tensor_tensor(
                    out=tt, in0=cgs[:, cc, 2:1026], scalar=conv_sb[:, c, 2:3], in1=tt, op0=ALU.mult, op1=ALU.add),
                    r=[f"cgs{cc}", "conv_sb", "tt"], w=["tt"], strict=True)
                for bi in range(2):
                    t0 = OWN0 + bi * 512
                    pst, pk = next_ps(2, 2)
                    mm_group(pst, pk, [(Wb[:, k, cc * 128:(cc + 1) * 128], xTb[:, k, t0:t0 + 512]) for k in range(16)],
                             [kb_, "xTb"], 512)
                    S.op("dve", lambda e, p=pst, bi=bi, c=c: e.tensor_tensor(
                        out=yT[:, 16 + c, bi * 512:(bi + 1) * 512], in0=p[:, :], in1=tt[:, bi * 512:(bi + 1) * 512],
                        op=ALU.mult), r=[pk, "tt"], w=[f"yT{16 + c}"], strict=True)

        if debug == "yT":
            for i in range(24):
                S.dma("sp", dbg[i, :, :], yT[:, i, :], r=[f"yT{i}"])

        S.barrier()
        AR.off = Z0
        macc = AR.take(4 * 1024 * 4, F32, (4, NT))
        gsb = [AR.take(2048, F32) for _ in range(2)]
        tmp = [AR.take(2048, F32) for _ in range(2)]
        mbf = [AR.take(1024, BF16) for _ in range(2)]
        wbr = [AR.take(8192, BF16, (8, 512)) for _ in range(2)]
        wbn = [0]
        w_br_v = w_br.rearrange("n (k p) d -> n p k d", p=128)
        gn = [0]
        for G in range(4):
            for n in range(3):
                Wg, kg = load_w(w_in_v, 7680 + n * 2048 + G * 512)
                Wr, kr = load_w(w_br_v[n], G * 512, 512, 8, bufs=wbr, pfx="wbr", ctr=wbn)
                for j in range(4):
                    chunk = G * 4 + j
                    for tb in range(2):
                        pg, pgk = next_ps(0, 2)
                        mm_group(pg, pgk, [(Wg[:, k, j * 128:(j + 1) * 128], xTb[:, k, OWN0 + tb * 512:OWN0 + (tb + 1) * 512])
                                           for k in range(16)], [kg, "xTb"], 512)
                        pb, pbk = next_ps(2, 2)
                        mm_group(pb, pbk, [(Wr[:, k, j * 128:(j + 1) * 128], yT[:, n * 8 + k, tb * 512:(tb + 1) * 512])
                                           for k in range(8)], [kr] + [f"yT{n * 8 + k}" for k in range(8)], 512)
                        gi = gn[0] % 2
                        gn[0] += 1
                        bcol = n * 16 + chunk
                        S.op("act", lambda e, p=pg, gi=gi, bcol=bcol: e.activation(
                            out=gsb[gi], in_=p[:, :], func=AF.Sigmoid, bias=bg_sb[:, bcol:bcol + 1]),
                            r=[pgk, "bg_sb"], w=[f"gsb{gi}"])
                        mslice = macc[:, j, tb * 512:(tb + 1) * 512]
                        mk = f"macc{j}_{tb}"
                        if n == 0:
                            S.op("dve", lambda e, p=pb, gi=gi, m=mslice: e.tensor_tensor(
                                out=m, in0=p[:, :], in1=gsb[gi], op=ALU.mult), r=[pbk, f"gsb{gi}"], w=[mk])
                        else:
                            S.op("dve", lambda e, p=pb, gi=gi: e.tensor_tensor(
                                out=tmp[gi], in0=p[:, :], in1=gsb[gi], op=ALU.mult), r=[pbk, f"gsb{gi}"], w=[f"tmp{gi}"])
                            if n == 1:
                                S.op("pool", lambda e, gi=gi, m=mslice: e.tensor_tensor(
                                    out=m, in0=m, in1=tmp[gi], op=ALU.add), r=[f"tmp{gi}", mk], w=[mk])
                            else:
                                S.op("pool", lambda e, gi=gi, m=mslice, tb=tb: e.tensor_tensor(
                                    out=mbf[tb], in0=m, in1=tmp[gi], op=ALU.add), r=[f"tmp{gi}", mk], w=[f"mbf{tb}"])
                                S.dma("pool", mrg[chunk, :, tb * 512:(tb + 1) * 512], mbf[tb], r=[f"mbf{tb}"], w=[f"mrg{chunk}"])
        if debug == "mrg":
            S.barrier()
            for i in range(16):
                S.dma("sp", yT[:, i, :], mrg[i, :, :], r=[f"mrg{i}"], w=[f"yTm{i}"])
                S.dma("sp", dbg[i, :, :], yT[:, i, :], r=[f"yTm{i}"])

        S.barrier()
        AR.off = Z0 - 24576
        wo_extra = [AR.take(16384, BF16, (16, 512)) for _ in range(2)]
        x1buf = AR.take(2 * 2048 * 4, F32, (2, 2048))
        x1Tt = AR.take(16 * 128 * 4, F32, (16, 128))
        wr_sb = AR.take(16 * 16 * 4, F32, (16, 16))
        ident = AR.take(512, F32)
        Eb = AR.take(512, F32, (8, 16))
        affb = AR.take(512, F32, (8, 16))
        ssum = AR.take(32, F32)
        xr = [AR.take(8192, F32) for _ in range(2)]
        g_sb = AR.take(8192, F32)
        b_sb = AR.take(8192, F32)
        stats = AR.take(4 * 6 * 4, F32, (4, 6))
        mv = AR.take(16, F32)
        rstd = AR.take(16, F32)
        mT = yT
        for i in range(16):
            S.dma("sp", mT[:, i, :], mrg[i, :, :], r=[f"mrg{i}"], w=[f"mT{i}"])
        S.dma("act", g_sb, lng[:, :], w=["g_sb"])
        S.dma("act", b_sb, lnb[:, :], w=["b_sb"])
        S.dma("act", wr_sb, wr.rearrange("(k p) e -> p k e", p=128), w=["wr"])
        S.dma("act", ident, identd[:, :], w=["ident"])
        w_out_v = w_out.rearrange("(k p) n -> p k n", p=128)
        wo_bufs = [(wbf[0], "wbf0"), (wbf[1], "wbf1"), (wo_extra[0], "woe0"), (wo_extra[1], "woe1")]
        for G in range(4):
            buf, key = wo_bufs[G]
            stage_cast(lambda k0, k1, G=G: w_out_v[:, k0:k1, G * 512:(G + 1) * 512],
                       lambda k0, k1, buf=buf: buf[:, k0:k1, 0:512], 16, 512, key)
        pending_router = []

        def router_part(t8, bi, k8):
                for g4 in range(4):
                    def tfn(e, g4=g4, bi=bi):
                        ins = None
                        for kk in range(4):
                            k = g4 * 4 + kk
                            ins = e.matmul(PS[4 + g4][:, kk * 128:(kk + 1) * 128], x1buf[:, bi, k * 128:(k + 1) * 128], ident,
                                           start=True, stop=True)
                        return ins
                    S.op("pe", tfn, r=[k8, "ident"], w=[f"ps{4 + g4}"])
                    if g4 % 2:
                        S.op("act", lambda e, g4=g4: e.copy(out=x1Tt[:, g4 * 4:(g4 + 1) * 4, :].rearrange("p a b -> p (a b)"),
                                                            in_=PS[4 + g4][:, :]), r=[f"ps{4 + g4}"], w=[f"x1Tt{g4}"])
                    else:
                        S.op("dve", lambda e, g4=g4: e.tensor_copy(out=x1Tt[:, g4 * 4:(g4 + 1) * 4, :].rearrange("p a b -> p (a b)"),
                                                                   in_=PS[4 + g4][:, :]), r=[f"ps{4 + g4}"], w=[f"x1Tt{g4}"])

                def lfn(e, t8=t8):
                    ins = None
                    for k in range(16):
                        ins = e.matmul(PS[3][:, t8 * 16:(t8 + 1) * 16], x1Tt[:, k, :], wr_sb[:, k, :], start=(k == 0), stop=(k == 15))
                    return ins
                S.op("pe", lfn, r=[f"x1Tt{g4}" for g4 in range(4)] + ["wr"], w=["lgps"])

        for t8 in range(8):
            xi = t8 % 2
            bi = t8 % 2
            k8 = f"x1b{bi}"
            S.dma("act", xr[xi], xrow[t8 * 128:(t8 + 1) * 128, :], w=[f"xr{xi}"])
            for G in range(4):
                buf, key = wo_bufs[G]
                pst, pk = next_ps(0, 3)
                mm_group(pst, pk, [(mT[:, k, t8 * 128:(t8 + 1) * 128], buf[:, k, 0:512]) for k in range(16)],
                         [key] + [f"mT{k}" for k in range(16)], 512)
                S.op("dve", lambda e, p=pst, xi=xi, bi=bi, G=G: e.scalar_tensor_tensor(
                    out=x1buf[:, bi, G * 512:(G + 1) * 512], in0=xr[xi][:, G * 512:(G + 1) * 512], scalar=ALPHA, in1=p[:, :],
                    op0=ALU.mult, op1=ALU.add), r=[pk, f"xr{xi}"], w=[k8])
            for q in range(4):
                S.op("dve", lambda e, bi=bi, q=q: e.bn_stats(out=stats[:, q, :], in_=x1buf[:, bi, q * 512:(q + 1) * 512]),
                     r=[k8], w=[f"stats{q}"], strict=True)
            S.op("dve", lambda e: e.bn_aggr(out=mv[:, 0:2], in_=stats.rearrange("p a b -> p (a b)")),
                 r=[f"stats{q}" for q in range(4)], w=["mv"], strict=True)
            S.op("dve", lambda e: e.tensor_scalar(out=rstd[:, 0:1], in0=mv[:, 1:2], scalar1=EPS, scalar2=None,
                                                  op0=ALU.add), r=["mv"], w=["rstd"], strict=True)
            S.op("act", lambda e: e.activation(out=rstd[:, 0:1], in_=rstd[:, 0:1], func=AF.Sqrt), r=["rstd"], w=["rstd"])
            S.op("dve", lambda e: e.reciprocal(out=rstd[:, 0:1], in_=rstd[:, 0:1]), r=["rstd"], w=["rstd"], strict=True)
            S.op("dve", lambda e, bi=bi: e.tensor_scalar(out=x1buf[:, bi, :], in0=x1buf[:, bi, :], scalar1=mv[:, 0:1],
                                                         scalar2=rstd[:, 0:1], op0=ALU.subtract, op1=ALU.mult),
                 r=[k8, "mv", "rstd"], w=[k8], strict=True)
            S.op("dve", lambda e, bi=bi: e.tensor_tensor(out=x1buf[:, bi, :], in0=x1buf[:, bi, :], in1=g_sb, op=ALU.mult),
                 r=[k8, "g_sb"], w=[k8], strict=True)
            S.op("pool", lambda e, bi=bi: e.tensor_tensor(out=x1buf[:, bi, :], in0=x1buf[:, bi, :], in1=b_sb, op=ALU.add),
                 r=[k8, "b_sb"], w=[k8])
            S.dma("pool", x1o[t8 * 128:(t8 + 1) * 128, :], x1buf[:, bi, :], r=[k8])
            pending_router.append((t8, bi, k8))
            if len(pending_router) > 1:
                router_part(*pending_router.pop(0))
        while pending_router:
            router_part(*pending_router.pop(0))
        S.op("act", lambda e: e.activation(out=Eb.rearrange("p a b -> p (a b)"), in_=PS[3][:, 0:128], func=AF.Exp),
             r=["lgps"], w=["Eb"])
        S.op("dve", lambda e: e.reduce_sum(out=ssum, in_=Eb, axis=AX.X), r=["Eb"], w=["ssum"], strict=True)
        S.op("dve", lambda e: e.reciprocal(out=ssum, in_=ssum), r=["ssum"], w=["ssum"], strict=True)
        for jj in range(8):
            S.op("dve", lambda e, jj=jj: e.tensor_scalar(out=affb[:, jj, :], in0=Eb[:, jj, :], scalar1=ssum[:, jj:jj + 1],
                                                         scalar2=None, op0=ALU.mult), r=["Eb", "ssum"], w=["affb"], strict=True)
        S.dma("pool", affo[:, :, :], affb, r=["affb"])
        S.emit(st)
    return nc

import numpy as np
import ml_dtypes
NPBF = ml_dtypes.bfloat16
NEG = -30000.0


def make_tabA(rpb_l, c):
    out = np.empty((8, 2, 8, 128, 512), np.float32)
    kk = np.arange(128)[:, None]
    qq = np.arange(512)[None, :]
    for qb in range(2):
        gq = 1024 * c + qb * 512 + qq
        QR, QC = gq // 64, gq % 64
        rs = np.clip(QR - 4, 0, 120)
        cs = np.clip(QC - 8, 0, 48)
        for kc in range(8):
            tile = qb * 4 + kc
            gk = 1024 * c - 256 + tile * 128 + kk
            KR, KC = gk // 64, gk % 64
            valid = (gk >= 0) & (gk < 8192) & (KR >= rs) & (KR <= rs + 7) & (KC >= cs) & (KC <= cs + 15)
            ri = np.clip(KR - QR + 7, 0, 14)
            ci = np.clip(KC - QC + 15, 0, 30)
            ri, ci = np.broadcast_arrays(ri, ci)
            vals = rpb_l[:, ri, ci]
            out[:, qb, kc] = np.where(valid[None], vals, np.float32(NEG))
    return out


_tabB_cache = {}


def make_tabB(c):
    if c in _tabB_cache:
        return _tabB_cache[c]
    out = np.empty((8, 2, 6, 128, 512), np.float32)
    kk = np.arange(128)[:, None]
    qq = np.arange(512)[None, :]
    slopes = 2.0 ** (-8.0 * np.arange(1, 9, dtype=np.float32) / 8)
    for qb in range(2):
        gq = 1024 * c + qb * 512 + qq
        for i in range(6):
            tile = qb * 4 + 1 + i
            gk = 1024 * c - 256 + tile * 128 + kk
            dist = np.abs(gk - gq)
            valid = (gk >= 0) & (gk < 8192) & (dist <= 128)
            for h in range(8):
                out[h, qb, i] = np.where(valid, -slopes[h] * dist.astype(np.float32), np.float32(NEG))
    o = out.astype(NPBF)
    _tabB_cache[c] = o
    return o


def m_inputs(c, x_full, w_in_l, w_br_l, w_out_l, b_gate_l, ln_g_l0, ln_b_l0, conv_w_l, sink_l, rpb_l, w_router_l):
    g0 = 1024 * c - 256
    ext = np.zeros((1536, 2048), np.float32)
    lo, hi = max(g0, 0), min(g0 + 1536, 8192)
    ext[lo - g0:hi - g0] = x_full[lo:hi]
    return {
        "xT": np.ascontiguousarray(ext.T),
        "xrow": np.ascontiguousarray(x_full[1024 * c:1024 * (c + 1)]),
        "w_in": w_in_l, "w_br": w_br_l, "w_out": w_out_l,
        "bgT": np.ascontiguousarray(b_gate_l.reshape(48, 128).T),
        "lng": np.ascontiguousarray(np.broadcast_to(ln_g_l0[None, :], (128, 2048))),
        "lnb": np.ascontiguousarray(np.broadcast_to(ln_b_l0[None, :], (128, 2048))),
        "convT": np.ascontiguousarray(conv_w_l.reshape(3, 8, 128).transpose(2, 1, 0)),
        "sinkb": np.ascontiguousarray(np.broadcast_to(sink_l[None, :], (128, 8))),
        "emask": np.ascontiguousarray(np.broadcast_to(
            np.array([0.0 if c == 0 else 1.0, 0.0 if c == 7 else 1.0], np.float32)[None, :], (128, 2))),
        "wr": w_router_l, "ident": np.eye(128, dtype=np.float32),
        "tabA": make_tabA(rpb_l, c),
        "tabB": make_tabB(c),
    }


def build_R1():
    nc = bass.Bass("TRN2", target_bir_lowering=False)
    dt = nc.dram_tensor
    Ain = dt("A", [128, 2, 64], F32, kind="ExternalInput").ap()
    selo = dt("sel", [128, 2, 64], F32, kind="ExternalOutput").ap()
    with ExitStack() as st:
        S = Sched(nc)
        ASZ = 4000
        AR = Arena(st.enter_context(nc.sbuf_tensor("arena", [128, ASZ], BF16)), ASZ)
        PS = [st.enter_context(nc.psum_tensor(f"ps{i}", [128, 512], F32)) for i in range(4)]
        A = AR.take(512, F32, (2, 64))
        cmp_ = AR.take(512, F32, (2, 64))
        ones = AR.take(256, BF16)
        lo = AR.take(8, F32); hi = AR.take(8, F32); mid = AR.take(8, F32)
        cnt = AR.take(8, F32); cntb = AR.take(4, BF16); AR.off += 2
        ge = AR.take(8, F32); d1 = AR.take(8, F32); d2 = AR.take(8, F32)
        S.op("pool", lambda e: e.memset(ones, 1.0), w=["ones"])
        S.op("pool", lambda e: e.memset(lo, 0.0), w=["lo"])
        S.op("pool", lambda e: e.memset(hi, 1.0), w=["hi"])
        S.dma("sp", A, Ain[:, :, :], w=["A"])
        for it in range(31):
            S.op("dve", lambda e: e.tensor_tensor(out=mid, in0=lo, in1=hi, op=ALU.add), r=["lo", "hi"], w=["mid"], strict=True)
            S.op("dve", lambda e: e.tensor_scalar(out=mid, in0=mid, scalar1=0.5, scalar2=None, op0=ALU.mult),
                 r=["mid"], w=["mid"], strict=True)
            for ee in range(2):
                S.op("dve", lambda e, ee=ee: e.tensor_scalar(out=cmp_[:, ee, :], in0=A[:, ee, :], scalar1=mid[:, ee:ee + 1],
                                                             scalar2=None, op0=ALU.is_gt), r=["A", "mid"], w=["cmp"], strict=True)
            S.op("dve", lambda e: e.reduce_sum(out=cnt, in_=cmp_, axis=AX.X), r=["cmp"], w=["cnt"], strict=True)
            S.op("dve", lambda e: e.tensor_copy(out=cntb, in_=cnt), r=["cnt"], w=["cntb"], strict=True)
            S.op("pe", lambda e: e.matmul(PS[2][:, 0:2], ones, cntb, start=True, stop=True), r=["ones", "cntb"], w=["tot"])
            S.op("dve", lambda e: e.tensor_scalar(out=ge, in0=PS[2][:, 0:2], scalar1=1023.5, scalar2=None, op0=ALU.is_gt),
                 r=["tot"], w=["ge"], strict=True)
            S.op("dve", lambda e: e.tensor_tensor(out=d1, in0=mid, in1=lo, op=ALU.subtract), r=["mid", "lo"], w=["d1"], strict=True)
            S.op("dve", lambda e: e.tensor_tensor(out=d1, in0=d1, in1=ge, op=ALU.mult), r=["d1", "ge"], w=["d1"], strict=True)
            S.op("dve", lambda e: e.tensor_tensor(out=d2, in0=hi, in1=mid, op=ALU.subtract), r=["mid", "hi"], w=["d2"], strict=True)
            S.op("dve", lambda e: e.tensor_tensor(out=d2, in0=d2, in1=ge, op=ALU.mult), r=["d2", "ge"], w=["d2"], strict=True)
            S.op("dve", lambda e: e.tensor_tensor(out=lo, in0=lo, in1=d1, op=ALU.add), r=["lo", "d1"], w=["lo"], strict=True)
            S.op("dve", lambda e: e.tensor_tensor(out=hi, in0=mid, in1=d2, op=ALU.add), r=["mid", "d2"], w=["hi"], strict=True)
        for ee in range(2):
            S.op("dve", lambda e, ee=ee: e.tensor_scalar(out=cmp_[:, ee, :], in0=A[:, ee, :], scalar1=lo[:, ee:ee + 1],
                                                         scalar2=None, op0=ALU.is_gt), r=["A", "lo"], w=["cmp"], strict=True)
        S.dma("sp", selo[:, :, :], cmp_, r=["cmp"])
        S.emit(st)
    return nc


def build_R2():
    nc = bass.Bass("TRN2", target_bir_lowering=False)
    dt = nc.dram_tensor
    xeT = dt("xeT", [2, 2048, 1024], F32, kind="ExternalInput").ap()
    gs = dt("gs", [2, 128, 8], F32, kind="ExternalInput").ap()
    wg = dt("wg", [2, 2048, 1536], F32, kind="ExternalInput").ap()
    wu = dt("wu", [2, 2048, 1536], F32, kind="ExternalInput").ap()
    wd = dt("wd", [2, 1536, 2048], F32, kind="ExternalInput").ap()
    yeo = dt("ye", [2, 1024, 2048], F32, kind="ExternalOutput").ap()
    with ExitStack() as st:
        S = Sched(nc)
        ASZ = 102000
        AR = Arena(st.enter_context(nc.sbuf_tensor("arena", [128, ASZ], BF16)), ASZ)
        PS = [st.enter_context(nc.psum_tensor(f"ps{i}", [128, 512], F32)) for i in range(8)]
        wst = [AR.take(8192, F32, (4, 512)) for _ in range(4)]
        wbf = [AR.take(16384, BF16, (16, 512)) for _ in range(4)]
        xebs = [AR.take(32768, BF16, (16, 1024)) for _ in range(2)]
        hid = AR.take(24576, BF16, (12, 1024))
        sg = [AR.take(2048, F32) for _ in range(2)]
        yo = [AR.take(2048, F32) for _ in range(2)]
        gs_sb = AR.take(64, F32, (2, 8))
        S.dma("sp", gs_sb, gs.rearrange("e p s -> p e s"), w=["gs"])
        stg_n = [0]

        def stage_cast(src_fn, dst_fn, nk, ncols, keyw):
            for k0 in range(0, nk, 4):
                k1 = min(nk, k0 + 4)
                i = stg_n[0] % 4
                stg_n[0] += 1
                S.dma("sp", wst[i][:, 0:k1 - k0, 0:ncols], src_fn(k0, k1), w=[f"wst{i}"])
                src = wst[i][:, 0:k1 - k0, 0:ncols]
                dst = dst_fn(k0, k1)
                ce = ("act", "dve")[stg_n[0] % 2]
                if ce == "act":
                    S.op("act", lambda e, s=src, d=dst: e.copy(out=d, in_=s), r=[f"wst{i}"], w=[keyw])
                else:
                    S.op(ce, lambda e, s=src, d=dst: e.tensor_copy(out=d, in_=s), r=[f"wst{i}"], w=[keyw])
        wn = [0]

        def load_w(src_v, col0, nk):
            i = wn[0] % 4
            wn[0] += 1
            stage_cast(lambda k0, k1: src_v[:, k0:k1, col0:col0 + 512],
                       lambda k0, k1: wbf[i][:, k0:k1, 0:512], nk, 512, f"wbf{i}")
            return wbf[i], f"wbf{i}"
        psn = [0]

        def next_ps(lo, n):
            i = lo + psn[0] % n
            psn[0] += 1
            return PS[i], f"ps{i}"

        def mm_group(pst, pskey, pairs, extra_r):
            def fn(e):
                ins = None
                n = len(pairs)
                for j, (l, r_) in enumerate(pairs):
                    ins = e.matmul(pst[:, :], l, r_, start=(j == 0), stop=(j == n - 1))
                return ins
            S.op("pe", fn, r=list(extra_r), w=[pskey])
        def load_xe(e2):
            xv = xeT[e2].rearrange("(k p) t -> p k t", p=128)
            for sbk in range(2):
                stage_cast(lambda k0, k1, sbk=sbk: xv[:, k0:k1, sbk * 512:(sbk + 1) * 512],
                           lambda k0, k1, sbk=sbk: xebs[e2][:, k0:k1, sbk * 512:(sbk + 1) * 512], 16, 512, f"xeb{e2}")
        load_xe(0)
        for e2 in range(2):
            xeb = xebs[e2]
            xek = f"xeb{e2}"
            wgv = wg[e2].rearrange("(k p) n -> p k n", p=128)
            wuv = wu[e2].rearrange("(k p) n -> p k n", p=128)
            wdv = wd[e2].rearrange("(k p) n -> p k n", p=128)
            gi = 0
            for fg in range(3):
                Wg, kg = load_w(wgv, fg * 512, 16)
                Wu, ku = load_w(wuv, fg * 512, 16)
                for j in range(4):
                    f = fg * 4 + j
                    for sbk in range(2):
                        pg, pgk = next_ps(0, 2)
                        mm_group(pg, pgk, [(Wg[:, k, j * 128:(j + 1) * 128], xeb[:, k, sbk * 512:(sbk + 1) * 512])
                                           for k in range(16)], [kg, xek])
                        pu, puk = next_ps(2, 2)
                        mm_group(pu, puk, [(Wu[:, k, j * 128:(j + 1) * 128], xeb[:, k, sbk * 512:(sbk + 1) * 512])
                                           for k in range(16)], [ku, xek])
                        gi += 1
                        S.op("act", lambda e, p=pg, g=sg[gi % 2]: e.activation(out=g, in_=p[:, :], func=AF.Silu),
                             r=[pgk], w=[f"sg{gi % 2}"])
                        S.op("dve", lambda e, p=pu, g=sg[gi % 2], f=f, sbk=sbk: e.tensor_tensor(
                            out=hid[:, f, sbk * 512:(sbk + 1) * 512], in0=p[:, :], in1=g, op=ALU.mult),
                            r=[puk, f"sg{gi % 2}"], w=[f"hid{f}"])
            for G in range(4):
                Wd, kd = load_w(wdv, G * 512, 12)
                if G == 1 and e2 == 0:
                    load_xe(1)
                for s8 in range(8):
                    py, pyk = next_ps(4, 4)
                    mm_group(py, pyk, [(hid[:, f, s8 * 128:(s8 + 1) * 128], Wd[:, f, 0:512]) for f in range(12)],
                             [kd] + [f"hid{f}" for f in range(12)])
                    gi += 1
                    S.op("dve", lambda e, p=py, y=yo[gi % 2], s8=s8, e2=e2: e.tensor_scalar(
                        out=y, in0=p[:, :], scalar1=gs_sb[:, e2, s8:s8 + 1], scalar2=None, op0=ALU.mult),
                        r=[pyk, "gs"], w=[f"yo{gi % 2}"])
                    S.dma("act", yeo[e2, s8 * 128:(s8 + 1) * 128, G * 512:(G + 1) * 512], yo[gi % 2], r=[f"yo{gi % 2}"])
        S.emit(st)
    return nc


def build_C(K):
    nc = bass.Bass("TRN2", target_bir_lowering=False)
    dt = nc.dram_tensor
    x1 = dt("x1", [1024, 2048], F32, kind="ExternalInput").ap()
    Y = dt("Y", [K, 1024, 2048], F32, kind="ExternalInput").ap()
    lng = dt("lng", [128, 2048], F32, kind="ExternalInput").ap()
    lnb = dt("lnb", [128, 2048], F32, kind="ExternalInput").ap()
    x2 = dt("x2", [1024, 2048], F32, kind="ExternalOutput").ap()
    with ExitStack() as st:
        S = Sched(nc)
        ASZ = 60000
        AR = Arena(st.enter_context(nc.sbuf_tensor("arena", [128, ASZ], BF16)), ASZ)
        acc = [AR.take(8192, F32) for _ in range(2)]
        yb = [AR.take(8192, F32) for _ in range(3)]
        g_sb = AR.take(8192, F32)
        b_sb = AR.take(8192, F32)
        stats = AR.take(96, F32, (4, 6))
        mv = AR.take(16, F32)
        rstd = AR.take(16, F32)
        S.dma("sp", g_sb, lng[:, :], w=["g_sb"])
        S.dma("sp", b_sb, lnb[:, :], w=["b_sb"])
        yn = 0
        for t8 in range(8):
            a = acc[t8 % 2]
            ak = f"acc{t8 % 2}"
            S.dma("sp", a, x1[t8 * 128:(t8 + 1) * 128, :], w=[ak])
            S.op("act", lambda e, a=a: e.mul(out=a, in_=a, mul=ALPHA), r=[ak], w=[ak])
            for k in range(K):
                yi = yn % 3
                yn += 1
                S.dma("sp", yb[yi], Y[k, t8 * 128:(t8 + 1) * 128, :], w=[f"yb{yi}"])
                S.op("dve", lambda e, a=a, yi=yi: e.tensor_tensor(out=a, in0=a, in1=yb[yi], op=ALU.add),
                     r=[ak, f"yb{yi}"], w=[ak], strict=True)
            for q in range(4):
                S.op("dve", lambda e, a=a, q=q: e.bn_stats(out=stats[:, q, :], in_=a[:, q * 512:(q + 1) * 512]),
                     r=[ak], w=[f"stats{q}"], strict=True)
            S.op("dve", lambda e: e.bn_aggr(out=mv[:, 0:2], in_=stats.rearrange("p a b -> p (a b)")),
                 r=[f"stats{q}" for q in range(4)], w=["mv"], strict=True)
            S.op("dve", lambda e: e.tensor_scalar(out=rstd[:, 0:1], in0=mv[:, 1:2], scalar1=EPS, scalar2=None, op0=ALU.add),
                 r=["mv"], w=["rstd"], strict=True)
            S.op("act", lambda e: e.activation(out=rstd[:, 0:1], in_=rstd[:, 0:1], func=AF.Sqrt), r=["rstd"], w=["rstd"])
            S.op("dve", lambda e: e.reciprocal(out=rstd[:, 0:1], in_=rstd[:, 0:1]), r=["rstd"], w=["rstd"], strict=True)
            S.op("dve", lambda e, a=a: e.tensor_scalar(out=a, in0=a, scalar1=mv[:, 0:1], scalar2=rstd[:, 0:1],
                                                       op0=ALU.subtract, op1=ALU.mult), r=[ak, "mv", "rstd"], w=[ak], strict=True)
            S.op("pool", lambda e, a=a: e.tensor_tensor(out=a, in0=a, in1=g_sb, op=ALU.mult), r=[ak, "g_sb"], w=[ak])
            S.op("pool", lambda e, a=a: e.tensor_tensor(out=a, in0=a, in1=b_sb, op=ALU.add), r=[ak, "b_sb"], w=[ak], strict=True)
            S.dma("act", x2[t8 * 128:(t8 + 1) * 128, :], a, r=[ak])
        S.emit(st)
    return nc

import numpy as np


_prog = {}
def prog(name, fn, *a):
    k = (name,) + a
    if k not in _prog:
        _prog[k] = fn(*a)
    return _prog[k]

def moe_layer(x1, aff_all, w_gate_l, w_up_l, w_down_l, ln_g_l1, ln_b_l1):
    cores = list(range(8))
    a3 = aff_all.reshape(64, 128, 16)
    in_maps = [{"A": np.ascontiguousarray(a3[:, :, 2 * c:2 * c + 2].transpose(1, 2, 0))} for c in cores]
    r1 = run_bass_kernel_spmd(prog("R1", build_R1), in_maps, core_ids=cores).results
    for c in cores:
        r1[c]["aff"] = in_maps[c]["A"]
    idx_all = np.zeros((16, 1024), np.int64)
    in_maps = []
    for c in cores:
        sel = r1[c]["sel"]; aff = r1[c]["aff"]
        xe = np.zeros((2, 2048, 1024), np.float32)
        gs = np.zeros((2, 128, 8), np.float32)
        for e2 in range(2):
            m = sel[:, e2, :].T.reshape(-1) > 0.5
            a = aff[:, e2, :].T.reshape(-1)
            idx = np.nonzero(m)[0]
            moe_layer.counts.append(len(idx))
            idx = idx[:1024]
            if len(idx) < 1024:
                idx = np.concatenate([idx, np.full(1024 - len(idx), -1)])
            idx_all[2 * c + e2] = idx
            ok = idx >= 0
            xe[e2][:, ok] = x1[idx[ok]].T
            g = np.zeros(1024, np.float32); g[ok] = a[idx[ok]]
            gs[e2] = g.reshape(8, 128).T
        in_maps.append({"xeT": xe, "gs": gs, "wg": w_gate_l[2 * c:2 * c + 2], "wu": w_up_l[2 * c:2 * c + 2],
                        "wd": w_down_l[2 * c:2 * c + 2]})
    r2 = run_bass_kernel_spmd(prog("R2", build_R2), in_maps, core_ids=cores).results
    ye = np.concatenate([r2[c]["ye"] for c in cores], axis=0)
    cnt = np.zeros(8192, np.int64)
    for e in range(16):
        ok = idx_all[e] >= 0
        cnt[idx_all[e][ok]] += 1
    K = max(int(cnt.max()), 1)
    Y = np.zeros((K, 8192, 2048), np.float32)
    cur = np.zeros(8192, np.int64)
    for e in range(16):
        ok = idx_all[e] >= 0
        ii = idx_all[e][ok]
        Y[cur[ii], ii] = ye[e][ok]
        cur[ii] += 1
    lng = np.ascontiguousarray(np.broadcast_to(ln_g_l1[None, :], (128, 2048)))
    lnb = np.ascontiguousarray(np.broadcast_to(ln_b_l1[None, :], (128, 2048)))
    in_maps = [{"x1": np.ascontiguousarray(x1[1024 * c:1024 * (c + 1)]),
                "Y": np.ascontiguousarray(Y[:, 1024 * c:1024 * (c + 1)]), "lng": lng, "lnb": lnb} for c in cores]
    r3 = run_bass_kernel_spmd(prog("C", build_C, K), in_maps, core_ids=cores).results
    return np.concatenate([r3[c]["x2"] for c in cores], axis=0)
moe_layer.counts = []


def kernel(x, w_in, b_gate, rpb, sink, conv_w, w_branch, w_out, ln_g, ln_b, w_router, w_gate, w_up, w_down):
    f = lambda a: np.ascontiguousarray(np.asarray(a, dtype=np.float32))
    xc = f(x)[0]
    cores = list(range(8))
    for l in range(4):
        args = (f(w_in[l]), f(w_branch[l]), f(w_out[l]), f(b_gate[l]), f(ln_g[l][0]), f(ln_b[l][0]), f(conv_w[l]),
                f(sink[l]), f(rpb[l]), f(w_router[l]))
        in_maps = [m_inputs(c, xc, *args) for c in cores]
        res = run_bass_kernel_spmd(prog("M", build_M), in_maps, core_ids=cores).results
        x1 = np.concatenate([res[c]["x1"] for c in cores], axis=0)
        aff_all = np.concatenate([res[c]["aff"].transpose(1, 0, 2).reshape(1024, 16) for c in cores], axis=0)
        del in_maps, res
        xc = moe_layer(x1, aff_all, f(w_gate[l]), f(w_up[l]), f(w_down[l]), f(ln_g[l][1]), f(ln_b[l][1]))
    return xc[None].astype(np.float32)
```
